# Optimizing a Trainium2 kernel written in Bass

```python
import math
import jax, jax.numpy as jnp
from jax import lax
import numpy as np

D_MODEL = 2048
BATCH = 4
SEQ = 8192
DEPTH = 2

SC_W = 512
SC_CONV = 3
SSM_W = 512
SSM_GROUP = 16
SSM_GROUPS = SSM_W // SSM_GROUP
SSM_STATE = 64
N_Q_HEADS = 16
N_KV_HEADS = 4
HEAD_DIM = 64
ATT_W = N_Q_HEADS * HEAD_DIM
KV_W = N_KV_HEADS * HEAD_DIM
ROT_DIM = HEAD_DIM // 4
ROPE_THETA = 500000.0
N_IDX_HEADS = 8
IDX_DIM = 64
TOPK_MAX = 256
BLOCK_Q = 128
N_BRANCH = 3
D_FF = 5632
FFN_CONV = 3
NORM_EPS = 1e-6

SPLIT_SIZES = (SC_W, SC_W, SC_W, SSM_W, ATT_W, KV_W, KV_W,
               N_IDX_HEADS * IDX_DIM, IDX_DIM, N_IDX_HEADS,
               D_MODEL, D_MODEL, D_MODEL)
N_IN = sum(SPLIT_SIZES)

kernel_name = "hybrid_conv_s5_dsa_block"


def rms_norm(x, g):
    xf = x.astype(jnp.float32)
    y = xf * lax.rsqrt(jnp.mean(xf * xf, axis=-1, keepdims=True) + NORM_EPS)
    return (y * g.astype(jnp.float32)).astype(x.dtype)


def causal_dwconv(x, w):
    width = w.shape[0]
    seq = x.shape[1]
    w = w.astype(x.dtype)
    xp = jnp.pad(x, ((0, 0), (width - 1, 0), (0, 0)))
    out = w[width - 1] * x
    for j in range(width - 1):
        out = out + w[j] * xp[:, j:j + seq]
    return out


def rope_tables(positions):
    inv_freq = ROPE_THETA ** (-jnp.arange(0, ROT_DIM, 2, dtype=jnp.float32) / ROT_DIM)
    ang = positions.astype(jnp.float32)[..., None] * inv_freq
    return jnp.cos(ang)[:, :, None, :], jnp.sin(ang)[:, :, None, :]


def partial_rope(x, cos, sin):
    half = ROT_DIM // 2
    x1 = x[..., :half].astype(jnp.float32)
    x2 = x[..., half:ROT_DIM].astype(jnp.float32)
    rot = jnp.concatenate([x1 * cos - x2 * sin, x2 * cos + x1 * sin], axis=-1).astype(x.dtype)
    return jnp.concatenate([rot, x[..., ROT_DIM:]], axis=-1)


def _ssm_combine(left, right):
    a_l, b_l = left
    a_r, b_r = right
    return a_l * a_r, a_r * b_l + b_r


def s5_mixer(u, a_re, a_im, log_dt, b_re, b_im, c_re, c_im, d_skip, w_glu):
    bsz, seq, _ = u.shape
    uf = u.astype(jnp.float32).reshape(bsz, seq, SSM_GROUPS, SSM_GROUP)
    a = lax.complex(a_re.astype(jnp.float32), a_im.astype(jnp.float32))
    dt = jnp.exp(log_dt.astype(jnp.float32))[:, None]
    a_bar = jnp.exp(dt * a)
    b_bar = ((a_bar - 1.0) / a)[..., None] * lax.complex(
        b_re.astype(jnp.float32), b_im.astype(jnp.float32))
    bu = jnp.einsum('bsgp,gnp->bsgn', uf.astype(jnp.complex64), b_bar)
    a_seq = jnp.broadcast_to(a_bar, (1, seq) + a_bar.shape)
    _, state = lax.associative_scan(_ssm_combine, (a_seq, bu), axis=1)
    c = lax.complex(c_re.astype(jnp.float32), c_im.astype(jnp.float32))
    y = jnp.real(jnp.einsum('bsgn,gpn->bsgp', state, c)) \
        + d_skip.astype(jnp.float32).reshape(SSM_GROUPS, SSM_GROUP) * uf
    z = jax.nn.gelu(y.reshape(bsz, seq, SSM_W))
    z = z * jax.nn.sigmoid(z @ w_glu.astype(jnp.float32))
    return z.astype(u.dtype)


def dsa_attention(q, k, v, qi, ki, wi):
    bsz, seq = q.shape[0], q.shape[1]
    k_sel = min(TOPK_MAX, seq // 4)
    n_blk = seq // BLOCK_Q
    rep = N_Q_HEADS // N_KV_HEADS

    def to_blocks(t):
        return jnp.swapaxes(t.reshape((bsz, n_blk, BLOCK_Q) + t.shape[2:]), 0, 1)

    qb = to_blocks(q.reshape(bsz, seq, N_KV_HEADS, rep, HEAD_DIM))
    qib = to_blocks(qi)
    wib = to_blocks(wi)
    t0s = jnp.arange(n_blk, dtype=jnp.int32) * BLOCK_Q
    key_pos = jnp.arange(seq, dtype=jnp.int32)
    bidx = jnp.arange(bsz)[:, None, None]

    def one_block(args):
        q_b, qi_b, wi_b, t0 = args
        t = t0 + jnp.arange(BLOCK_Q, dtype=jnp.int32)
        dots = jnp.einsum('bqhd,bsd->bqhs', qi_b, ki).astype(jnp.float32) * (IDX_DIM ** -0.5)
        iscore = jnp.einsum('bqhs,bqh->bqs', jax.nn.relu(dots),
                            wi_b.astype(jnp.float32) * (N_IDX_HEADS ** -0.5))
        causal = key_pos[None, :] <= t[:, None]
        iscore = jnp.where(causal[None], iscore, -jnp.inf)
        _, idx = lax.top_k(iscore, k_sel)
        valid = idx <= t[None, :, None]
        kg = k[bidx, idx]
        vg = v[bidx, idx]
        logits = jnp.einsum('bqgrd,bqkgd->bqgrk', q_b, kg).astype(jnp.float32) * (HEAD_DIM ** -0.5)
        logits = jnp.where(valid[:, :, None, None, :], logits, -jnp.inf)
        p = jax.nn.softmax(logits, axis=-1).astype(vg.dtype)
        return jnp.einsum('bqgrk,bqkgd->bqgrd', p, vg)

    ob = lax.map(one_block, (qb, qib, wib, t0s))
    return jnp.swapaxes(ob, 0, 1).reshape(bsz, seq, ATT_W)


def hybrid_mixer(h, cos, sin, w_in, sc_conv, ssm_a_re, ssm_a_im, ssm_log_dt, ssm_b_re, ssm_b_im,
                 ssm_c_re, ssm_c_im, ssm_d, ssm_glu, w_branch_a, w_branch_b, w_branch_c, w_out):
    bsz, seq, _ = h.shape
    offsets = []
    acc = 0
    for s in SPLIT_SIZES[:-1]:
        acc += s
        offsets.append(acc)
    (sc_x, sc_b, sc_c, ssm_u, q, k, v, qi, ki, wi,
     g_a, g_b, g_c) = jnp.split(h @ w_in, offsets, axis=-1)

    y_a = sc_b * causal_dwconv(sc_c * sc_x, sc_conv)
    y_b = s5_mixer(ssm_u, ssm_a_re, ssm_a_im, ssm_log_dt, ssm_b_re, ssm_b_im,
                   ssm_c_re, ssm_c_im, ssm_d, ssm_glu)
    q = partial_rope(q.reshape(bsz, seq, N_Q_HEADS, HEAD_DIM), cos, sin)
    k = partial_rope(k.reshape(bsz, seq, N_KV_HEADS, HEAD_DIM), cos, sin)
    v = v.reshape(bsz, seq, N_KV_HEADS, HEAD_DIM)
    qi = partial_rope(qi.reshape(bsz, seq, N_IDX_HEADS, IDX_DIM), cos, sin)
    ki = partial_rope(ki[:, :, None, :], cos, sin)[:, :, 0, :]
    y_c = dsa_attention(q, k, v, qi, ki, wi)

    merged = (jax.nn.sigmoid(g_a) * (y_a @ w_branch_a)
              + jax.nn.sigmoid(g_b) * (y_b @ w_branch_b)
              + jax.nn.sigmoid(g_c) * (y_c @ w_branch_c))
    return merged @ w_out


def conv_glu_ffn(h, w_up, conv_w, w_down):
    up = causal_dwconv(h @ w_up, conv_w)
    g, u = jnp.split(up, 2, axis=-1)
    return (jax.nn.silu(g) * u) @ w_down


def setup_inputs(seed: int = 0) -> dict:
    key = jax.random.key(seed)
    ks = jax.random.split(key, 24)
    f32 = jnp.float32

    def nrm(k, shape, scale):
        return jax.random.normal(k, shape, f32) * scale

    x = jax.random.normal(ks[0], (BATCH, SEQ, D_MODEL), f32)
    positions = jnp.broadcast_to(jnp.arange(SEQ, dtype=jnp.int32), (BATCH, SEQ))
    norm_mix = 1.0 + nrm(ks[1], (DEPTH, D_MODEL), 0.02)
    w_in = nrm(ks[2], (DEPTH, D_MODEL, N_IN), D_MODEL ** -0.5)
    sc_conv = nrm(ks[3], (DEPTH, SC_CONV, SC_W), SC_CONV ** -0.5)
    ssm_a_re = -0.5 + nrm(ks[4], (DEPTH, SSM_GROUPS, SSM_STATE), 0.01)
    ssm_a_im = math.pi * jnp.arange(SSM_STATE, dtype=f32) + nrm(ks[5], (DEPTH, SSM_GROUPS, SSM_STATE), 0.01)
    ssm_log_dt = jax.random.uniform(ks[6], (DEPTH, SSM_GROUPS), f32, math.log(1e-3), math.log(1e-1))
    b_scale = (2.0 * SSM_GROUP) ** -0.5
    c_scale = (2.0 * SSM_STATE) ** -0.5
    ssm_b_re = nrm(ks[7], (DEPTH, SSM_GROUPS, SSM_STATE, SSM_GROUP), b_scale)
    ssm_b_im = nrm(ks[8], (DEPTH, SSM_GROUPS, SSM_STATE, SSM_GROUP), b_scale)
    ssm_c_re = nrm(ks[9], (DEPTH, SSM_GROUPS, SSM_GROUP, SSM_STATE), c_scale)
    ssm_c_im = nrm(ks[10], (DEPTH, SSM_GROUPS, SSM_GROUP, SSM_STATE), c_scale)
    ssm_d = nrm(ks[11], (DEPTH, SSM_W), 1.0)
    ssm_glu = nrm(ks[12], (DEPTH, SSM_W, SSM_W), SSM_W ** -0.5)
    w_branch_a = nrm(ks[13], (DEPTH, SC_W, D_MODEL), SC_W ** -0.5)
    w_branch_b = nrm(ks[14], (DEPTH, SSM_W, D_MODEL), SSM_W ** -0.5)
    w_branch_c = nrm(ks[15], (DEPTH, ATT_W, D_MODEL), ATT_W ** -0.5)
    w_out = nrm(ks[16], (DEPTH, D_MODEL, D_MODEL), D_MODEL ** -0.5)
    norm_ffn = 1.0 + nrm(ks[17], (DEPTH, D_MODEL), 0.02)
    w_up = nrm(ks[18], (DEPTH, D_MODEL, 2 * D_FF), D_MODEL ** -0.5)
    ffn_conv = nrm(ks[19], (DEPTH, FFN_CONV, 2 * D_FF), FFN_CONV ** -0.5)
    w_down = nrm(ks[20], (DEPTH, D_FF, D_MODEL), D_FF ** -0.5)
    norm_final = 1.0 + nrm(ks[21], (D_MODEL,), 0.02)
    return {"x": x, "positions": positions, "norm_mix": norm_mix, "w_in": w_in,
            "sc_conv": sc_conv, "ssm_a_re": ssm_a_re, "ssm_a_im": ssm_a_im,
            "ssm_log_dt": ssm_log_dt, "ssm_b_re": ssm_b_re, "ssm_b_im": ssm_b_im,
            "ssm_c_re": ssm_c_re, "ssm_c_im": ssm_c_im, "ssm_d": ssm_d, "ssm_glu": ssm_glu,
            "w_branch_a": w_branch_a, "w_branch_b": w_branch_b, "w_branch_c": w_branch_c,
            "w_out": w_out, "norm_ffn": norm_ffn, "w_up": w_up, "ffn_conv": ffn_conv,
            "w_down": w_down, "norm_final": norm_final}


def reference(x, positions, norm_mix, w_in, sc_conv, ssm_a_re, ssm_a_im, ssm_log_dt,
              ssm_b_re, ssm_b_im, ssm_c_re, ssm_c_im, ssm_d, ssm_glu, w_branch_a,
              w_branch_b, w_branch_c, w_out, norm_ffn, w_up, ffn_conv, w_down, norm_final):
    cos, sin = rope_tables(positions)
    for l in range(DEPTH):
        h = rms_norm(x, norm_mix[l])
        x = x + hybrid_mixer(h, cos, sin, w_in[l], sc_conv[l], ssm_a_re[l], ssm_a_im[l],
                             ssm_log_dt[l], ssm_b_re[l], ssm_b_im[l], ssm_c_re[l], ssm_c_im[l],
                             ssm_d[l], ssm_glu[l], w_branch_a[l], w_branch_b[l], w_branch_c[l],
                             w_out[l])
        h = rms_norm(x, norm_ffn[l])
        x = x + conv_glu_ffn(h, w_up[l], ffn_conv[l], w_down[l])
    return rms_norm(x, norm_final)
```

```python
import math
from contextlib import ExitStack

import numpy as np
import ml_dtypes

import concourse.bass as bass
import concourse.mybir as mybir
from concourse.bass_utils import run_bass_kernel_spmd

F32 = mybir.dt.float32
BF16 = mybir.dt.bfloat16
I32 = mybir.dt.int32
AF = mybir.ActivationFunctionType
ALU = mybir.AluOpType
AX = mybir.AxisListType

D = 2048
KC = D // 128
DEPTH = 2
N_IN = 10312
D_FF = 5632
NEG = -1.0e30
TOPK = 256
NORM_EPS = 1e-6

ENGS = ("pe", "act", "dve", "pool", "sp")


class Buf:
    __slots__ = ("name", "w", "r", "dsem", "dcnt", "ap")

    def __init__(self, name, ap=None):
        self.name = name
        self.w = None
        self.r = {}
        self.dsem = None
        self.dcnt = 0
        self.ap = ap


class SemPool:
    def __init__(self, nc, n=96):
        self.stack = ExitStack()
        self.handles = {}
        self.free = []
        for i in range(n):
            key = f"s{i}"
            self.handles[key] = self.stack.enter_context(nc.semaphore(f"gs{i}"))
            self.free.append((0, i, key))

    def get(self):
        self.free.sort()
        cnt, i, key = self.free.pop(0)
        return key, cnt

    def put(self, key, cnt):
        if cnt < Phase.SEM_ROLL:
            self.free.append((cnt, int(key[1:]), key))


class Phase:
    SEM_ROLL = 30000

    def __init__(self, nc, name, pool):
        self.nc = nc
        self.name = name
        self.pool = pool
        self.stack = ExitStack()
        self.lists = {e: [] for e in ENGS}
        self.cnt = {e: 0 for e in ENGS}
        self.waited = {e: {} for e in ENGS}
        self.sems = pool.handles
        self.key_eng = {}
        self.curkey = {}
        self.retired = []
        for e in ENGS:
            self._roll(e)
        self.dma_bufs = []
        self.dma_done = []

    def _roll(self, e):
        key, cnt = self.pool.get()
        self.key_eng[key] = e
        self.curkey[e] = key
        self.cnt[e] = cnt

    def sbuf(self, name, shape, dtype):
        t = self.stack.enter_context(self.nc.sbuf_tensor(f"{self.name}_{name}", list(shape), dtype))
        return Buf(name, t)

    def psum(self, name, shape, dtype):
        n = 2048 // (2 if dtype == BF16 else 4)
        t = self.stack.enter_context(self.nc.psum_tensor(f"{self.name}_{name}", [128, n], dtype))
        return Buf(name, t)

    def token(self, name):
        return Buf(name)

    def _deps(self, eng, reads, writes):
        deps = {}
        ke = self.key_eng

        def add(k, v):
            if deps.get(k, 0) < v:
                deps[k] = v

        for b in reads:
            if b.w is not None:
                add(*b.w)
        for b in writes:
            if b.w is not None and ke.get(b.w[0]) != eng:
                add(*b.w)
            for k, v in b.r.items():
                if ke.get(k) != eng:
                    add(k, v)
        return deps

    def _emit_waits(self, eng, deps):
        wd = self.waited[eng]
        for k, v in deps.items():
            if wd.get(k, 0) < v:
                wd[k] = v
                self.lists[eng].append(("wait", self.sems[k], v))

    def op(self, eng, fn, reads=(), writes=()):
        deps = self._deps(eng, reads, writes)
        self._emit_waits(eng, deps)
        if self.cnt[eng] >= self.SEM_ROLL:
            self._roll(eng)
        self.cnt[eng] += 1
        key = self.curkey[eng]
        ev = (key, self.cnt[eng])
        self.lists[eng].append(("op", fn, self.sems[key], 1))
        for b in reads:
            if b.r.get(key, 0) < ev[1]:
                b.r[key] = ev[1]
        for b in writes:
            b.w = ev
            b.r = {}
        return ev

    def dma(self, q, fn, carrier, reads=(), writes=()):
        sb = carrier
        if sb.dsem is None or sb.dcnt >= self.SEM_ROLL:
            if sb.dsem is None:
                self.dma_bufs.append(sb)
            else:
                self.dma_done.append((sb.dsem, sb.dcnt))
            sb.dsem, sb.dcnt = self.pool.get()
        deps = self._deps(q, reads, writes)
        self._emit_waits(q, deps)
        sb.dcnt += 16
        ev = (sb.dsem, sb.dcnt)
        self.lists[q].append(("op", fn, self.sems[sb.dsem], 16))
        for b in reads:
            if b.r.get(ev[0], 0) < ev[1]:
                b.r[ev[0]] = ev[1]
        for b in writes:
            b.w = ev
            b.r = {}
        return ev

    def finish(self):
        deps = {}
        for b in self.dma_bufs:
            deps[b.dsem] = b.dcnt
        for k, v in self.dma_done:
            deps[k] = v
        self._emit_waits("sp", deps)
        lists = self.lists

        def run(engh, lst):
            for it in lst:
                if it[0] == "wait":
                    engh.wait_ge(it[1], it[2])
                else:
                    it[1](engh).then_inc(it[2], it[3])

        with self.nc.Block() as block:
            @block.tensor
            def _(e):
                run(e, lists["pe"])

            @block.scalar
            def _(e):
                run(e, lists["act"])

            @block.vector
            def _(e):
                run(e, lists["dve"])

            @block.gpsimd
            def _(e):
                run(e, lists["pool"])

            @block.sync
            def _(e):
                run(e, lists["sp"])
        n = sum(len(v) for v in lists.values())
        self.stack.close()
        for e in ENGS:
            self.pool.put(self.curkey[e], self.cnt[e])
        for b in self.dma_bufs:
            self.pool.put(b.dsem, b.dcnt)
        return n


def host_consts():
    c = {}
    c["c_ident_bf"] = np.eye(128, dtype=np.float32).astype(ml_dtypes.bfloat16)
    c["c_ident_f"] = np.eye(128, dtype=np.float32)
    inv_freq = (500000.0 ** (-np.arange(0, 16, 2, dtype=np.float32) / np.float32(16))).astype(np.float32)
    c["c_invf"] = np.tile(inv_freq[None, :], (128, 1)).astype(np.float32)
    tri = np.zeros((128, 128), np.float32)
    tri[np.triu_indices(128, 1)] = NEG
    c["c_causal"] = tri
    zm = np.zeros((128, 4, 4, 2, 16), np.float32)
    for i in range(4):
        for g2 in range(2):
            zm[g2 * 64:(g2 + 1) * 64, i, i, g2, :] = 1.0
    c["c_zmask"] = zm.reshape(128, 4, 128)
    sel = np.zeros((128, 64), np.float32)
    sel[64 + np.arange(64), np.arange(64)] = 1.0
    c["c_sel"] = sel
    return c


CONST_SPECS = {
    "c_ident_bf": ([128, 128], BF16),
    "c_ident_f": ([128, 128], F32),
    "c_invf": ([128, 8], F32),
    "c_causal": ([128, 128], F32),
    "c_zmask": ([128, 4, 128], F32),
    "c_sel": ([128, 64], F32),
}

W_SPECS = {
    "norm_mix": [DEPTH, D], "w_in": [DEPTH, D, N_IN], "sc_conv": [DEPTH, 3, 512],
    "ssm_a_re": [DEPTH, 32, 64], "ssm_a_im": [DEPTH, 32, 64], "ssm_log_dt": [DEPTH, 32],
    "ssm_b_re": [DEPTH, 32, 64, 16], "ssm_b_im": [DEPTH, 32, 64, 16],
    "ssm_c_re": [DEPTH, 32, 64, 16], "ssm_c_im": [DEPTH, 32, 64, 16],
    "ssm_d": [DEPTH, 512], "ssm_glu": [DEPTH, 512, 512],
    "w_branch_a": [DEPTH, 512, D], "w_branch_b": [DEPTH, 512, D], "w_branch_c": [DEPTH, 1024, D],
    "w_out": [DEPTH, D, D], "norm_ffn": [DEPTH, D], "w_up": [DEPTH, D, 2 * D_FF],
    "ffn_conv": [DEPTH, 3, 2 * D_FF], "w_down": [DEPTH, D_FF, D], "norm_final": [D],
}


def w_specs(nl):
    return {k: ([nl] + v[1:] if k != "norm_final" else v) for k, v in W_SPECS.items()}


BIG_W = ["w_in", "ssm_glu", "w_branch_a", "w_branch_b", "w_branch_c", "w_out", "w_up", "w_down"]


class Prog:
    def __init__(self, S, debug_outs=(), skip=(), nl=DEPTH):
        self.S = S
        self.nl = nl
        self.skip = tuple(skip)
        WS = w_specs(nl)
        self.WS = WS
        nc = bass.Bass("TRN2", target_bir_lowering=False)
        self.nc = nc
        self.pool = SemPool(nc)
        t = {}
        t["xT"] = nc.dram_tensor("xT", [D, S], F32, kind="ExternalInput").ap()
        t["pos"] = nc.dram_tensor("pos", [S], I32, kind="ExternalInput").ap()
        for k, shp in WS.items():
            if k in self.skip:
                continue
            t[k] = nc.dram_tensor(k, shp, F32, kind="ExternalInput").ap()
        for k, (shp, dt) in CONST_SPECS.items():
            t[k] = nc.dram_tensor(k, shp, dt, kind="ExternalInput").ap()
        t["outT"] = nc.dram_tensor("outT", [D, S], F32, kind=("Internal" if "xres" in debug_outs else "ExternalOutput")).ap()

        def scratch(name, shape, dt):
            kind = "ExternalOutput" if name in debug_outs else "Internal"
            t[name] = nc.dram_tensor(name, shape, dt, kind=kind).ap()

        for k in BIG_W:
            if k in self.skip:
                continue
            scratch("b_" + k, WS[k], BF16)
        scratch("cs", [S, 16], F32)
        scratch("xres", [D, S], F32)
        scratch("uT", [512, S], BF16)
        scratch("qT", [64, 16, S], BF16)
        scratch("kT", [64, 4, S], BF16)
        scratch("qiT", [64, 8, S], BF16)
        scratch("kiT", [64, S], BF16)
        scratch("vtm", [S, 4, 128], BF16)
        scratch("witm", [S, 8], F32)
        scratch("zT", [512, S], BF16)
        scratch("ybT", [512, S], BF16)
        scratch("ycT", [1024, S], BF16)
        self.t = t


def emit_range_reduce(ph, red, ki, kf, msk):
    two_pi = 2.0 * math.pi
    ph.op("dve", lambda e: e.tensor_scalar(out=kf.ap[:], in0=red.ap[:], scalar1=1.0 / two_pi, scalar2=None, op0=ALU.mult), reads=[red], writes=[kf])
    ph.op("dve", lambda e: e.tensor_copy(out=ki.ap[:], in_=kf.ap[:]), reads=[kf], writes=[ki])
    ph.op("dve", lambda e: e.tensor_copy(out=kf.ap[:], in_=ki.ap[:]), reads=[ki], writes=[kf])
    ph.op("dve", lambda e: e.scalar_tensor_tensor(out=red.ap[:], in0=kf.ap[:], scalar=-two_pi, in1=red.ap[:], op0=ALU.mult, op1=ALU.add), reads=[kf, red], writes=[red])
    ph.op("dve", lambda e: e.tensor_scalar(out=msk.ap[:], in0=red.ap[:], scalar1=math.pi, scalar2=None, op0=ALU.is_gt), reads=[red], writes=[msk])
    ph.op("dve", lambda e: e.scalar_tensor_tensor(out=red.ap[:], in0=msk.ap[:], scalar=-two_pi, in1=red.ap[:], op0=ALU.mult, op1=ALU.add), reads=[msk, red], writes=[red])
    ph.op("dve", lambda e: e.tensor_scalar(out=msk.ap[:], in0=red.ap[:], scalar1=-math.pi, scalar2=None, op0=ALU.is_lt), reads=[red], writes=[msk])
    ph.op("dve", lambda e: e.scalar_tensor_tensor(out=red.ap[:], in0=msk.ap[:], scalar=two_pi, in1=red.ap[:], op0=ALU.mult, op1=ALU.add), reads=[msk, red], writes=[red])


def phase0(P):
    nc, t, S = P.nc, P.t, P.S
    ph = Phase(nc, "p0", P.pool)
    CW = 4096
    NBUF = 3
    cin = [ph.sbuf(f"cin{i}", [128, CW], F32) for i in range(NBUF)]
    cout = [ph.sbuf(f"cout{i}", [128, CW], BF16) for i in range(NBUF)]
    n = 0
    for k in BIG_W:
        if k in P.skip:
            continue
        shp = W_SPECS[k]
        rows, cols = shp[1], shp[2]
        for l in range(P.nl):
            for r0 in range(0, rows, 128):
                for c0 in range(0, cols, CW):
                    w = min(CW, cols - c0)
                    bi, bo = cin[n % NBUF], cout[n % NBUF]
                    src = t[k][l, r0:r0 + 128, c0:c0 + w]
                    dst = t["b_" + k][l, r0:r0 + 128, c0:c0 + w]
                    ph.dma("sp", lambda e, src=src, bi=bi, w=w: e.dma_start(out=bi.ap[:, 0:w], in_=src), bi, writes=[bi])
                    eng = ("dve", "act", "pool")[n % 3]
                    if eng == "act":
                        ph.op("act", lambda e, bi=bi, bo=bo, w=w: e.activation(out=bo.ap[:, 0:w], in_=bi.ap[:, 0:w], func=AF.Copy), reads=[bi], writes=[bo])
                    else:
                        ph.op(eng, lambda e, bi=bi, bo=bo, w=w: e.tensor_copy(out=bo.ap[:, 0:w], in_=bi.ap[:, 0:w]), reads=[bi], writes=[bo])
                    ph.dma("sp", lambda e, dst=dst, bo=bo, w=w: e.dma_start(out=dst, in_=bo.ap[:, 0:w]), bo, reads=[bo])
                    n += 1
    NB = S // 128
    posi = ph.sbuf("posi", [128, NB], I32)
    posf = ph.sbuf("posf", [128, NB], F32)
    invf = ph.sbuf("invf", [128, 8], F32)
    ang = ph.sbuf("ang", [128, NB, 8], F32)
    red = ph.sbuf("red", [128, NB, 16], F32)
    cs = ph.sbuf("cs", [128, NB, 16], F32)
    negpi = ph.sbuf("negpi", [128, 1], F32)
    ph.dma("sp", lambda e: e.dma_start(out=posi.ap[:], in_=t["pos"].rearrange("(b p) -> p b", p=128),
                                       allow_slow_non_contiguous=True), posi, writes=[posi])
    ph.dma("sp", lambda e: e.dma_start(out=invf.ap[:], in_=t["c_invf"]), invf, writes=[invf])
    ph.op("dve", lambda e: e.memset(negpi.ap[:], -math.pi), writes=[negpi])
    ph.op("dve", lambda e: e.tensor_copy(out=posf.ap[:], in_=posi.ap[:]), reads=[posi], writes=[posf])
    ph.op("dve", lambda e: e.tensor_tensor(out=ang.ap[:], in0=posf.ap[:].unsqueeze(2).to_broadcast([128, NB, 8]),
                                           in1=invf.ap[:].unsqueeze(1).to_broadcast([128, NB, 8]), op=ALU.mult),
          reads=[posf, invf], writes=[ang])
    two_pi = 2.0 * math.pi
    ki = ph.sbuf("ki", [128, NB, 16], I32)
    kf = ph.sbuf("kf", [128, NB, 16], F32)
    msk = ph.sbuf("msk", [128, NB, 16], F32)
    ph.op("dve", lambda e: e.tensor_scalar(out=red.ap[:, :, 0:8], in0=ang.ap[:], scalar1=0.5 * math.pi, scalar2=None, op0=ALU.add), reads=[ang], writes=[red])
    ph.op("dve", lambda e: e.tensor_copy(out=red.ap[:, :, 8:16], in_=ang.ap[:]), reads=[ang], writes=[red])
    emit_range_reduce(ph, red, ki, kf, msk)
    ph.op("act", lambda e: e.activation(out=cs.ap[:], in_=red.ap[:], func=AF.Sin),
          reads=[red], writes=[cs])
    ph.dma("sp", lambda e: e.dma_start(out=t["cs"].rearrange("(b p) c -> p b c", p=128), in_=cs.ap[:]), cs, reads=[cs])
    return ph.finish()


def emit_norm(ph, x_sb, h_sb, g_sb, ones_bf, eps_sb, ps_ss, sq_bufs, rstd_sb, T):
    for kc in range(KC):
        sq = sq_bufs[kc % len(sq_bufs)]
        ph.op("act", lambda e, kc=kc, sq=sq: e.activation(out=sq.ap[:], in_=x_sb.ap[:, kc, :], func=AF.Square),
              reads=[x_sb], writes=[sq])
        ph.op("pe", lambda e, kc=kc, sq=sq: e.matmul(ps_ss.ap[:], ones_bf.ap[:], sq.ap[:], start=(kc == 0), stop=(kc == KC - 1)),
              reads=[sq, ones_bf], writes=[ps_ss])
    ph.op("act", lambda e: e.activation(out=rstd_sb.ap[:], in_=ps_ss.ap[:], func=AF.Sqrt, bias=eps_sb.ap[:, 0:1], scale=1.0 / D),
          reads=[ps_ss, eps_sb], writes=[rstd_sb])
    ph.op("dve", lambda e: e.reciprocal(out=rstd_sb.ap[:], in_=rstd_sb.ap[:]), reads=[rstd_sb], writes=[rstd_sb])
    for kc in range(KC):
        ph.op("dve", lambda e, kc=kc: e.scalar_tensor_tensor(out=h_sb.ap[:, kc, :], in0=x_sb.ap[:, kc, :], scalar=g_sb.ap[:, kc:kc + 1],
                                                            in1=rstd_sb.ap[:], op0=ALU.mult, op1=ALU.mult),
              reads=[x_sb, g_sb, rstd_sb], writes=[h_sb])


A_C0 = 1536
A_NC = 2632
A_TM = 2120


def phaseA(P, l, xsrc):
    nc, t, S = P.nc, P.t, P.S
    T = 512
    NT = S // T
    ph = Phase(nc, f"A{l}", P.pool)
    wA = ph.sbuf("wA", [128, KC, A_NC], BF16)
    for kc in range(KC):
        ph.dma("sp", lambda e, kc=kc: e.dma_start(out=wA.ap[:, kc, :], in_=t["b_w_in"][l, kc * 128:(kc + 1) * 128, A_C0:A_C0 + A_NC]),
               wA, writes=[wA])
    ones_bf = ph.sbuf("ones", [128, 128], BF16)
    ident = ph.sbuf("ident", [128, 128], BF16)
    eps_sb = ph.sbuf("eps", [128, 1], F32)
    g_sb = ph.sbuf("g", [128, KC], F32)
    cs_all = ph.sbuf("csall", [128, S // 128, 16], F32)
    ph.op("dve", lambda e: e.memset(ones_bf.ap[:], 1.0), writes=[ones_bf])
    ph.op("dve", lambda e: e.memset(eps_sb.ap[:], NORM_EPS), writes=[eps_sb])
    ph.dma("sp", lambda e: e.dma_start(out=ident.ap[:], in_=t["c_ident_bf"]), ident, writes=[ident])
    ph.dma("sp", lambda e: e.dma_start(out=g_sb.ap[:], in_=t["norm_mix"][l].rearrange("(kc p) -> p kc", p=128),
                                       allow_slow_non_contiguous=True), g_sb, writes=[g_sb])
    ph.dma("sp", lambda e: e.dma_start(out=cs_all.ap[:], in_=t["cs"].rearrange("(b p) c -> p b c", p=128)), cs_all, writes=[cs_all])

    xb = [ph.sbuf(f"x{i}", [128, KC, T], F32) for i in range(1)]
    h_sb = ph.sbuf("h", [128, KC, T], BF16)
    sqb = [ph.sbuf(f"sq{i}", [128, T], BF16) for i in range(2)]
    rstd = ph.sbuf("rstd", [128, T], F32)
    ps_ss = ph.psum("ss", [128, T], F32)
    ps_u = [ph.psum(f"pu{i}", [128, 512], F32) for i in range(2)]
    ps_p = [ph.psum(f"pp{i}", [128, 512], F32) for i in range(2)]
    ps_t = [ph.psum(f"pt{i}", [64, 512], BF16) for i in range(2)]
    u_st = [ph.sbuf(f"ust{i}", [128, 512], BF16) for i in range(2)]
    P_sb = [ph.sbuf(f"P{i}", [128, A_TM], F32) for i in range(2)]
    R_sb = ph.sbuf("R", [128, 4, A_TM - 8], BF16)
    tmp = [ph.sbuf(f"tmp{i}", [128, 20, 8], F32) for i in range(4)]
    v_st = [ph.sbuf(f"vst{i}", [128, 4, 4, 128], BF16) for i in range(2)]
    wi_st = [ph.sbuf(f"wist{i}", [128, 4, 8], F32) for i in range(2)]
    hd_st = [ph.sbuf(f"hdst{i}", [64, 16, 512], BF16) for i in range(1)]
    for i in range(2):
        ph.op("pool", lambda e, i=i: e.memset(v_st[i].ap[:], 1.0), writes=[v_st[i]])

    xsrc_r = xsrc.rearrange("(kc p) t -> p kc t", p=128)
    npu = 0
    npp = 0
    npt = 0
    for tt in range(NT):
        x_sb = xb[0]
        for q4 in range(4):
            ph.dma("sp", lambda e, q4=q4, tt=tt, x_sb=x_sb: e.dma_start(out=x_sb.ap[:, q4 * 4:(q4 + 1) * 4, :],
                                                                        in_=xsrc_r[:, q4 * 4:(q4 + 1) * 4, tt * T:(tt + 1) * T]),
                   x_sb, writes=[x_sb])
        emit_norm(ph, x_sb, h_sb, g_sb, ones_bf, eps_sb, ps_ss, sqb, rstd, T)
        for c in range(4):
            pu = ps_u[npu % 2]
            ust = u_st[npu % 2]
            npu += 1
            for kc in range(KC):
                ph.op("pe", lambda e, kc=kc, c=c, pu=pu: e.matmul(pu.ap[:], wA.ap[:, kc, c * 128:(c + 1) * 128], h_sb.ap[:, kc, :],
                                                                  start=(kc == 0), stop=(kc == KC - 1)),
                      reads=[wA, h_sb], writes=[pu])
            ph.op("act", lambda e, pu=pu, ust=ust: e.activation(out=ust.ap[:], in_=pu.ap[:], func=AF.Copy), reads=[pu], writes=[ust])
            ph.dma("sp", lambda e, c=c, tt=tt, ust=ust: e.dma_start(out=t["uT"][c * 128:(c + 1) * 128, tt * T:(tt + 1) * T], in_=ust.ap[:]),
                   ust, reads=[ust])
        vst = v_st[tt % 2]
        wist = wi_st[tt % 2]
        hdst = hd_st[0]
        for tb in range(4):
            Pb = P_sb[tb % 2]
            for ct in range(5):
                c0 = 512 + ct * 512
                c1 = min(A_NC, c0 + 512)
                w = c1 - c0
                pp = ps_p[npp % 2]
                npp += 1
                for kc in range(KC):
                    ph.op("pe", lambda e, kc=kc, tb=tb, c0=c0, c1=c1, w=w, pp=pp: e.matmul(
                        pp.ap[:, 0:w], h_sb.ap[:, kc, tb * 128:(tb + 1) * 128], wA.ap[:, kc, c0:c1],
                        start=(kc == 0), stop=(kc == KC - 1)), reads=[wA, h_sb], writes=[pp])
                ph.op("act", lambda e, w=w, pp=pp, ct=ct, Pb=Pb: e.activation(out=Pb.ap[:, ct * 512:ct * 512 + w], in_=pp.ap[:, 0:w], func=AF.Copy),
                      reads=[pp], writes=[Pb])
            blk = tt * 4 + tb
            for (c0, nh) in ((0, 20), (1536, 9)):
                hv = Pb.ap[:, c0:c0 + nh * 64].rearrange("p (h d) -> p h d", d=64)
                x1 = hv[:, :, 0:8]
                x2 = hv[:, :, 8:16]
                cosb = cs_all.ap[:, blk, 0:8].unsqueeze(1).to_broadcast([128, nh, 8])
                sinb = cs_all.ap[:, blk, 8:16].unsqueeze(1).to_broadcast([128, nh, 8])
                tm = [tmp[i].ap[:, 0:nh, :] for i in range(4)]
                ph.op("dve", lambda e, x1=x1, cosb=cosb, o=tm[0]: e.tensor_tensor(out=o, in0=x1, in1=cosb, op=ALU.mult), reads=[Pb, cs_all], writes=[tmp[0]])
                ph.op("dve", lambda e, x2=x2, sinb=sinb, o=tm[1]: e.tensor_tensor(out=o, in0=x2, in1=sinb, op=ALU.mult), reads=[Pb, cs_all], writes=[tmp[1]])
                ph.op("dve", lambda e, x2=x2, cosb=cosb, o=tm[2]: e.tensor_tensor(out=o, in0=x2, in1=cosb, op=ALU.mult), reads=[Pb, cs_all], writes=[tmp[2]])
                ph.op("dve", lambda e, x1=x1, sinb=sinb, o=tm[3]: e.tensor_tensor(out=o, in0=x1, in1=sinb, op=ALU.mult), reads=[Pb, cs_all], writes=[tmp[3]])
                ph.op("dve", lambda e, x1=x1, a=tm[0], b=tm[1]: e.tensor_tensor(out=x1, in0=a, in1=b, op=ALU.subtract), reads=[tmp[0], tmp[1]], writes=[Pb])
                ph.op("dve", lambda e, x2=x2, a=tm[2], b=tm[3]: e.tensor_tensor(out=x2, in0=a, in1=b, op=ALU.add), reads=[tmp[2], tmp[3]], writes=[Pb])
            ph.op("act", lambda e, tb=tb, Pb=Pb: e.activation(out=R_sb.ap[:, tb, 0:1024], in_=Pb.ap[:, 0:1024], func=AF.Copy, scale=0.125), reads=[Pb], writes=[R_sb])
            ph.op("pool", lambda e, tb=tb, Pb=Pb: e.tensor_copy(out=R_sb.ap[:, tb, 1024:1280], in_=Pb.ap[:, 1024:1280]), reads=[Pb], writes=[R_sb])
            ph.op("act", lambda e, tb=tb, Pb=Pb: e.activation(out=R_sb.ap[:, tb, 1536:2048], in_=Pb.ap[:, 1536:2048], func=AF.Copy, scale=0.125), reads=[Pb], writes=[R_sb])
            ph.op("pool", lambda e, tb=tb, Pb=Pb: e.tensor_copy(out=R_sb.ap[:, tb, 2048:2112], in_=Pb.ap[:, 2048:2112]), reads=[Pb], writes=[R_sb])
            ph.op("pool", lambda e, tb=tb, Pb=Pb, vst=vst: e.tensor_copy(out=vst.ap[:, tb, :, 0:64], in_=Pb.ap[:, 1280:1536].rearrange("p (g d) -> p g d", d=64)),
                  reads=[Pb], writes=[vst])
            ph.op("pool", lambda e, tb=tb, Pb=Pb, wist=wist: e.tensor_scalar(out=wist.ap[:, tb, :], in0=Pb.ap[:, 2112:2120], scalar1=8.0 ** -0.5, scalar2=None, op0=ALU.mult),
                  reads=[Pb], writes=[wist])
        heads = [(h * 64) for h in range(20)] + [1536 + h * 64 for h in range(9)]
        ts = slice(tt * T, (tt + 1) * T)
        for (h0, h1) in ((0, 16), (16, 29)):
            for hi in range(h0, h1):
                c0 = heads[hi]
                pt = ps_t[npt % 2]
                npt += 1
                for tb in range(4):
                    ph.op("pe", lambda e, tb=tb, c0=c0, pt=pt: e.transpose(pt.ap[0:64, tb * 128:(tb + 1) * 128], R_sb.ap[:, tb, c0:c0 + 64], ident.ap[:]),
                          reads=[R_sb, ident], writes=[pt])
                if hi % 2 == 0:
                    ph.op("act", lambda e, hi=hi, h0=h0, pt=pt, hdst=hdst: e.activation(out=hdst.ap[:, hi - h0, :], in_=pt.ap[0:64, 0:512], func=AF.Copy), reads=[pt], writes=[hdst])
                else:
                    ph.op("dve", lambda e, hi=hi, h0=h0, pt=pt, hdst=hdst: e.tensor_copy(out=hdst.ap[:, hi - h0, :], in_=pt.ap[0:64, 0:512]), reads=[pt], writes=[hdst])
            if h0 == 0:
                ph.dma("sp", lambda e, hdst=hdst, ts=ts: e.dma_start(out=t["qT"][:, :, ts], in_=hdst.ap[:, 0:16, :]), hdst, reads=[hdst])
            else:
                ph.dma("sp", lambda e, hdst=hdst, ts=ts: e.dma_start(out=t["kT"][:, :, ts], in_=hdst.ap[:, 0:4, :]), hdst, reads=[hdst])
                ph.dma("sp", lambda e, hdst=hdst, ts=ts: e.dma_start(out=t["qiT"][:, :, ts], in_=hdst.ap[:, 4:12, :]), hdst, reads=[hdst])
                ph.dma("sp", lambda e, hdst=hdst, ts=ts: e.dma_start(out=t["kiT"][:, ts], in_=hdst.ap[:, 12, :]), hdst, reads=[hdst])
        ph.dma("sp", lambda e, vst=vst, ts=ts: e.dma_start(out=t["vtm"][ts].rearrange("(tb p) g d -> p tb g d", p=128), in_=vst.ap[:]), vst, reads=[vst])
        ph.dma("sp", lambda e, wist=wist, ts=ts: e.dma_start(out=t["witm"][ts].rearrange("(tb p) c -> p tb c", p=128), in_=wist.ap[:]), wist, reads=[wist])
    return ph.finish()


LCH = 8


def phaseB(P, l):
    nc, t, S = P.nc, P.t, P.S
    ph = Phase(nc, f"B{l}", P.pool)
    NCH = S // LCH
    NK = int(round(math.log2(NCH)))
    assert (1 << NK) == NCH
    PW = min(512, NCH)
    NPART = NCH // PW

    def sb(name, shape, dt=F32):
        return ph.sbuf(name, shape, dt)

    lam_re, lam_im, ldt = sb("lam_re", [128, 16]), sb("lam_im", [128, 16]), sb("ldt", [128, 16])
    Bre, Bim, Cre, Cim = sb("Bre", [128, 16, 16]), sb("Bim", [128, 16, 16]), sb("Cre", [128, 16, 16]), sb("Cim", [128, 16, 16])
    d_sb = sb("d", [128, 4])
    zmask = sb("zmask", [128, 4, 128])
    ident_f = sb("identf", [128, 128])
    ident_b = sb("identb", [128, 128], BF16)
    for g2 in range(2):
        ps = slice(g2 * 64, (g2 + 1) * 64)
        ph.dma("sp", lambda e, ps=ps, g2=g2: e.dma_start(out=lam_re.ap[ps, :], in_=t["ssm_a_re"][l].rearrange("(q g2) n -> g2 n q", g2=2)[g2],
                                                         allow_slow_non_contiguous=True), lam_re, writes=[lam_re])
        ph.dma("sp", lambda e, ps=ps, g2=g2: e.dma_start(out=lam_im.ap[ps, :], in_=t["ssm_a_im"][l].rearrange("(q g2) n -> g2 n q", g2=2)[g2],
                                                         allow_slow_non_contiguous=True), lam_im, writes=[lam_im])
        ph.dma("sp", lambda e, ps=ps, g2=g2: e.dma_start(out=ldt.ap[ps, :], in_=t["ssm_log_dt"][l].rearrange("(q g2) -> g2 q", g2=2)[g2:g2 + 1, :].to_broadcast([64, 16]),
                                                         allow_slow_non_contiguous=True), ldt, writes=[ldt])
        for (dst, nm) in ((Bre, "ssm_b_re"), (Bim, "ssm_b_im"), (Cre, "ssm_c_re"), (Cim, "ssm_c_im")):
            ph.dma("sp", lambda e, ps=ps, g2=g2, dst=dst, nm=nm: e.dma_start(out=dst.ap[ps, :, :], in_=t[nm][l].rearrange("(q g2) n p -> g2 n q p", g2=2)[g2]),
                   dst, writes=[dst])
    ph.dma("sp", lambda e: e.dma_start(out=d_sb.ap[:], in_=t["ssm_d"][l].rearrange("(b p) -> p b", p=128), allow_slow_non_contiguous=True), d_sb, writes=[d_sb])
    ph.dma("sp", lambda e: e.dma_start(out=zmask.ap[:], in_=t["c_zmask"]), zmask, writes=[zmask])
    ph.dma("sp", lambda e: e.dma_start(out=ident_f.ap[:], in_=t["c_ident_f"]), ident_f, writes=[ident_f])
    ph.dma("sp", lambda e: e.dma_start(out=ident_b.ap[:], in_=t["c_ident_bf"]), ident_b, writes=[ident_b])

    tmpc = [sb(f"tc{i}", [128, 16]) for i in range(6)]
    dt_sb, mag = sb("dt", [128, 16]), sb("mag", [128, 16])
    red = sb("red", [128, 2, 16])
    ki, kf, msk = sb("ki", [128, 2, 16], I32), sb("kf", [128, 2, 16]), sb("msk", [128, 2, 16])
    cs = sb("cs", [128, 2, 16])
    pw_re, pw_im = sb("pw_re", [128, 9, 16]), sb("pw_im", [128, 9, 16])
    sc_re, sc_im, sc_nim = sb("sc_re", [128, NK, 16]), sb("sc_im", [128, NK, 16]), sb("sc_nim", [128, NK, 16])
    cf_re, cf_im = sb("cf_re", [128, 16]), sb("cf_im", [128, 16])
    Bb_re, Bb_im = sb("Bb_re", [128, 16, 16]), sb("Bb_im", [128, 16, 16])

    def tt(out, obuf, a, abuf, b, bbuf, op):
        ph.op("dve", lambda e: e.tensor_tensor(out=out, in0=a, in1=b, op=op), reads=[abuf, bbuf], writes=[obuf])

    def cmul(o_re, o_im, obuf_re, obuf_im, a_re, a_im, abufs, b_re, b_im, bbufs):
        tcs = [x.ap[:] for x in tmpc]
        rb = list(abufs) + list(bbufs)
        ph.op("dve", lambda e: e.tensor_tensor(out=tcs[0], in0=a_re, in1=b_re, op=ALU.mult), reads=rb, writes=[tmpc[0]])
        ph.op("dve", lambda e: e.tensor_tensor(out=tcs[1], in0=a_im, in1=b_im, op=ALU.mult), reads=rb, writes=[tmpc[1]])
        ph.op("dve", lambda e: e.tensor_tensor(out=tcs[2], in0=a_re, in1=b_im, op=ALU.mult), reads=rb, writes=[tmpc[2]])
        ph.op("dve", lambda e: e.tensor_tensor(out=tcs[3], in0=a_im, in1=b_re, op=ALU.mult), reads=rb, writes=[tmpc[3]])
        ph.op("dve", lambda e: e.tensor_tensor(out=o_re, in0=tcs[0], in1=tcs[1], op=ALU.subtract), reads=[tmpc[0], tmpc[1]], writes=[obuf_re])
        ph.op("dve", lambda e: e.tensor_tensor(out=o_im, in0=tcs[2], in1=tcs[3], op=ALU.add), reads=[tmpc[2], tmpc[3]], writes=[obuf_im])

    ph.op("act", lambda e: e.activation(out=dt_sb.ap[:], in_=ldt.ap[:], func=AF.Exp), reads=[ldt], writes=[dt_sb])
    tt(mag.ap[:], mag, dt_sb.ap[:], dt_sb, lam_re.ap[:], lam_re, ALU.mult)
    ph.op("act", lambda e: e.activation(out=mag.ap[:], in_=mag.ap[:], func=AF.Exp), reads=[mag], writes=[mag])
    tt(red.ap[:, 1, :], red, dt_sb.ap[:], dt_sb, lam_im.ap[:], lam_im, ALU.mult)
    ph.op("dve", lambda e: e.tensor_scalar(out=red.ap[:, 0, :], in0=red.ap[:, 1, :], scalar1=0.5 * math.pi, scalar2=None, op0=ALU.add), reads=[red], writes=[red])
    emit_range_reduce(ph, red, ki, kf, msk)
    ph.op("act", lambda e: e.activation(out=cs.ap[:], in_=red.ap[:], func=AF.Sin), reads=[red], writes=[cs])
    ph.op("dve", lambda e: e.memset(pw_re.ap[:, 0, :], 1.0), writes=[pw_re])
    ph.op("dve", lambda e: e.memset(pw_im.ap[:, 0, :], 0.0), writes=[pw_im])
    tt(pw_re.ap[:, 1, :], pw_re, mag.ap[:], mag, cs.ap[:, 0, :], cs, ALU.mult)
    tt(pw_im.ap[:, 1, :], pw_im, mag.ap[:], mag, cs.ap[:, 1, :], cs, ALU.mult)
    for j in range(1, 8):
        cmul(pw_re.ap[:, j + 1, :], pw_im.ap[:, j + 1, :], pw_re, pw_im, pw_re.ap[:, j, :], pw_im.ap[:, j, :], [pw_re, pw_im],
             pw_re.ap[:, 1, :], pw_im.ap[:, 1, :], [pw_re, pw_im])
    ph.op("dve", lambda e: e.tensor_copy(out=sc_re.ap[:, 0, :], in_=pw_re.ap[:, 8, :]), reads=[pw_re], writes=[sc_re])
    ph.op("dve", lambda e: e.tensor_copy(out=sc_im.ap[:, 0, :], in_=pw_im.ap[:, 8, :]), reads=[pw_im], writes=[sc_im])
    for k in range(NK - 1):
        cmul(sc_re.ap[:, k + 1, :], sc_im.ap[:, k + 1, :], sc_re, sc_im, sc_re.ap[:, k, :], sc_im.ap[:, k, :], [sc_re, sc_im],
             sc_re.ap[:, k, :], sc_im.ap[:, k, :], [sc_re, sc_im])
    ph.op("dve", lambda e: e.tensor_scalar(out=sc_nim.ap[:], in0=sc_im.ap[:], scalar1=-1.0, scalar2=None, op0=ALU.mult), reads=[sc_im], writes=[sc_nim])
    nr, den = sb("nr", [128, 16]), sb("den", [128, 16])
    ph.op("dve", lambda e: e.tensor_scalar(out=nr.ap[:], in0=pw_re.ap[:, 1, :], scalar1=-1.0, scalar2=None, op0=ALU.add), reads=[pw_re], writes=[nr])
    tt(den.ap[:], den, lam_re.ap[:], lam_re, lam_re.ap[:], lam_re, ALU.mult)
    tt(tmpc[4].ap[:], tmpc[4], lam_im.ap[:], lam_im, lam_im.ap[:], lam_im, ALU.mult)
    tt(den.ap[:], den, den.ap[:], den, tmpc[4].ap[:], tmpc[4], ALU.add)
    ph.op("dve", lambda e: e.reciprocal(out=den.ap[:], in_=den.ap[:]), reads=[den], writes=[den])
    tt(tmpc[0].ap[:], tmpc[0], nr.ap[:], nr, lam_re.ap[:], lam_re, ALU.mult)
    tt(tmpc[1].ap[:], tmpc[1], pw_im.ap[:, 1, :], pw_im, lam_im.ap[:], lam_im, ALU.mult)
    tt(tmpc[2].ap[:], tmpc[2], pw_im.ap[:, 1, :], pw_im, lam_re.ap[:], lam_re, ALU.mult)
    tt(tmpc[3].ap[:], tmpc[3], nr.ap[:], nr, lam_im.ap[:], lam_im, ALU.mult)
    tt(cf_re.ap[:], cf_re, tmpc[0].ap[:], tmpc[0], tmpc[1].ap[:], tmpc[1], ALU.add)
    tt(cf_im.ap[:], cf_im, tmpc[2].ap[:], tmpc[2], tmpc[3].ap[:], tmpc[3], ALU.subtract)
    tt(cf_re.ap[:], cf_re, cf_re.ap[:], cf_re, den.ap[:], den, ALU.mult)
    tt(cf_im.ap[:], cf_im, cf_im.ap[:], cf_im, den.ap[:], den, ALU.mult)
    tb3 = [sb(f"tb3{i}", [128, 16, 16]) for i in range(4)]

    def bc(ap2):
        return ap2.unsqueeze(2).to_broadcast([128, 16, 16])

    tt(tb3[0].ap[:], tb3[0], Bre.ap[:], Bre, bc(cf_re.ap[:]), cf_re, ALU.mult)
    tt(tb3[1].ap[:], tb3[1], Bim.ap[:], Bim, bc(cf_im.ap[:]), cf_im, ALU.mult)
    tt(tb3[2].ap[:], tb3[2], Bim.ap[:], Bim, bc(cf_re.ap[:]), cf_re, ALU.mult)
    tt(tb3[3].ap[:], tb3[3], Bre.ap[:], Bre, bc(cf_im.ap[:]), cf_im, ALU.mult)
    tt(Bb_re.ap[:], Bb_re, tb3[0].ap[:], tb3[0], tb3[1].ap[:], tb3[1], ALU.subtract)
    tt(Bb_im.ap[:], Bb_im, tb3[2].ap[:], tb3[2], tb3[3].ap[:], tb3[3], ALU.add)

    E_re, E_im = sb("E_re", [128, 8, 4, 16]), sb("E_im", [128, 8, 4, 16])
    G_re, G_nim = sb("G_re", [128, 9, 4, 16]), sb("G_nim", [128, 9, 4, 16])
    et = [sb(f"et{i}", [128, 4, 16]) for i in range(4)]
    ZE_re, ZE_im = sb("ZE_re", [128, 8, 4, 128], BF16), sb("ZE_im", [128, 8, 4, 128], BF16)
    ZG_re, ZG_nim = sb("ZG_re", [128, 9, 4, 128], BF16), sb("ZG_nim", [128, 9, 4, 128], BF16)
    W_re, W_im = sb("W_re", [128, 8, 4, 128], BF16), sb("W_im", [128, 8, 4, 128], BF16)
    Kb = sb("Kb", [128, 8, 128], BF16)
    u_sb = sb("u", [128, S], BF16)
    S_re = [sb(f"S_re{i}", [128, NCH]) for i in range(4)]
    S_im = [sb(f"S_im{i}", [128, NCH]) for i in range(4)]
    T_re, T_im = sb("T_re", [128, NCH]), sb("T_im", [128, NCH])
    t1, t2 = sb("t1", [128, NCH]), sb("t2", [128, NCH])
    Xb_re = [sb(f"Xb_re{i}", [128, NCH], BF16) for i in range(4)]
    Xb_im = [sb(f"Xb_im{i}", [128, NCH], BF16) for i in range(4)]
    y_sb = sb("y", [128, S])
    GW = min(1024, S)
    g1, g2b = sb("g1", [128, GW]), sb("g2", [128, GW])
    z_bf = sb("zbf", [128, GW], BF16)
    ps_s = [ph.psum(f"pss{i}", [128, 512], F32) for i in range(4)]
    ps_y = [ph.psum(f"psy{i}", [128, 512], F32) for i in range(2)]
    ps_t = [ph.psum(f"pst{i}", [128, 128], BF16) for i in range(1)]
    ps_k = [ph.psum(f"psk{i}", [128, 128], F32) for i in range(1)]
    nps = 0
    npy = 0

    for b in range(4):
        ph.dma("sp", lambda e, b=b: e.dma_start(out=u_sb.ap[:], in_=t["uT"][b * 128:(b + 1) * 128, :]), u_sb, writes=[u_sb])
        prs = slice(4 * b, 4 * b + 4)
        for j in range(9):
            pr = pw_re.ap[:, j, prs].unsqueeze(2).to_broadcast([128, 4, 16])
            pi = pw_im.ap[:, j, prs].unsqueeze(2).to_broadcast([128, 4, 16])
            if j < 8:
                tt(et[0].ap[:], et[0], Bb_re.ap[:, prs, :], Bb_re, pr, pw_re, ALU.mult)
                tt(et[1].ap[:], et[1], Bb_im.ap[:, prs, :], Bb_im, pi, pw_im, ALU.mult)
                tt(et[2].ap[:], et[2], Bb_im.ap[:, prs, :], Bb_im, pr, pw_re, ALU.mult)
                tt(et[3].ap[:], et[3], Bb_re.ap[:, prs, :], Bb_re, pi, pw_im, ALU.mult)
                tt(E_re.ap[:, j], E_re, et[0].ap[:], et[0], et[1].ap[:], et[1], ALU.subtract)
                tt(E_im.ap[:, j], E_im, et[2].ap[:], et[2], et[3].ap[:], et[3], ALU.add)
            tt(et[0].ap[:], et[0], Cre.ap[:, prs, :], Cre, pr, pw_re, ALU.mult)
            tt(et[1].ap[:], et[1], Cim.ap[:, prs, :], Cim, pi, pw_im, ALU.mult)
            tt(et[2].ap[:], et[2], Cim.ap[:, prs, :], Cim, pr, pw_re, ALU.mult)
            tt(et[3].ap[:], et[3], Cre.ap[:, prs, :], Cre, pi, pw_im, ALU.mult)
            tt(G_re.ap[:, j], G_re, et[0].ap[:], et[0], et[1].ap[:], et[1], ALU.subtract)
            ph.op("dve", lambda e, j=j: e.scalar_tensor_tensor(out=G_nim.ap[:, j], in0=et[2].ap[:], scalar=-1.0, in1=et[3].ap[:], op0=ALU.mult, op1=ALU.subtract),
                  reads=[et[2], et[3]], writes=[G_nim])
        for i in range(4):
            mk = zmask.ap[:, i, :].rearrange("p (a c) -> p a c", c=16)
            for j in range(9):
                for (src, dst) in (((E_re, ZE_re), (E_im, ZE_im)) if j < 8 else ()) + ((G_re, ZG_re), (G_nim, ZG_nim)):
                    s_ap = src.ap[:, j, i, :].unsqueeze(1).to_broadcast([128, 8, 16])
                    d_ap = dst.ap[:, j, i, :].rearrange("p (a c) -> p a c", c=16)
                    ph.op("dve", lambda e, s_ap=s_ap, d_ap=d_ap, mk=mk: e.tensor_tensor(out=d_ap, in0=s_ap, in1=mk, op=ALU.mult),
                          reads=[src, zmask], writes=[dst])
        nw = 0
        for (zs, wd) in ((ZE_re, W_re), (ZE_im, W_im)):
            for j in range(8):
                for i in range(4):
                    pt = ps_t[0]
                    ph.op("pe", lambda e, zs=zs, j=j, i=i, pt=pt: e.transpose(pt.ap[:, 0:128], zs.ap[:, j, i, :], ident_b.ap[:]), reads=[zs, ident_b], writes=[pt])
                    if nw % 2 == 0:
                        ph.op("act", lambda e, wd=wd, j=j, i=i, pt=pt: e.activation(out=wd.ap[:, j, i, :], in_=pt.ap[:, 0:128], func=AF.Copy), reads=[pt], writes=[wd])
                    else:
                        ph.op("dve", lambda e, wd=wd, j=j, i=i, pt=pt: e.tensor_copy(out=wd.ap[:, j, i, :], in_=pt.ap[:, 0:128]), reads=[pt], writes=[wd])
                    nw += 1
        for j in range(8):
            pk = ps_k[0]
            n = 0
            for i in range(4):
                for (za, zb) in ((ZE_re, ZG_re), (ZE_im, ZG_nim)):
                    ph.op("pe", lambda e, za=za, zb=zb, j=j, i=i, n=n, pk=pk: e.matmul(pk.ap[:, 0:128], za.ap[:, j, i, :], zb.ap[:, 0, i, :], start=(n == 0), stop=(n == 7)),
                          reads=[za, zb], writes=[pk])
                    n += 1
            if j == 0:
                ph.op("dve", lambda e, pk=pk, b=b: e.scalar_tensor_tensor(out=Kb.ap[:, 0, :], in0=ident_f.ap[:], scalar=d_sb.ap[:, b:b + 1], in1=pk.ap[:, 0:128],
                                                                         op0=ALU.mult, op1=ALU.add), reads=[pk, ident_f, d_sb], writes=[Kb])
            else:
                ph.op("act", lambda e, pk=pk, j=j: e.activation(out=Kb.ap[:, j, :], in_=pk.ap[:, 0:128], func=AF.Copy), reads=[pk], writes=[Kb])
        uv = u_sb.ap[:].rearrange("p (c l) -> p c l", l=LCH)
        for i in range(4):
            for (wd, Sd) in ((W_re, S_re[i]), (W_im, S_im[i])):
                for part in range(NPART):
                    cs_ = slice(part * PW, (part + 1) * PW)
                    pss = ps_s[nps % 4]
                    nps += 1
                    for tl in range(LCH):
                        ph.op("pe", lambda e, wd=wd, i=i, tl=tl, cs_=cs_, pss=pss: e.matmul(pss.ap[:, 0:PW], wd.ap[:, LCH - 1 - tl, i, :], uv[:, cs_, tl],
                                                                                         start=(tl == 0), stop=(tl == LCH - 1)),
                              reads=[wd, u_sb], writes=[pss])
                    ph.op("act", lambda e, Sd=Sd, cs_=cs_, pss=pss: e.activation(out=Sd.ap[:, cs_], in_=pss.ap[:, 0:PW], func=AF.Copy), reads=[pss], writes=[Sd])
        for i in range(4):
            pair = 4 * b + i
            cur = (S_re[i], S_im[i])
            nxt = (T_re, T_im)
            for k in range(NK):
                s = 1 << k
                pr = sc_re.ap[:, k, pair:pair + 1]
                pi = sc_im.ap[:, k, pair:pair + 1]
                npi = sc_nim.ap[:, k, pair:pair + 1]
                cr, ci = cur
                nr_, ni_ = nxt
                ph.op("dve", lambda e, cr=cr, pr=pr, s=s: e.scalar_tensor_tensor(out=t1.ap[:, s:], in0=cr.ap[:, :NCH - s], scalar=pr, in1=cr.ap[:, s:], op0=ALU.mult, op1=ALU.add),
                      reads=[cr, sc_re], writes=[t1])
                ph.op("dve", lambda e, ci=ci, npi=npi, nr_=nr_, s=s: e.scalar_tensor_tensor(out=nr_.ap[:, s:], in0=ci.ap[:, :NCH - s], scalar=npi, in1=t1.ap[:, s:], op0=ALU.mult, op1=ALU.add),
                      reads=[ci, sc_nim, t1], writes=[nr_])
                ph.op("dve", lambda e, ci=ci, pr=pr, s=s: e.scalar_tensor_tensor(out=t2.ap[:, s:], in0=ci.ap[:, :NCH - s], scalar=pr, in1=ci.ap[:, s:], op0=ALU.mult, op1=ALU.add),
                      reads=[ci, sc_re], writes=[t2])
                ph.op("dve", lambda e, cr=cr, pi=pi, ni_=ni_, s=s: e.scalar_tensor_tensor(out=ni_.ap[:, s:], in0=cr.ap[:, :NCH - s], scalar=pi, in1=t2.ap[:, s:], op0=ALU.mult, op1=ALU.add),
                      reads=[cr, sc_im, t2], writes=[ni_])
                ph.op("pool", lambda e, cr=cr, nr_=nr_, s=s: e.tensor_copy(out=nr_.ap[:, 0:s], in_=cr.ap[:, 0:s]), reads=[cr], writes=[nr_])
                ph.op("pool", lambda e, ci=ci, ni_=ni_, s=s: e.tensor_copy(out=ni_.ap[:, 0:s], in_=ci.ap[:, 0:s]), reads=[ci], writes=[ni_])
                cur, nxt = nxt, cur
            fr, fi = cur
            ph.op("pool", lambda e, i=i: e.memset(Xb_re[i].ap[:, 0:1], 0.0), writes=[Xb_re[i]])
            ph.op("pool", lambda e, i=i: e.memset(Xb_im[i].ap[:, 0:1], 0.0), writes=[Xb_im[i]])
            ph.op("act", lambda e, i=i, fr=fr: e.activation(out=Xb_re[i].ap[:, 1:], in_=fr.ap[:, :NCH - 1], func=AF.Copy), reads=[fr], writes=[Xb_re[i]])
            ph.op("act", lambda e, i=i, fi=fi: e.activation(out=Xb_im[i].ap[:, 1:], in_=fi.ap[:, :NCH - 1], func=AF.Copy), reads=[fi], writes=[Xb_im[i]])
        yv = y_sb.ap[:].rearrange("p (c l) -> p c l", l=LCH)
        for tl in range(LCH):
            for part in range(NPART):
                cs_ = slice(part * PW, (part + 1) * PW)
                py = ps_y[npy % 2]
                npy += 1
                nmm = (tl + 1) + 8
                n = 0
                for j in range(tl + 1):
                    ph.op("pe", lambda e, j=j, tl=tl, cs_=cs_, py=py, n=n, nmm=nmm: e.matmul(py.ap[:, 0:PW], Kb.ap[:, j, :], uv[:, cs_, tl - j], start=(n == 0), stop=(n == nmm - 1)),
                          reads=[Kb, u_sb], writes=[py])
                    n += 1
                for i in range(4):
                    for (zg, xb) in ((ZG_re, Xb_re[i]), (ZG_nim, Xb_im[i])):
                        ph.op("pe", lambda e, zg=zg, xb=xb, i=i, tl=tl, cs_=cs_, py=py, n=n, nmm=nmm: e.matmul(py.ap[:, 0:PW], zg.ap[:, tl + 1, i, :], xb.ap[:, cs_],
                                                                                                          start=(n == 0), stop=(n == nmm - 1)),
                              reads=[zg, xb], writes=[py])
                        n += 1
                ph.op("act", lambda e, tl=tl, cs_=cs_, py=py: e.activation(out=yv[:, cs_, tl], in_=py.ap[:, 0:PW], func=AF.Copy), reads=[py], writes=[y_sb])
        for c0 in range(0, S, GW):
            ysl = y_sb.ap[:, c0:c0 + GW]
            ph.op("act", lambda e, ysl=ysl: e.activation(out=g1.ap[:], in_=ysl, func=AF.Square), reads=[y_sb], writes=[g1])
            ph.op("dve", lambda e: e.tensor_scalar(out=g1.ap[:], in0=g1.ap[:], scalar1=0.044715, scalar2=1.0, op0=ALU.mult, op1=ALU.add), reads=[g1], writes=[g1])
            ph.op("dve", lambda e, ysl=ysl: e.tensor_tensor(out=g2b.ap[:], in0=g1.ap[:], in1=ysl, op=ALU.mult), reads=[g1, y_sb], writes=[g2b])
            ph.op("act", lambda e: e.activation(out=g1.ap[:], in_=g2b.ap[:], func=AF.Sigmoid, scale=1.5957691216057308), reads=[g2b], writes=[g1])
            ph.op("dve", lambda e, ysl=ysl: e.tensor_tensor(out=z_bf.ap[:], in0=g1.ap[:], in1=ysl, op=ALU.mult), reads=[g1, y_sb], writes=[z_bf])
            ph.dma("sp", lambda e, b=b, c0=c0: e.dma_start(out=t["zT"][b * 128:(b + 1) * 128, c0:c0 + GW], in_=z_bf.ap[:]), z_bf, reads=[z_bf])
    return ph.finish()


def phaseB3(P, l):
    nc, t, S = P.nc, P.t, P.S
    ph = Phase(nc, f"G{l}", P.pool)
    T = 512
    wg = ph.sbuf("wg", [128, 4, 512], BF16)
    ph.dma("sp", lambda e: e.dma_start(out=wg.ap[:], in_=t["b_ssm_glu"][l].rearrange("(kb p) n -> p kb n", p=128)), wg, writes=[wg])
    zb = [ph.sbuf(f"z{i}", [128, 4, T], BF16) for i in range(2)]
    sg = [ph.sbuf(f"sg{i}", [128, T], F32) for i in range(2)]
    yo = [ph.sbuf(f"yo{i}", [128, 4, T], BF16) for i in range(2)]
    pg = [ph.psum(f"pg{i}", [128, T], F32) for i in range(2)]
    n = 0
    for tt_ in range(S // T):
        z = zb[tt_ % 2]
        y = yo[tt_ % 2]
        ts = slice(tt_ * T, (tt_ + 1) * T)
        ph.dma("sp", lambda e, z=z, ts=ts: e.dma_start(out=z.ap[:], in_=t["zT"].rearrange("(kb p) s -> p kb s", p=128)[:, :, ts]), z, writes=[z])
        for ob in range(4):
            p_ = pg[n % 2]
            s_ = sg[n % 2]
            n += 1
            for kb in range(4):
                ph.op("pe", lambda e, kb=kb, ob=ob, p_=p_, z=z: e.matmul(p_.ap[:], wg.ap[:, kb, ob * 128:(ob + 1) * 128], z.ap[:, kb, :], start=(kb == 0), stop=(kb == 3)),
                      reads=[wg, z], writes=[p_])
            ph.op("act", lambda e, p_=p_, s_=s_: e.activation(out=s_.ap[:], in_=p_.ap[:], func=AF.Sigmoid), reads=[p_], writes=[s_])
            ph.op("dve", lambda e, ob=ob, z=z, y=y, s_=s_: e.tensor_tensor(out=y.ap[:, ob, :], in0=z.ap[:, ob, :], in1=s_.ap[:], op=ALU.mult), reads=[z, s_], writes=[y])
        ph.dma("sp", lambda e, y=y, ts=ts: e.dma_start(out=t["ybT"].rearrange("(kb p) s -> p kb s", p=128)[:, :, ts], in_=y.ap[:]), y, reads=[y])
    return ph.finish()


NBIS = 20


def phaseC(P, l, qb0, qb1):
    nc, t, S = P.nc, P.t, P.S
    ph = Phase(nc, f"C{l}_{qb0}", P.pool)
    NKMAX = qb1 * 128
    NKB = qb1
    kT_sb = ph.sbuf("kT", [128, 2, NKMAX], BF16)
    kiT_sb = ph.sbuf("kiT", [64, NKMAX], BF16)
    v_sb = ph.sbuf("v", [128, NKB, 4, 128], BF16)
    causal = ph.sbuf("causal", [128, 128], F32)
    ident = ph.sbuf("ident", [128, 128], BF16)
    sel = ph.sbuf("sel", [128, 64], F32)
    half = ph.sbuf("half", [128, 1], F32)
    for hp in range(2):
        ph.dma("sp", lambda e, hp=hp: e.dma_start(out=kT_sb.ap[hp * 64:(hp + 1) * 64, :, :], in_=t["kT"][:, 2 * hp:2 * hp + 2, 0:NKMAX]), kT_sb, writes=[kT_sb])
    ph.dma("sp", lambda e: e.dma_start(out=kiT_sb.ap[:], in_=t["kiT"][:, 0:NKMAX]), kiT_sb, writes=[kiT_sb])
    for k0 in range(0, NKB, 16):
        k1 = min(NKB, k0 + 16)
        ph.dma("sp", lambda e, k0=k0, k1=k1: e.dma_start(out=v_sb.ap[:, k0:k1], in_=t["vtm"][k0 * 128:k1 * 128].rearrange("(kb p) g d -> p kb g d", p=128)),
               v_sb, writes=[v_sb])
    ph.dma("sp", lambda e: e.dma_start(out=causal.ap[:], in_=t["c_causal"]), causal, writes=[causal])
    ph.dma("sp", lambda e: e.dma_start(out=ident.ap[:], in_=t["c_ident_bf"]), ident, writes=[ident])
    ph.dma("sp", lambda e: e.dma_start(out=sel.ap[:], in_=t["c_sel"]), sel, writes=[sel])
    ph.op("dve", lambda e: e.memset(half.ap[:], 0.5), writes=[half])

    qblk = [ph.sbuf(f"q{i}", [128, 8, 128], BF16) for i in range(2)]
    qiblk = [ph.sbuf(f"qi{i}", [64, 8, 128], BF16) for i in range(2)]
    wi_sb = [ph.sbuf(f"wi{i}", [128, 8], F32) for i in range(2)]
    acc = ph.sbuf("acc", [128, NKMAX], F32)
    mask = ph.sbuf("mask", [128, NKMAX], BF16)
    maskT = ph.sbuf("maskT", [128, NKB, 128], BF16)
    rl = [ph.sbuf(f"rl{i}", [128, 512], F32) for i in range(2)]
    pT = [ph.sbuf(f"pT{i}", [128, 4, 128], BF16) for i in range(3)]
    pm = [ph.sbuf(f"pm{i}", [128, 4, 128], BF16) for i in range(3)]
    osb = [ph.sbuf(f"osb{i}", [128, 512], F32) for i in range(2)]
    rec = [ph.sbuf(f"rec{i}", [64, 512], F32) for i in range(2)]
    yst = [ph.sbuf(f"yst{i}", [64, 4, 128], BF16) for i in range(2)]
    lo, hi, mid, ge, dd, ee = [ph.sbuf(nm, [128, 1], F32) for nm in ("lo", "hi", "mid", "ge", "dd", "ee")]
    cnt = ph.sbuf("cnt", [128, NBIS], F32)
    ps_o = [ph.psum(f"po{g}", [128, 512], F32) for g in range(4)]
    ps_s = [ph.psum(f"ps{i}", [128, 512], F32) for i in range(2)]
    ps_m = [ph.psum(f"psm{i}", [128, 512], F32) for i in range(2)]
    ps_mt = [ps_m[i].ap[:].bitcast(BF16) for i in range(2)]
    nm_ = 0
    nsc = 0
    npt = 0

    for qb in range(qb0, qb1):
        nk = (qb + 1) * 128
        qs = slice(qb * 128, (qb + 1) * 128)
        qbk, qik, wik = qblk[qb % 2], qiblk[qb % 2], wi_sb[qb % 2]
        for hp in range(2):
            ph.dma("sp", lambda e, hp=hp, qbk=qbk, qs=qs: e.dma_start(out=qbk.ap[hp * 64:(hp + 1) * 64, :, :], in_=t["qT"][:, 8 * hp:8 * hp + 8, qs]), qbk, writes=[qbk])
        ph.dma("sp", lambda e, qik=qik, qs=qs: e.dma_start(out=qik.ap[:], in_=t["qiT"][:, :, qs]), qik, writes=[qik])
        ph.dma("sp", lambda e, wik=wik, qs=qs: e.dma_start(out=wik.ap[:], in_=t["witm"][qs, :]), wik, writes=[wik])
        for k0 in range(0, nk, 512):
            w = min(512, nk - k0)
            for h in range(8):
                pi_ = ps_m[nm_ % 2]
                r_ = rl[nm_ % 2]
                nm_ += 1
                ph.op("pe", lambda e, pi_=pi_, h=h, k0=k0, w=w, qik=qik: e.matmul(pi_.ap[:, 0:w], qik.ap[:, h, :], kiT_sb.ap[:, k0:k0 + w], start=True, stop=True),
                      reads=[qik, kiT_sb], writes=[pi_])
                ph.op("act", lambda e, pi_=pi_, r_=r_, w=w: e.activation(out=r_.ap[:, 0:w], in_=pi_.ap[:, 0:w], func=AF.Relu), reads=[pi_], writes=[r_])
                if h == 0:
                    ph.op("dve", lambda e, r_=r_, k0=k0, w=w, wik=wik: e.tensor_scalar(out=acc.ap[:, k0:k0 + w], in0=r_.ap[:, 0:w], scalar1=wik.ap[:, 0:1], scalar2=None, op0=ALU.mult),
                          reads=[r_, wik], writes=[acc])
                else:
                    ph.op("dve", lambda e, r_=r_, k0=k0, w=w, h=h, wik=wik: e.scalar_tensor_tensor(out=acc.ap[:, k0:k0 + w], in0=r_.ap[:, 0:w], scalar=wik.ap[:, h:h + 1],
                                                                                                 in1=acc.ap[:, k0:k0 + w], op0=ALU.mult, op1=ALU.add),
                          reads=[r_, wik, acc], writes=[acc])
        ph.op("dve", lambda e, nk=nk: e.tensor_reduce(out=lo.ap[:], in_=acc.ap[:, 0:nk], axis=AX.X, op=ALU.min), reads=[acc], writes=[lo])
        ph.op("dve", lambda e, qs=qs: e.tensor_tensor(out=acc.ap[:, qs], in0=acc.ap[:, qs], in1=causal.ap[:], op=ALU.add), reads=[acc, causal], writes=[acc])
        ph.op("dve", lambda e, nk=nk: e.tensor_reduce(out=hi.ap[:], in_=acc.ap[:, 0:nk], axis=AX.X, op=ALU.max), reads=[acc], writes=[hi])
        ph.op("dve", lambda e: e.tensor_scalar(out=hi.ap[:], in0=hi.ap[:], scalar1=1.0, scalar2=None, op0=ALU.add), reads=[hi], writes=[hi])
        ph.op("dve", lambda e: e.memset(cnt.ap[:], 0.0), writes=[cnt])
        for it in range(NBIS):
            ph.op("dve", lambda e: e.scalar_tensor_tensor(out=mid.ap[:], in0=lo.ap[:], scalar=hi.ap[:, 0:1], in1=half.ap[:], op0=ALU.add, op1=ALU.mult),
                  reads=[lo, hi, half], writes=[mid])
            ph.op("dve", lambda e, nk=nk, it=it: e.tensor_scalar(out=mask.ap[:, 0:nk], in0=acc.ap[:, 0:nk], scalar1=mid.ap[:, 0:1], scalar2=0.0, op0=ALU.is_ge, op1=ALU.add,
                                                                accum_out=cnt.ap[:, it:it + 1]), reads=[acc, mid], writes=[mask, cnt])
            ph.op("dve", lambda e, it=it: e.tensor_scalar(out=ge.ap[:], in0=cnt.ap[:, it:it + 1], scalar1=TOPK - 0.5, scalar2=None, op0=ALU.is_ge), reads=[cnt], writes=[ge])
            ph.op("dve", lambda e: e.tensor_tensor(out=dd.ap[:], in0=mid.ap[:], in1=lo.ap[:], op=ALU.subtract), reads=[mid, lo], writes=[dd])
            ph.op("dve", lambda e: e.tensor_tensor(out=ee.ap[:], in0=hi.ap[:], in1=mid.ap[:], op=ALU.subtract), reads=[mid, hi], writes=[ee])
            ph.op("dve", lambda e: e.scalar_tensor_tensor(out=lo.ap[:], in0=dd.ap[:], scalar=ge.ap[:, 0:1], in1=lo.ap[:], op0=ALU.mult, op1=ALU.add),
                  reads=[dd, ge, lo], writes=[lo])
            ph.op("dve", lambda e: e.scalar_tensor_tensor(out=hi.ap[:], in0=ee.ap[:], scalar=ge.ap[:, 0:1], in1=mid.ap[:], op0=ALU.mult, op1=ALU.add),
                  reads=[ee, ge, mid], writes=[hi])
        ph.op("dve", lambda e, nk=nk: e.tensor_scalar(out=mask.ap[:, 0:nk], in0=acc.ap[:, 0:nk], scalar1=lo.ap[:, 0:1], scalar2=None, op0=ALU.is_ge),
              reads=[acc, lo], writes=[mask])
        for kb0 in range(0, qb + 1, 4):
            kb1 = min(qb + 1, kb0 + 4)
            pmt_buf = ps_m[nm_ % 2]
            pmt = ps_mt[nm_ % 2]
            nm_ += 1
            for kb in range(kb0, kb1):
                ph.op("pe", lambda e, kb=kb, kb0=kb0, pmt=pmt: e.transpose(pmt[:, (kb - kb0) * 128:(kb - kb0 + 1) * 128], mask.ap[:, kb * 128:(kb + 1) * 128], ident.ap[:]),
                      reads=[mask, ident], writes=[pmt_buf])
            ph.op("act", lambda e, kb0=kb0, kb1=kb1, pmt=pmt: e.activation(out=maskT.ap[:, kb0:kb1, :], in_=pmt[:, 0:(kb1 - kb0) * 128].rearrange("p (a b) -> p a b", b=128), func=AF.Copy),
                  reads=[pmt_buf], writes=[maskT])
        for kb in range(qb + 1):
            for g in range(4):
                pb = (g // 2) * 64
                gi = g % 2
                ps_ = ps_s[nsc % 2]
                nsc += 1
                pT_ = pT[npt % 3]
                pm_ = pm[npt % 3]
                npt += 1
                ph.op("pe", lambda e, ps_=ps_, pb=pb, gi=gi, kb=kb, qbk=qbk: e.matmul(ps_.ap[:], kT_sb.ap[pb:pb + 64, gi, kb * 128:(kb + 1) * 128],
                                                                                    qbk.ap[pb:pb + 64, gi * 4:gi * 4 + 4, :], start=True, stop=True),
                      reads=[kT_sb, qbk], writes=[ps_])
                ph.op("act", lambda e, ps_=ps_, pT_=pT_: e.activation(out=pT_.ap[:].rearrange("p a b -> p (a b)"), in_=ps_.ap[:], func=AF.Exp), reads=[ps_], writes=[pT_])
                meng = "dve" if (g % 4) != 3 else "pool"
                ph.op(meng, lambda e, pT_=pT_, pm_=pm_, kb=kb: e.tensor_tensor(out=pm_.ap[:], in0=pT_.ap[:], in1=maskT.ap[:, kb, :].unsqueeze(1).to_broadcast([128, 4, 128]), op=ALU.mult),
                      reads=[pT_, maskT], writes=[pm_])
                ph.op("pe", lambda e, g=g, kb=kb, pm_=pm_, qb=qb: e.matmul(ps_o[g].ap[:], v_sb.ap[:, kb, g, :], pm_.ap[:].rearrange("p a b -> p (a b)"), start=(kb == 0), stop=(kb == qb)),
                      reads=[v_sb, pm_], writes=[ps_o[g]])
        for g in range(4):
            ob, rc, ys = osb[g % 2], rec[g % 2], yst[g % 2]
            ph.op("act", lambda e, g=g, ob=ob: e.activation(out=ob.ap[:], in_=ps_o[g].ap[:], func=AF.Copy), reads=[ps_o[g]], writes=[ob])
            pr_ = ps_m[nm_ % 2]
            nm_ += 1
            ph.op("pe", lambda e, pr_=pr_, ob=ob: e.matmul(pr_.ap[0:64, :], sel.ap[:], ob.ap[:], start=True, stop=True), reads=[sel, ob], writes=[pr_])
            ph.op("dve", lambda e, pr_=pr_, rc=rc: e.reciprocal(out=rc.ap[:], in_=pr_.ap[0:64, :]), reads=[pr_], writes=[rc])
            ph.op("dve", lambda e, ob=ob, rc=rc, ys=ys: e.tensor_tensor(out=ys.ap[:].rearrange("p a b -> p (a b)"), in0=ob.ap[0:64, :], in1=rc.ap[:], op=ALU.mult),
                  reads=[ob, rc], writes=[ys])
            ph.dma("sp", lambda e, g=g, ys=ys, qs=qs: e.dma_start(out=t["ycT"].rearrange("(h d) s -> d h s", d=64)[:, 4 * g:4 * g + 4, qs], in_=ys.ap[:]), ys, reads=[ys])
    return ph.finish()


G0 = 4168
NWB = 5


def phaseD(P, l, xsrc, last):
    nc, t, S = P.nc, P.t, P.S
    T = 512
    NT = S // T
    ph = Phase(nc, f"D{l}", P.pool)
    ones_bf = ph.sbuf("ones", [128, 128], BF16)
    eps_sb = ph.sbuf("eps", [128, 1], F32)
    g1_sb = ph.sbuf("g1", [128, KC], F32)
    g2_sb = ph.sbuf("g2", [128, KC], F32)
    g3_sb = ph.sbuf("g3", [128, KC], F32)
    scw = ph.sbuf("scw", [128, 4, 3], F32)
    fw = ph.sbuf("fw", [128, 88, 3], F32)
    ph.op("dve", lambda e: e.memset(ones_bf.ap[:], 1.0), writes=[ones_bf])
    ph.op("dve", lambda e: e.memset(eps_sb.ap[:], NORM_EPS), writes=[eps_sb])
    ph.dma("sp", lambda e: e.dma_start(out=g1_sb.ap[:], in_=t["norm_mix"][l].rearrange("(kc p) -> p kc", p=128), allow_slow_non_contiguous=True), g1_sb, writes=[g1_sb])
    ph.dma("sp", lambda e: e.dma_start(out=g2_sb.ap[:], in_=t["norm_ffn"][l].rearrange("(kc p) -> p kc", p=128), allow_slow_non_contiguous=True), g2_sb, writes=[g2_sb])
    ph.dma("sp", lambda e: e.dma_start(out=g3_sb.ap[:], in_=t["norm_final"].rearrange("(kc p) -> p kc", p=128), allow_slow_non_contiguous=True), g3_sb, writes=[g3_sb])
    for j in range(3):
        ph.dma("sp", lambda e, j=j: e.dma_start(out=scw.ap[:, :, j], in_=t["sc_conv"][l, j].rearrange("(c p) -> p c", p=128), allow_slow_non_contiguous=True), scw, writes=[scw])
        for r0 in range(0, 88, 22):
            ph.dma("sp", lambda e, r0=r0, j=j: e.dma_start(out=fw.ap[:, r0:r0 + 22, j], in_=t["ffn_conv"][l, j].rearrange("(r p) -> p r", p=128)[:, r0:r0 + 22],
                                                           allow_slow_non_contiguous=True), fw, writes=[fw])

    x_sb = ph.sbuf("x", [128, KC, T], F32)
    h_sb = ph.sbuf("h", [128, KC, T], BF16)
    big = ph.sbuf("big", [128, 22, T], BF16)
    ya = ph.sbuf("ya", [128, 4, T], BF16)
    cx = ph.sbuf("cx", [128, 4, T + 2], F32)
    yb = ph.sbuf("yb", [128, 4, T], BF16)
    yc = ph.sbuf("yc", [128, 8, T], BF16)
    sqb = [ph.sbuf(f"sq{i}", [128, T], BF16) for i in range(2)]
    rstd = ph.sbuf("rstd", [128, T], F32)
    cacc = [ph.sbuf(f"cacc{i}", [128, T], F32) for i in range(2)]
    sg = [ph.sbuf(f"sg{i}", [128, T], F32) for i in range(3)]
    mt = [ph.sbuf(f"mt{i}", [128, T], F32) for i in range(2)]
    tmpx = sg[2]
    sl = sg[0:2]
    ost = mt
    ug = [ph.sbuf(f"ug{i}", [128, T + 2], F32) for i in range(2)]
    uu = [ph.sbuf(f"uu{i}", [128, T + 2], F32) for i in range(2)]
    carry = ph.sbuf("carry", [128, 88, 2], F32)
    wb = [ph.sbuf(f"wb{i}", [128, KC * 512], BF16) for i in range(NWB)]
    ps_ss = ph.psum("ss", [128, T], F32)
    psg = [ph.psum(f"pg{i}", [128, T], F32) for i in range(7)]
    ph.op("pool", lambda e: e.memset(cx.ap[:], 0.0), writes=[cx])
    ph.op("pool", lambda e: e.memset(carry.ap[:], 0.0), writes=[carry])
    st = {"nw": 0, "np": 0}

    def wbuf():
        b = wb[st["nw"] % NWB]
        st["nw"] += 1
        return b

    def pbank():
        b = psg[st["np"] % 7]
        st["np"] += 1
        return b

    def load_cols(src2d, c0, ncols, krows=KC):
        b = wbuf()
        v = b.ap[:, 0:krows * ncols].rearrange("p (k n) -> p k n", n=ncols)
        ph.dma("sp", lambda e, v=v: e.dma_start(out=v, in_=src2d.rearrange("(k p) n -> p k n", p=128)[:, :, c0:c0 + ncols]), b, writes=[b])
        return b, v

    xsrc_r = xsrc.rearrange("(kc p) s -> p kc s", p=128)
    xdst_r = t["xres"].rearrange("(kc p) s -> p kc s", p=128)
    out_r = t["outT"].rearrange("(kc p) s -> p kc s", p=128)
    w_in = t["b_w_in"][l]
    for tt_ in range(NT):
        ts = slice(tt_ * T, (tt_ + 1) * T)
        for q4 in range(4):
            ph.dma("sp", lambda e, q4=q4, ts=ts: e.dma_start(out=x_sb.ap[:, q4 * 4:(q4 + 1) * 4, :], in_=xsrc_r[:, q4 * 4:(q4 + 1) * 4, ts]), x_sb, writes=[x_sb])
        ph.dma("sp", lambda e, ts=ts: e.dma_start(out=yb.ap[:], in_=t["ybT"].rearrange("(k p) s -> p k s", p=128)[:, :, ts]), yb, writes=[yb])
        ph.dma("sp", lambda e, ts=ts: e.dma_start(out=yc.ap[:], in_=t["ycT"].rearrange("(k p) s -> p k s", p=128)[:, :, ts]), yc, writes=[yc])
        emit_norm(ph, x_sb, h_sb, g1_sb, ones_bf, eps_sb, ps_ss, sqb, rstd, T)
        pcs = [load_cols(w_in, j * 512, 512) for j in range(3)]
        for c in range(4):
            pacc = []
            for j in range(3):
                pb_ = pbank()
                wbuf_, wv = pcs[j]
                for kc in range(KC):
                    ph.op("pe", lambda e, pb_=pb_, wv=wv, kc=kc, c=c: e.matmul(pb_.ap[:], wv[:, kc, c * 128:(c + 1) * 128], h_sb.ap[:, kc, :], start=(kc == 0), stop=(kc == KC - 1)),
                          reads=[wbuf_, h_sb], writes=[pb_])
                pacc.append(pb_)
            px, pbb, pcc = pacc
            ca = cacc[c % 2]
            ph.op("act", lambda e, px=px: e.activation(out=tmpx.ap[:], in_=px.ap[:], func=AF.Copy), reads=[px], writes=[tmpx])
            ph.op("dve", lambda e, c=c, pcc=pcc: e.tensor_tensor(out=cx.ap[:, c, 2:T + 2], in0=tmpx.ap[:], in1=pcc.ap[:], op=ALU.mult), reads=[tmpx, pcc], writes=[cx])
            ph.op("dve", lambda e, c=c, ca=ca: e.tensor_scalar(out=ca.ap[:], in0=cx.ap[:, c, 2:T + 2], scalar1=scw.ap[:, c, 2:3], scalar2=None, op0=ALU.mult), reads=[cx, scw], writes=[ca])
            ph.op("dve", lambda e, c=c, ca=ca: e.scalar_tensor_tensor(out=ca.ap[:], in0=cx.ap[:, c, 1:T + 1], scalar=scw.ap[:, c, 1:2], in1=ca.ap[:], op0=ALU.mult, op1=ALU.add),
                  reads=[cx, scw, ca], writes=[ca])
            ph.op("dve", lambda e, c=c, ca=ca: e.scalar_tensor_tensor(out=ca.ap[:], in0=cx.ap[:, c, 0:T], scalar=scw.ap[:, c, 0:1], in1=ca.ap[:], op0=ALU.mult, op1=ALU.add),
                  reads=[cx, scw, ca], writes=[ca])
            ph.op("dve", lambda e, c=c, ca=ca, pbb=pbb: e.tensor_tensor(out=ya.ap[:, c, :], in0=ca.ap[:], in1=pbb.ap[:], op=ALU.mult), reads=[ca, pbb], writes=[ya])
            ph.op("pool", lambda e, c=c: e.tensor_copy(out=cx.ap[:, c, 0:2], in_=cx.ap[:, c, T:T + 2]), reads=[cx], writes=[cx])
        for fg in range(4):
            gp = [load_cols(w_in, G0 + j * 2048 + fg * 512, 512) for j in range(3)]
            bb = wbuf()
            bv = bb.ap[:].rearrange("p (k n) -> p k n", n=512)
            ph.dma("sp", lambda e, bv=bv, fg=fg: e.dma_start(out=bv[:, 0:4, :], in_=t["b_w_branch_a"][l].rearrange("(k p) n -> p k n", p=128)[:, :, fg * 512:(fg + 1) * 512]), bb, writes=[bb])
            ph.dma("sp", lambda e, bv=bv, fg=fg: e.dma_start(out=bv[:, 4:8, :], in_=t["b_w_branch_b"][l].rearrange("(k p) n -> p k n", p=128)[:, :, fg * 512:(fg + 1) * 512]), bb, writes=[bb])
            ph.dma("sp", lambda e, bv=bv, fg=fg: e.dma_start(out=bv[:, 8:16, :], in_=t["b_w_branch_c"][l].rearrange("(k p) n -> p k n", p=128)[:, :, fg * 512:(fg + 1) * 512]), bb, writes=[bb])
            for cc in range(4):
                fc = fg * 4 + cc
                cs_ = slice(cc * 128, (cc + 1) * 128)
                sgs = []
                for j in range(3):
                    pb_ = pbank()
                    wbuf_, wv = gp[j]
                    for kc in range(KC):
                        ph.op("pe", lambda e, pb_=pb_, wv=wv, kc=kc, cs_=cs_: e.matmul(pb_.ap[:], wv[:, kc, cs_], h_sb.ap[:, kc, :], start=(kc == 0), stop=(kc == KC - 1)),
                              reads=[wbuf_, h_sb], writes=[pb_])
                    ph.op("act", lambda e, pb_=pb_, j=j: e.activation(out=sg[j].ap[:], in_=pb_.ap[:], func=AF.Sigmoid), reads=[pb_], writes=[sg[j]])
                prods = []
                for j, (src_, k0, nk_) in enumerate(((ya, 0, 4), (yb, 4, 4), (yc, 8, 8))):
                    pb_ = pbank()
                    for kk in range(nk_):
                        ph.op("pe", lambda e, pb_=pb_, kk=kk, k0=k0, nk_=nk_, src_=src_, cs_=cs_, bv=bv: e.matmul(pb_.ap[:], bv[:, k0 + kk, cs_], src_.ap[:, kk, :],
                                                                                                             start=(kk == 0), stop=(kk == nk_ - 1)),
                              reads=[bb, src_], writes=[pb_])
                    prods.append(pb_)
                ph.op("dve", lambda e, p0=prods[0]: e.tensor_tensor(out=mt[0].ap[:], in0=p0.ap[:], in1=sg[0].ap[:], op=ALU.mult), reads=[prods[0], sg[0]], writes=[mt[0]])
                ph.op("dve", lambda e, p1=prods[1]: e.tensor_tensor(out=mt[1].ap[:], in0=p1.ap[:], in1=sg[1].ap[:], op=ALU.mult), reads=[prods[1], sg[1]], writes=[mt[1]])
                ph.op("dve", lambda e: e.tensor_tensor(out=mt[0].ap[:], in0=mt[0].ap[:], in1=mt[1].ap[:], op=ALU.add), reads=[mt[0], mt[1]], writes=[mt[0]])
                ph.op("dve", lambda e, p2=prods[2]: e.tensor_tensor(out=mt[1].ap[:], in0=p2.ap[:], in1=sg[2].ap[:], op=ALU.mult), reads=[prods[2], sg[2]], writes=[mt[1]])
                ph.op("dve", lambda e, fc=fc: e.tensor_tensor(out=big.ap[:, fc, :], in0=mt[0].ap[:], in1=mt[1].ap[:], op=ALU.add), reads=[mt[0], mt[1]], writes=[big])
        for fg in range(4):
            wbuf_, wv = load_cols(t["b_w_out"][l], fg * 512, 512)
            for cc in range(4):
                fc = fg * 4 + cc
                pb_ = pbank()
                for kc in range(KC):
                    ph.op("pe", lambda e, pb_=pb_, wv=wv, kc=kc, cc=cc: e.matmul(pb_.ap[:], wv[:, kc, cc * 128:(cc + 1) * 128], big.ap[:, kc, :], start=(kc == 0), stop=(kc == KC - 1)),
                          reads=[wbuf_, big], writes=[pb_])
                ph.op("dve", lambda e, pb_=pb_, fc=fc: e.tensor_tensor(out=x_sb.ap[:, fc, :], in0=x_sb.ap[:, fc, :], in1=pb_.ap[:], op=ALU.add), reads=[x_sb, pb_], writes=[x_sb])
        emit_norm(ph, x_sb, h_sb, g2_sb, ones_bf, eps_sb, ps_ss, sqb, rstd, T)
        for hf in range(2):
            for i2 in range(11):
                ci0 = hf * 22 + i2 * 2
                b_ = wbuf()
                v_ = b_.ap[:].rearrange("p (k n) -> p k n", n=512)
                ph.dma("sp", lambda e, v_=v_, ci0=ci0: e.dma_start(out=v_[:, :, 0:256], in_=t["b_w_up"][l].rearrange("(k p) n -> p k n", p=128)[:, :, ci0 * 128:ci0 * 128 + 256]), b_, writes=[b_])
                ph.dma("sp", lambda e, v_=v_, ci0=ci0: e.dma_start(out=v_[:, :, 256:512], in_=t["b_w_up"][l].rearrange("(k p) n -> p k n", p=128)[:, :, D_FF + ci0 * 128:D_FF + ci0 * 128 + 256]), b_, writes=[b_])
                for s2 in range(2):
                    ci = ci0 + s2
                    ii = i2 * 2 + s2
                    res = []
                    for gu in range(2):
                        pb_ = pbank()
                        co = gu * 256 + s2 * 128
                        for kc in range(KC):
                            ph.op("pe", lambda e, pb_=pb_, v_=v_, kc=kc, co=co: e.matmul(pb_.ap[:], v_[:, kc, co:co + 128], h_sb.ap[:, kc, :], start=(kc == 0), stop=(kc == KC - 1)),
                                  reads=[b_, h_sb], writes=[pb_])
                        r = gu * 44 + ci
                        stg = (ug if gu == 0 else uu)[ii % 2]
                        ca = cacc[gu]
                        ph.op("pool", lambda e, stg=stg, r=r: e.tensor_copy(out=stg.ap[:, 0:2], in_=carry.ap[:, r, :]), reads=[carry], writes=[stg])
                        ph.op("act", lambda e, stg=stg, pb_=pb_: e.activation(out=stg.ap[:, 2:T + 2], in_=pb_.ap[:], func=AF.Copy), reads=[pb_], writes=[stg])
                        ph.op("pool", lambda e, stg=stg, r=r: e.tensor_copy(out=carry.ap[:, r, :], in_=stg.ap[:, T:T + 2]), reads=[stg], writes=[carry])
                        ph.op("dve", lambda e, stg=stg, ca=ca, r=r: e.tensor_scalar(out=ca.ap[:], in0=stg.ap[:, 2:T + 2], scalar1=fw.ap[:, r, 2:3], scalar2=None, op0=ALU.mult), reads=[stg, fw], writes=[ca])
                        ph.op("dve", lambda e, stg=stg, ca=ca, r=r: e.scalar_tensor_tensor(out=ca.ap[:], in0=stg.ap[:, 1:T + 1], scalar=fw.ap[:, r, 1:2], in1=ca.ap[:], op0=ALU.mult, op1=ALU.add),
                              reads=[stg, fw, ca], writes=[ca])
                        ph.op("dve", lambda e, stg=stg, ca=ca, r=r: e.scalar_tensor_tensor(out=ca.ap[:], in0=stg.ap[:, 0:T], scalar=fw.ap[:, r, 0:1], in1=ca.ap[:], op0=ALU.mult, op1=ALU.add),
                              reads=[stg, fw, ca], writes=[ca])
                        res.append(ca)
                    s_ = sl[ii % 2]
                    ph.op("act", lambda e, s_=s_, cg=res[0]: e.activation(out=s_.ap[:], in_=cg.ap[:], func=AF.Silu), reads=[res[0]], writes=[s_])
                    ph.op("dve", lambda e, s_=s_, cu=res[1], ii=ii: e.tensor_tensor(out=big.ap[:, ii, :], in0=s_.ap[:], in1=cu.ap[:], op=ALU.mult), reads=[s_, res[1]], writes=[big])
            for o2 in range(8):
                b_ = wbuf()
                v_ = b_.ap[:, 0:22 * 256].rearrange("p (k n) -> p k n", n=256)
                ph.dma("sp", lambda e, v_=v_, o2=o2, hf=hf: e.dma_start(out=v_, in_=t["b_w_down"][l, hf * 2816:(hf + 1) * 2816, :].rearrange("(k p) n -> p k n", p=128)[:, :, o2 * 256:(o2 + 1) * 256]),
                       b_, writes=[b_])
                for s2 in range(2):
                    oc = o2 * 2 + s2
                    pb_ = pbank()
                    for i in range(22):
                        ph.op("pe", lambda e, pb_=pb_, v_=v_, i=i, s2=s2: e.matmul(pb_.ap[:], v_[:, i, s2 * 128:(s2 + 1) * 128], big.ap[:, i, :], start=(i == 0), stop=(i == 21)),
                              reads=[b_, big], writes=[pb_])
                    ph.op("dve", lambda e, pb_=pb_, oc=oc: e.tensor_tensor(out=x_sb.ap[:, oc, :], in0=x_sb.ap[:, oc, :], in1=pb_.ap[:], op=ALU.add), reads=[x_sb, pb_], writes=[x_sb])
        if not last:
            for q4 in range(4):
                ph.dma("sp", lambda e, q4=q4, ts=ts: e.dma_start(out=xdst_r[:, q4 * 4:(q4 + 1) * 4, ts], in_=x_sb.ap[:, q4 * 4:(q4 + 1) * 4, :]), x_sb, reads=[x_sb])
        else:
            for kc in range(KC):
                sq = sqb[kc % 2]
                ph.op("act", lambda e, kc=kc, sq=sq: e.activation(out=sq.ap[:], in_=x_sb.ap[:, kc, :], func=AF.Square), reads=[x_sb], writes=[sq])
                ph.op("pe", lambda e, kc=kc, sq=sq: e.matmul(ps_ss.ap[:], ones_bf.ap[:], sq.ap[:], start=(kc == 0), stop=(kc == KC - 1)), reads=[sq, ones_bf], writes=[ps_ss])
            ph.op("act", lambda e: e.activation(out=rstd.ap[:], in_=ps_ss.ap[:], func=AF.Sqrt, bias=eps_sb.ap[:, 0:1], scale=1.0 / D), reads=[ps_ss, eps_sb], writes=[rstd])
            ph.op("dve", lambda e: e.reciprocal(out=rstd.ap[:], in_=rstd.ap[:]), reads=[rstd], writes=[rstd])
            for kc in range(KC):
                o_ = ost[kc % 2]
                ph.op("dve", lambda e, kc=kc, o_=o_: e.scalar_tensor_tensor(out=o_.ap[:], in0=x_sb.ap[:, kc, :], scalar=g3_sb.ap[:, kc:kc + 1], in1=rstd.ap[:], op0=ALU.mult, op1=ALU.mult),
                      reads=[x_sb, g3_sb, rstd], writes=[o_])
                ph.dma("sp", lambda e, kc=kc, o_=o_, ts=ts: e.dma_start(out=out_r[:, kc, ts], in_=o_.ap[:]), o_, reads=[o_])
    return ph.finish()


QCH = 16


def build_program(S, debug_outs=(), nl=DEPTH, last=True):
    P = Prog(S, debug_outs=debug_outs, nl=nl)
    counts = [phase0(P)]
    NQB = S // 128
    for l in range(nl):
        xsrc = P.t["xT"] if l == 0 else P.t["xres"]
        counts.append(phaseA(P, l, xsrc))
        counts.append(phaseB(P, l))
        counts.append(phaseB3(P, l))
        for qb0 in range(0, NQB, QCH):
            counts.append(phaseC(P, l, qb0, min(NQB, qb0 + QCH)))
        counts.append(phaseD(P, l, xsrc, last=(last and l == nl - 1)))
    P.counts = counts
    return P


def make_in_map(inputs, b, S, l0=0, nl=DEPTH, xT=None):
    m = {}
    if xT is None:
        m["xT"] = np.ascontiguousarray(np.asarray(inputs["x"])[b, :S].T)
    else:
        m["xT"] = np.ascontiguousarray(xT)
    m["pos"] = np.ascontiguousarray(np.asarray(inputs["positions"])[b, :S]).astype(np.int32)
    for k in W_SPECS:
        a = np.asarray(inputs[k], dtype=np.float32)
        if k != "norm_final":
            a = a[l0:l0 + nl]
        if k in ("ssm_c_re", "ssm_c_im"):
            a = np.transpose(a, (0, 1, 3, 2))
        m[k] = np.ascontiguousarray(a)
    m.update(host_consts())
    return m


FUSED = False


def kernel(**inputs):
    x = np.asarray(inputs["x"])
    B, S, _ = x.shape
    n_cores = 8
    out = np.empty((B, S, D), np.float32)
    if FUSED:
        P = build_program(S)
        in_maps = [make_in_map(inputs, c % B, S) for c in range(n_cores)]
        res = run_bass_kernel_spmd(P.nc, in_maps, core_ids=list(range(n_cores)))
        for b in range(B):
            out[b] = np.asarray(res.results[b]["outT"], dtype=np.float32).T
        return out
    xs = [None] * B
    for l in range(DEPTH):
        lastl = (l == DEPTH - 1)
        P = build_program(S, debug_outs=(() if lastl else ("xres",)), nl=1, last=lastl)
        in_maps = [make_in_map(inputs, c % B, S, l0=l, nl=1, xT=xs[c % B]) for c in range(n_cores)]
        res = run_bass_kernel_spmd(P.nc, in_maps, core_ids=list(range(n_cores)))
        if lastl:
            for b in range(B):
                out[b] = np.asarray(res.results[b]["outT"], dtype=np.float32).T
        else:
            xs = [np.asarray(res.results[b]["xres"], dtype=np.float32) for b in range(B)]
        del res, in_maps
    return out
```

```python
import math
from contextlib import ExitStack

import numpy as np
import ml_dtypes

import concourse.bass as bass
import concourse.mybir as mybir
from concourse.bass_utils import run_bass_kernel_spmd

F32 = mybir.dt.float32
BF16 = mybir.dt.bfloat16
I32 = mybir.dt.int32
AF = mybir.ActivationFunctionType
ALU = mybir.AluOpType
AX = mybir.AxisListType

D = 2048
KC = D // 128
DEPTH = 2
N_IN = 10312
D_FF = 5632
NEG = -1.0e30
TOPK = 256
NORM_EPS = 1e-6

ENGS = ("pe", "act", "dve", "pool", "sp")


class Buf:
    __slots__ = ("name", "w", "r", "dsem", "dcnt", "ap")

    def __init__(self, name, ap=None):
        self.name = name
        self.w = None
        self.r = {}
        self.dsem = None
        self.dcnt = 0
        self.ap = ap


class SemPool:
    def __init__(self, nc, n=96):
        self.stack = ExitStack()
        self.handles = {}
        self.free = []
        for i in range(n):
            key = f"s{i}"
            self.handles[key] = self.stack.enter_context(nc.semaphore(f"gs{i}"))
            self.free.append((0, i, key))

    def get(self):
        self.free.sort()
        cnt, i, key = self.free.pop(0)
        return key, cnt

    def put(self, key, cnt):
        if cnt < Phase.SEM_ROLL:
            self.free.append((cnt, int(key[1:]), key))


class Phase:
    SEM_ROLL = 30000

    def __init__(self, nc, name, pool):
        self.nc = nc
        self.name = name
        self.pool = pool
        self.stack = ExitStack()
        self.lists = {e: [] for e in ENGS}
        self.cnt = {e: 0 for e in ENGS}
        self.waited = {e: {} for e in ENGS}
        self.sems = pool.handles
        self.key_eng = {}
        self.curkey = {}
        self.retired = []
        for e in ENGS:
            self._roll(e)
        self.dma_bufs = []
        self.dma_done = []

    def _roll(self, e):
        key, cnt = self.pool.get()
        self.key_eng[key] = e
        self.curkey[e] = key
        self.cnt[e] = cnt

    def sbuf(self, name, shape, dtype):
        t = self.stack.enter_context(self.nc.sbuf_tensor(f"{self.name}_{name}", list(shape), dtype))
        return Buf(name, t)

    def psum(self, name, shape, dtype):
        n = 2048 // (2 if dtype == BF16 else 4)
        t = self.stack.enter_context(self.nc.psum_tensor(f"{self.name}_{name}", [128, n], dtype))
        return Buf(name, t)

    def token(self, name):
        return Buf(name)

    def _deps(self, eng, reads, writes):
        deps = {}
        ke = self.key_eng

        def add(k, v):
            if deps.get(k, 0) < v:
                deps[k] = v

        for b in reads:
            if b.w is not None:
                add(*b.w)
        for b in writes:
            if b.w is not None and ke.get(b.w[0]) != eng:
                add(*b.w)
            for k, v in b.r.items():
                if ke.get(k) != eng:
                    add(k, v)
        return deps

    def _emit_waits(self, eng, deps):
        wd = self.waited[eng]
        for k, v in deps.items():
            if wd.get(k, 0) < v:
                wd[k] = v
                self.lists[eng].append(("wait", self.sems[k], v))

    def op(self, eng, fn, reads=(), writes=()):
        deps = self._deps(eng, reads, writes)
        self._emit_waits(eng, deps)
        if self.cnt[eng] >= self.SEM_ROLL:
            self._roll(eng)
        self.cnt[eng] += 1
        key = self.curkey[eng]
        ev = (key, self.cnt[eng])
        self.lists[eng].append(("op", fn, self.sems[key], 1))
        for b in reads:
            if b.r.get(key, 0) < ev[1]:
                b.r[key] = ev[1]
        for b in writes:
            b.w = ev
            b.r = {}
        return ev

    def dma(self, q, fn, carrier, reads=(), writes=()):
        sb = carrier
        if sb.dsem is None or sb.dcnt >= self.SEM_ROLL:
            if sb.dsem is None:
                self.dma_bufs.append(sb)
            else:
                self.dma_done.append((sb.dsem, sb.dcnt))
            sb.dsem, sb.dcnt = self.pool.get()
        deps = self._deps(q, reads, writes)
        self._emit_waits(q, deps)
        sb.dcnt += 16
        ev = (sb.dsem, sb.dcnt)
        self.lists[q].append(("op", fn, self.sems[sb.dsem], 16))
        for b in reads:
            if b.r.get(ev[0], 0) < ev[1]:
                b.r[ev[0]] = ev[1]
        for b in writes:
            b.w = ev
            b.r = {}
        return ev

    def finish(self):
        deps = {}
        for b in self.dma_bufs:
            deps[b.dsem] = b.dcnt
        for k, v in self.dma_done:
            deps[k] = v
        self._emit_waits("sp", deps)
        lists = self.lists

        def run(engh, lst):
            for it in lst:
                if it[0] == "wait":
                    engh.wait_ge(it[1], it[2])
                else:
                    it[1](engh).then_inc(it[2], it[3])

        with self.nc.Block() as block:
            @block.tensor
            def _(e):
                run(e, lists["pe"])

            @block.scalar
            def _(e):
                run(e, lists["act"])

            @block.vector
            def _(e):
                run(e, lists["dve"])

            @block.gpsimd
            def _(e):
                run(e, lists["pool"])

            @block.sync
            def _(e):
                run(e, lists["sp"])
        n = sum(len(v) for v in lists.values())
        self.stack.close()
        for e in ENGS:
            self.pool.put(self.curkey[e], self.cnt[e])
        for b in self.dma_bufs:
            self.pool.put(b.dsem, b.dcnt)
        return n


def host_consts():
    c = {}
    c["c_ident_bf"] = np.eye(128, dtype=np.float32).astype(ml_dtypes.bfloat16)
    c["c_ident_f"] = np.eye(128, dtype=np.float32)
    inv_freq = (500000.0 ** (-np.arange(0, 16, 2, dtype=np.float32) / np.float32(16))).astype(np.float32)
    c["c_invf"] = np.tile(inv_freq[None, :], (128, 1)).astype(np.float32)
    tri = np.zeros((128, 128), np.float32)
    tri[np.triu_indices(128, 1)] = NEG
    c["c_causal"] = tri
    zm = np.zeros((128, 4, 4, 2, 16), np.float32)
    for i in range(4):
        for g2 in range(2):
            zm[g2 * 64:(g2 + 1) * 64, i, i, g2, :] = 1.0
    c["c_zmask"] = zm.reshape(128, 4, 128)
    sel = np.zeros((128, 64), np.float32)
    sel[64 + np.arange(64), np.arange(64)] = 1.0
    c["c_sel"] = sel
    return c


CONST_SPECS = {
    "c_ident_bf": ([128, 128], BF16),
    "c_ident_f": ([128, 128], F32),
    "c_invf": ([128, 8], F32),
    "c_causal": ([128, 128], F32),
    "c_zmask": ([128, 4, 128], F32),
    "c_sel": ([128, 64], F32),
}

W_SPECS = {
    "norm_mix": [DEPTH, D], "w_in": [DEPTH, D, N_IN], "sc_conv": [DEPTH, 3, 512],
    "ssm_a_re": [DEPTH, 32, 64], "ssm_a_im": [DEPTH, 32, 64], "ssm_log_dt": [DEPTH, 32],
    "ssm_b_re": [DEPTH, 32, 64, 16], "ssm_b_im": [DEPTH, 32, 64, 16],
    "ssm_c_re": [DEPTH, 32, 64, 16], "ssm_c_im": [DEPTH, 32, 64, 16],
    "ssm_d": [DEPTH, 512], "ssm_glu": [DEPTH, 512, 512],
    "w_branch_a": [DEPTH, 512, D], "w_branch_b": [DEPTH, 512, D], "w_branch_c": [DEPTH, 1024, D],
    "w_out": [DEPTH, D, D], "norm_ffn": [DEPTH, D], "w_up": [DEPTH, D, 2 * D_FF],
    "ffn_conv": [DEPTH, 3, 2 * D_FF], "w_down": [DEPTH, D_FF, D], "norm_final": [D],
}


def w_specs(nl):
    return {k: ([nl] + v[1:] if k != "norm_final" else v) for k, v in W_SPECS.items()}


BIG_W = ["w_in", "ssm_glu", "w_branch_a", "w_branch_b", "w_branch_c", "w_out", "w_up", "w_down"]


class Prog:
    def __init__(self, S, debug_outs=(), skip=(), nl=DEPTH):
        self.S = S
        self.nl = nl
        self.skip = tuple(skip)
        WS = w_specs(nl)
        self.WS = WS
        nc = bass.Bass("TRN2", target_bir_lowering=False)
        self.nc = nc
        self.pool = SemPool(nc)
        t = {}
        t["xT"] = nc.dram_tensor("xT", [D, S], F32, kind="ExternalInput").ap()
        t["pos"] = nc.dram_tensor("pos", [S], I32, kind="ExternalInput").ap()
        for k, shp in WS.items():
            if k in self.skip:
                continue
            t[k] = nc.dram_tensor(k, shp, F32, kind="ExternalInput").ap()
        for k, (shp, dt) in CONST_SPECS.items():
            t[k] = nc.dram_tensor(k, shp, dt, kind="ExternalInput").ap()
        t["outT"] = nc.dram_tensor("outT", [D, S], F32, kind=("Internal" if "xres" in debug_outs else "ExternalOutput")).ap()

        def scratch(name, shape, dt):
            kind = "ExternalOutput" if name in debug_outs else "Internal"
            t[name] = nc.dram_tensor(name, shape, dt, kind=kind).ap()

        for k in BIG_W:
            if k in self.skip:
                continue
            scratch("b_" + k, WS[k], BF16)
        scratch("cs", [S, 16], F32)
        scratch("xres", [D, S], F32)
        scratch("uT", [512, S], BF16)
        scratch("qT", [64, 16, S], BF16)
        scratch("kT", [64, 4, S], BF16)
        scratch("qiT", [64, 8, S], BF16)
        scratch("kiT", [64, S], BF16)
        scratch("vtm", [S, 4, 128], BF16)
        scratch("witm", [S, 8], F32)
        scratch("zT", [512, S], BF16)
        scratch("ybT", [512, S], BF16)
        scratch("ycT", [1024, S], BF16)
        self.t = t


def emit_range_reduce(ph, red, ki, kf, msk):
    two_pi = 2.0 * math.pi
    ph.op("dve", lambda e: e.tensor_scalar(out=kf.ap[:], in0=red.ap[:], scalar1=1.0 / two_pi, scalar2=None, op0=ALU.mult), reads=[red], writes=[kf])
    ph.op("dve", lambda e: e.tensor_copy(out=ki.ap[:], in_=kf.ap[:]), reads=[kf], writes=[ki])
    ph.op("dve", lambda e: e.tensor_copy(out=kf.ap[:], in_=ki.ap[:]), reads=[ki], writes=[kf])
    ph.op("dve", lambda e: e.scalar_tensor_tensor(out=red.ap[:], in0=kf.ap[:], scalar=-two_pi, in1=red.ap[:], op0=ALU.mult, op1=ALU.add), reads=[kf, red], writes=[red])
    ph.op("dve", lambda e: e.tensor_scalar(out=msk.ap[:], in0=red.ap[:], scalar1=math.pi, scalar2=None, op0=ALU.is_gt), reads=[red], writes=[msk])
    ph.op("dve", lambda e: e.scalar_tensor_tensor(out=red.ap[:], in0=msk.ap[:], scalar=-two_pi, in1=red.ap[:], op0=ALU.mult, op1=ALU.add), reads=[msk, red], writes=[red])
    ph.op("dve", lambda e: e.tensor_scalar(out=msk.ap[:], in0=red.ap[:], scalar1=-math.pi, scalar2=None, op0=ALU.is_lt), reads=[red], writes=[msk])
    ph.op("dve", lambda e: e.scalar_tensor_tensor(out=red.ap[:], in0=msk.ap[:], scalar=two_pi, in1=red.ap[:], op0=ALU.mult, op1=ALU.add), reads=[msk, red], writes=[red])


def phase0(P):
    nc, t, S = P.nc, P.t, P.S
    ph = Phase(nc, "p0", P.pool)
    CW = 4096
    NBUF = 3
    cin = [ph.sbuf(f"cin{i}", [128, CW], F32) for i in range(NBUF)]
    cout = [ph.sbuf(f"cout{i}", [128, CW], BF16) for i in range(NBUF)]
    n = 0
    for k in BIG_W:
        if k in P.skip:
            continue
        shp = W_SPECS[k]
        rows, cols = shp[1], shp[2]
        for l in range(P.nl):
            for r0 in range(0, rows, 128):
                for c0 in range(0, cols, CW):
                    w = min(CW, cols - c0)
                    bi, bo = cin[n % NBUF], cout[n % NBUF]
                    src = t[k][l, r0:r0 + 128, c0:c0 + w]
                    dst = t["b_" + k][l, r0:r0 + 128, c0:c0 + w]
                    ph.dma("sp", lambda e, src=src, bi=bi, w=w: e.dma_start(out=bi.ap[:, 0:w], in_=src), bi, writes=[bi])
                    eng = ("dve", "act", "pool")[n % 3]
                    if eng == "act":
                        ph.op("act", lambda e, bi=bi, bo=bo, w=w: e.activation(out=bo.ap[:, 0:w], in_=bi.ap[:, 0:w], func=AF.Copy), reads=[bi], writes=[bo])
                    else:
                        ph.op(eng, lambda e, bi=bi, bo=bo, w=w: e.tensor_copy(out=bo.ap[:, 0:w], in_=bi.ap[:, 0:w]), reads=[bi], writes=[bo])
                    ph.dma("sp", lambda e, dst=dst, bo=bo, w=w: e.dma_start(out=dst, in_=bo.ap[:, 0:w]), bo, reads=[bo])
                    n += 1
    NB = S // 128
    posi = ph.sbuf("posi", [128, NB], I32)
    posf = ph.sbuf("posf", [128, NB], F32)
    invf = ph.sbuf("invf", [128, 8], F32)
    ang = ph.sbuf("ang", [128, NB, 8], F32)
    red = ph.sbuf("red", [128, NB, 16], F32)
    cs = ph.sbuf("cs", [128, NB, 16], F32)
    negpi = ph.sbuf("negpi", [128, 1], F32)
    ph.dma("sp", lambda e: e.dma_start(out=posi.ap[:], in_=t["pos"].rearrange("(b p) -> p b", p=128),
                                       allow_slow_non_contiguous=True), posi, writes=[posi])
    ph.dma("sp", lambda e: e.dma_start(out=invf.ap[:], in_=t["c_invf"]), invf, writes=[invf])
    ph.op("dve", lambda e: e.memset(negpi.ap[:], -math.pi), writes=[negpi])
    ph.op("dve", lambda e: e.tensor_copy(out=posf.ap[:], in_=posi.ap[:]), reads=[posi], writes=[posf])
    ph.op("dve", lambda e: e.tensor_tensor(out=ang.ap[:], in0=posf.ap[:].unsqueeze(2).to_broadcast([128, NB, 8]),
                                           in1=invf.ap[:].unsqueeze(1).to_broadcast([128, NB, 8]), op=ALU.mult),
          reads=[posf, invf], writes=[ang])
    two_pi = 2.0 * math.pi
    ki = ph.sbuf("ki", [128, NB, 16], I32)
    kf = ph.sbuf("kf", [128, NB, 16], F32)
    msk = ph.sbuf("msk", [128, NB, 16], F32)
    ph.op("dve", lambda e: e.tensor_scalar(out=red.ap[:, :, 0:8], in0=ang.ap[:], scalar1=0.5 * math.pi, scalar2=None, op0=ALU.add), reads=[ang], writes=[red])
    ph.op("dve", lambda e: e.tensor_copy(out=red.ap[:, :, 8:16], in_=ang.ap[:]), reads=[ang], writes=[red])
    emit_range_reduce(ph, red, ki, kf, msk)
    ph.op("act", lambda e: e.activation(out=cs.ap[:], in_=red.ap[:], func=AF.Sin),
          reads=[red], writes=[cs])
    ph.dma("sp", lambda e: e.dma_start(out=t["cs"].rearrange("(b p) c -> p b c", p=128), in_=cs.ap[:]), cs, reads=[cs])
    return ph.finish()


def emit_norm(ph, x_sb, h_sb, g_sb, ones_bf, eps_sb, ps_ss, sq_bufs, rstd_sb, T):
    for kc in range(KC):
        sq = sq_bufs[kc % len(sq_bufs)]
        ph.op("act", lambda e, kc=kc, sq=sq: e.activation(out=sq.ap[:], in_=x_sb.ap[:, kc, :], func=AF.Square),
              reads=[x_sb], writes=[sq])
        ph.op("pe", lambda e, kc=kc, sq=sq: e.matmul(ps_ss.ap[:], ones_bf.ap[:], sq.ap[:], start=(kc == 0), stop=(kc == KC - 1)),
              reads=[sq, ones_bf], writes=[ps_ss])
    ph.op("act", lambda e: e.activation(out=rstd_sb.ap[:], in_=ps_ss.ap[:], func=AF.Sqrt, bias=eps_sb.ap[:, 0:1], scale=1.0 / D),
          reads=[ps_ss, eps_sb], writes=[rstd_sb])
    ph.op("dve", lambda e: e.reciprocal(out=rstd_sb.ap[:], in_=rstd_sb.ap[:]), reads=[rstd_sb], writes=[rstd_sb])
    for kc in range(KC):
        ph.op("dve", lambda e, kc=kc: e.scalar_tensor_tensor(out=h_sb.ap[:, kc, :], in0=x_sb.ap[:, kc, :], scalar=g_sb.ap[:, kc:kc + 1],
                                                            in1=rstd_sb.ap[:], op0=ALU.mult, op1=ALU.mult),
              reads=[x_sb, g_sb, rstd_sb], writes=[h_sb])


A_C0 = 1536
A_NC = 2632
A_TM = 2120


def phaseA(P, l, xsrc):
    nc, t, S = P.nc, P.t, P.S
    T = 512
    NT = S // T
    ph = Phase(nc, f"A{l}", P.pool)
    wA = ph.sbuf("wA", [128, KC, A_NC], BF16)
    for kc in range(KC):
        ph.dma("sp", lambda e, kc=kc: e.dma_start(out=wA.ap[:, kc, :], in_=t["b_w_in"][l, kc * 128:(kc + 1) * 128, A_C0:A_C0 + A_NC]),
               wA, writes=[wA])
    ones_bf = ph.sbuf("ones", [128, 128], BF16)
    ident = ph.sbuf("ident", [128, 128], BF16)
    eps_sb = ph.sbuf("eps", [128, 1], F32)
    g_sb = ph.sbuf("g", [128, KC], F32)
    cs_all = ph.sbuf("csall", [128, S // 128, 16], F32)
    ph.op("dve", lambda e: e.memset(ones_bf.ap[:], 1.0), writes=[ones_bf])
    ph.op("dve", lambda e: e.memset(eps_sb.ap[:], NORM_EPS), writes=[eps_sb])
    ph.dma("sp", lambda e: e.dma_start(out=ident.ap[:], in_=t["c_ident_bf"]), ident, writes=[ident])
    ph.dma("sp", lambda e: e.dma_start(out=g_sb.ap[:], in_=t["norm_mix"][l].rearrange("(kc p) -> p kc", p=128),
                                       allow_slow_non_contiguous=True), g_sb, writes=[g_sb])
    ph.dma("sp", lambda e: e.dma_start(out=cs_all.ap[:], in_=t["cs"].rearrange("(b p) c -> p b c", p=128)), cs_all, writes=[cs_all])

    xb = [ph.sbuf(f"x{i}", [128, KC, T], F32) for i in range(1)]
    h_sb = ph.sbuf("h", [128, KC, T], BF16)
    sqb = [ph.sbuf(f"sq{i}", [128, T], BF16) for i in range(2)]
    rstd = ph.sbuf("rstd", [128, T], F32)
    ps_ss = ph.psum("ss", [128, T], F32)
    ps_u = [ph.psum(f"pu{i}", [128, 512], F32) for i in range(2)]
    ps_p = [ph.psum(f"pp{i}", [128, 512], F32) for i in range(2)]
    ps_t = [ph.psum(f"pt{i}", [64, 512], BF16) for i in range(2)]
    u_st = [ph.sbuf(f"ust{i}", [128, 512], BF16) for i in range(2)]
    P_sb = [ph.sbuf(f"P{i}", [128, A_TM], F32) for i in range(2)]
    R_sb = ph.sbuf("R", [128, 4, A_TM - 8], BF16)
    tmp = [ph.sbuf(f"tmp{i}", [128, 20, 8], F32) for i in range(4)]
    v_st = [ph.sbuf(f"vst{i}", [128, 4, 4, 128], BF16) for i in range(2)]
    wi_st = [ph.sbuf(f"wist{i}", [128, 4, 8], F32) for i in range(2)]
    hd_st = [ph.sbuf(f"hdst{i}", [64, 16, 512], BF16) for i in range(1)]
    for i in range(2):
        ph.op("pool", lambda e, i=i: e.memset(v_st[i].ap[:], 1.0), writes=[v_st[i]])

    xsrc_r = xsrc.rearrange("(kc p) t -> p kc t", p=128)
    npu = 0
    npp = 0
    npt = 0
    for tt in range(NT):
        x_sb = xb[0]
        for q4 in range(4):
            ph.dma("sp", lambda e, q4=q4, tt=tt, x_sb=x_sb: e.dma_start(out=x_sb.ap[:, q4 * 4:(q4 + 1) * 4, :],
                                                                        in_=xsrc_r[:, q4 * 4:(q4 + 1) * 4, tt * T:(tt + 1) * T]),
                   x_sb, writes=[x_sb])
        emit_norm(ph, x_sb, h_sb, g_sb, ones_bf, eps_sb, ps_ss, sqb, rstd, T)
        for c in range(4):
            pu = ps_u[npu % 2]
            ust = u_st[npu % 2]
            npu += 1
            for kc in range(KC):
                ph.op("pe", lambda e, kc=kc, c=c, pu=pu: e.matmul(pu.ap[:], wA.ap[:, kc, c * 128:(c + 1) * 128], h_sb.ap[:, kc, :],
                                                                  start=(kc == 0), stop=(kc == KC - 1)),
                      reads=[wA, h_sb], writes=[pu])
            ph.op("act", lambda e, pu=pu, ust=ust: e.activation(out=ust.ap[:], in_=pu.ap[:], func=AF.Copy), reads=[pu], writes=[ust])
            ph.dma("sp", lambda e, c=c, tt=tt, ust=ust: e.dma_start(out=t["uT"][c * 128:(c + 1) * 128, tt * T:(tt + 1) * T], in_=ust.ap[:]),
                   ust, reads=[ust])
        vst = v_st[tt % 2]
        wist = wi_st[tt % 2]
        hdst = hd_st[0]
        for tb in range(4):
            Pb = P_sb[tb % 2]
            for ct in range(5):
                c0 = 512 + ct * 512
                c1 = min(A_NC, c0 + 512)
                w = c1 - c0
                pp = ps_p[npp % 2]
                npp += 1
                for kc in range(KC):
                    ph.op("pe", lambda e, kc=kc, tb=tb, c0=c0, c1=c1, w=w, pp=pp: e.matmul(
                        pp.ap[:, 0:w], h_sb.ap[:, kc, tb * 128:(tb + 1) * 128], wA.ap[:, kc, c0:c1],
                        start=(kc == 0), stop=(kc == KC - 1)), reads=[wA, h_sb], writes=[pp])
                ph.op("act", lambda e, w=w, pp=pp, ct=ct, Pb=Pb: e.activation(out=Pb.ap[:, ct * 512:ct * 512 + w], in_=pp.ap[:, 0:w], func=AF.Copy),
                      reads=[pp], writes=[Pb])
            blk = tt * 4 + tb
            for (c0, nh) in ((0, 20), (1536, 9)):
                hv = Pb.ap[:, c0:c0 + nh * 64].rearrange("p (h d) -> p h d", d=64)
                x1 = hv[:, :, 0:8]
                x2 = hv[:, :, 8:16]
                cosb = cs_all.ap[:, blk, 0:8].unsqueeze(1).to_broadcast([128, nh, 8])
                sinb = cs_all.ap[:, blk, 8:16].unsqueeze(1).to_broadcast([128, nh, 8])
                tm = [tmp[i].ap[:, 0:nh, :] for i in range(4)]
                ph.op("dve", lambda e, x1=x1, cosb=cosb, o=tm[0]: e.tensor_tensor(out=o, in0=x1, in1=cosb, op=ALU.mult), reads=[Pb, cs_all], writes=[tmp[0]])
                ph.op("dve", lambda e, x2=x2, sinb=sinb, o=tm[1]: e.tensor_tensor(out=o, in0=x2, in1=sinb, op=ALU.mult), reads=[Pb, cs_all], writes=[tmp[1]])
                ph.op("dve", lambda e, x2=x2, cosb=cosb, o=tm[2]: e.tensor_tensor(out=o, in0=x2, in1=cosb, op=ALU.mult), reads=[Pb, cs_all], writes=[tmp[2]])
                ph.op("dve", lambda e, x1=x1, sinb=sinb, o=tm[3]: e.tensor_tensor(out=o, in0=x1, in1=sinb, op=ALU.mult), reads=[Pb, cs_all], writes=[tmp[3]])
                ph.op("dve", lambda e, x1=x1, a=tm[0], b=tm[1]: e.tensor_tensor(out=x1, in0=a, in1=b, op=ALU.subtract), reads=[tmp[0], tmp[1]], writes=[Pb])
                ph.op("dve", lambda e, x2=x2, a=tm[2], b=tm[3]: e.tensor_tensor(out=x2, in0=a, in1=b, op=ALU.add), reads=[tmp[2], tmp[3]], writes=[Pb])
            ph.op("act", lambda e, tb=tb, Pb=Pb: e.activation(out=R_sb.ap[:, tb, 0:1024], in_=Pb.ap[:, 0:1024], func=AF.Copy, scale=0.125), reads=[Pb], writes=[R_sb])
            ph.op("pool", lambda e, tb=tb, Pb=Pb: e.tensor_copy(out=R_sb.ap[:, tb, 1024:1280], in_=Pb.ap[:, 1024:1280]), reads=[Pb], writes=[R_sb])
            ph.op("act", lambda e, tb=tb, Pb=Pb: e.activation(out=R_sb.ap[:, tb, 1536:2048], in_=Pb.ap[:, 1536:2048], func=AF.Copy, scale=0.125), reads=[Pb], writes=[R_sb])
            ph.op("pool", lambda e, tb=tb, Pb=Pb: e.tensor_copy(out=R_sb.ap[:, tb, 2048:2112], in_=Pb.ap[:, 2048:2112]), reads=[Pb], writes=[R_sb])
            ph.op("pool", lambda e, tb=tb, Pb=Pb, vst=vst: e.tensor_copy(out=vst.ap[:, tb, :, 0:64], in_=Pb.ap[:, 1280:1536].rearrange("p (g d) -> p g d", d=64)),
                  reads=[Pb], writes=[vst])
            ph.op("pool", lambda e, tb=tb, Pb=Pb, wist=wist: e.tensor_scalar(out=wist.ap[:, tb, :], in0=Pb.ap[:, 2112:2120], scalar1=8.0 ** -0.5, scalar2=None, op0=ALU.mult),
                  reads=[Pb], writes=[wist])
        heads = [(h * 64) for h in range(20)] + [1536 + h * 64 for h in range(9)]
        ts = slice(tt * T, (tt + 1) * T)
        for (h0, h1) in ((0, 16), (16, 29)):
            for hi in range(h0, h1):
                c0 = heads[hi]
                pt = ps_t[npt % 2]
                npt += 1
                for tb in range(4):
                    ph.op("pe", lambda e, tb=tb, c0=c0, pt=pt: e.transpose(pt.ap[0:64, tb * 128:(tb + 1) * 128], R_sb.ap[:, tb, c0:c0 + 64], ident.ap[:]),
                          reads=[R_sb, ident], writes=[pt])
                if hi % 2 == 0:
                    ph.op("act", lambda e, hi=hi, h0=h0, pt=pt, hdst=hdst: e.activation(out=hdst.ap[:, hi - h0, :], in_=pt.ap[0:64, 0:512], func=AF.Copy), reads=[pt], writes=[hdst])
                else:
                    ph.op("dve", lambda e, hi=hi, h0=h0, pt=pt, hdst=hdst: e.tensor_copy(out=hdst.ap[:, hi - h0, :], in_=pt.ap[0:64, 0:512]), reads=[pt], writes=[hdst])
            if h0 == 0:
                ph.dma("sp", lambda e, hdst=hdst, ts=ts: e.dma_start(out=t["qT"][:, :, ts], in_=hdst.ap[:, 0:16, :]), hdst, reads=[hdst])
            else:
                ph.dma("sp", lambda e, hdst=hdst, ts=ts: e.dma_start(out=t["kT"][:, :, ts], in_=hdst.ap[:, 0:4, :]), hdst, reads=[hdst])
                ph.dma("sp", lambda e, hdst=hdst, ts=ts: e.dma_start(out=t["qiT"][:, :, ts], in_=hdst.ap[:, 4:12, :]), hdst, reads=[hdst])
                ph.dma("sp", lambda e, hdst=hdst, ts=ts: e.dma_start(out=t["kiT"][:, ts], in_=hdst.ap[:, 12, :]), hdst, reads=[hdst])
        ph.dma("sp", lambda e, vst=vst, ts=ts: e.dma_start(out=t["vtm"][ts].rearrange("(tb p) g d -> p tb g d", p=128), in_=vst.ap[:]), vst, reads=[vst])
        ph.dma("sp", lambda e, wist=wist, ts=ts: e.dma_start(out=t["witm"][ts].rearrange("(tb p) c -> p tb c", p=128), in_=wist.ap[:]), wist, reads=[wist])
    return ph.finish()


LCH = 8


def phaseB(P, l):
    nc, t, S = P.nc, P.t, P.S
    ph = Phase(nc, f"B{l}", P.pool)
    NCH = S // LCH
    NK = int(round(math.log2(NCH)))
    assert (1 << NK) == NCH
    PW = min(512, NCH)
    NPART = NCH // PW

    def sb(name, shape, dt=F32):
        return ph.sbuf(name, shape, dt)

    lam_re, lam_im, ldt = sb("lam_re", [128, 16]), sb("lam_im", [128, 16]), sb("ldt", [128, 16])
    Bre, Bim, Cre, Cim = sb("Bre", [128, 16, 16]), sb("Bim", [128, 16, 16]), sb("Cre", [128, 16, 16]), sb("Cim", [128, 16, 16])
    d_sb = sb("d", [128, 4])
    zmask = sb("zmask", [128, 4, 128])
    ident_f = sb("identf", [128, 128])
    ident_b = sb("identb", [128, 128], BF16)
    for g2 in range(2):
        ps = slice(g2 * 64, (g2 + 1) * 64)
        ph.dma("sp", lambda e, ps=ps, g2=g2: e.dma_start(out=lam_re.ap[ps, :], in_=t["ssm_a_re"][l].rearrange("(q g2) n -> g2 n q", g2=2)[g2],
                                                         allow_slow_non_contiguous=True), lam_re, writes=[lam_re])
        ph.dma("sp", lambda e, ps=ps, g2=g2: e.dma_start(out=lam_im.ap[ps, :], in_=t["ssm_a_im"][l].rearrange("(q g2) n -> g2 n q", g2=2)[g2],
                                                         allow_slow_non_contiguous=True), lam_im, writes=[lam_im])
        ph.dma("sp", lambda e, ps=ps, g2=g2: e.dma_start(out=ldt.ap[ps, :], in_=t["ssm_log_dt"][l].rearrange("(q g2) -> g2 q", g2=2)[g2:g2 + 1, :].to_broadcast([64, 16]),
                                                         allow_slow_non_contiguous=True), ldt, writes=[ldt])
        for (dst, nm) in ((Bre, "ssm_b_re"), (Bim, "ssm_b_im"), (Cre, "ssm_c_re"), (Cim, "ssm_c_im")):
            ph.dma("sp", lambda e, ps=ps, g2=g2, dst=dst, nm=nm: e.dma_start(out=dst.ap[ps, :, :], in_=t[nm][l].rearrange("(q g2) n p -> g2 n q p", g2=2)[g2]),
                   dst, writes=[dst])
    ph.dma("sp", lambda e: e.dma_start(out=d_sb.ap[:], in_=t["ssm_d"][l].rearrange("(b p) -> p b", p=128), allow_slow_non_contiguous=True), d_sb, writes=[d_sb])
    ph.dma("sp", lambda e: e.dma_start(out=zmask.ap[:], in_=t["c_zmask"]), zmask, writes=[zmask])
    ph.dma("sp", lambda e: e.dma_start(out=ident_f.ap[:], in_=t["c_ident_f"]), ident_f, writes=[ident_f])
    ph.dma("sp", lambda e: e.dma_start(out=ident_b.ap[:], in_=t["c_ident_bf"]), ident_b, writes=[ident_b])

    tmpc = [sb(f"tc{i}", [128, 16]) for i in range(6)]
    dt_sb, mag = sb("dt", [128, 16]), sb("mag", [128, 16])
    red = sb("red", [128, 2, 16])
    ki, kf, msk = sb("ki", [128, 2, 16], I32), sb("kf", [128, 2, 16]), sb("msk", [128, 2, 16])
    cs = sb("cs", [128, 2, 16])
    pw_re, pw_im = sb("pw_re", [128, 9, 16]), sb("pw_im", [128, 9, 16])
    sc_re, sc_im, sc_nim = sb("sc_re", [128, NK, 16]), sb("sc_im", [128, NK, 16]), sb("sc_nim", [128, NK, 16])
    cf_re, cf_im = sb("cf_re", [128, 16]), sb("cf_im", [128, 16])
    Bb_re, Bb_im = sb("Bb_re", [128, 16, 16]), sb("Bb_im", [128, 16, 16])

    def tt(out, obuf, a, abuf, b, bbuf, op):
        ph.op("dve", lambda e: e.tensor_tensor(out=out, in0=a, in1=b, op=op), reads=[abuf, bbuf], writes=[obuf])

    def cmul(o_re, o_im, obuf_re, obuf_im, a_re, a_im, abufs, b_re, b_im, bbufs):
        tcs = [x.ap[:] for x in tmpc]
        rb = list(abufs) + list(bbufs)
        ph.op("dve", lambda e: e.tensor_tensor(out=tcs[0], in0=a_re, in1=b_re, op=ALU.mult), reads=rb, writes=[tmpc[0]])
        ph.op("dve", lambda e: e.tensor_tensor(out=tcs[1], in0=a_im, in1=b_im, op=ALU.mult), reads=rb, writes=[tmpc[1]])
        ph.op("dve", lambda e: e.tensor_tensor(out=tcs[2], in0=a_re, in1=b_im, op=ALU.mult), reads=rb, writes=[tmpc[2]])
        ph.op("dve", lambda e: e.tensor_tensor(out=tcs[3], in0=a_im, in1=b_re, op=ALU.mult), reads=rb, writes=[tmpc[3]])
        ph.op("dve", lambda e: e.tensor_tensor(out=o_re, in0=tcs[0], in1=tcs[1], op=ALU.subtract), reads=[tmpc[0], tmpc[1]], writes=[obuf_re])
        ph.op("dve", lambda e: e.tensor_tensor(out=o_im, in0=tcs[2], in1=tcs[3], op=ALU.add), reads=[tmpc[2], tmpc[3]], writes=[obuf_im])

    ph.op("act", lambda e: e.activation(out=dt_sb.ap[:], in_=ldt.ap[:], func=AF.Exp), reads=[ldt], writes=[dt_sb])
    tt(mag.ap[:], mag, dt_sb.ap[:], dt_sb, lam_re.ap[:], lam_re, ALU.mult)
    ph.op("act", lambda e: e.activation(out=mag.ap[:], in_=mag.ap[:], func=AF.Exp), reads=[mag], writes=[mag])
    tt(red.ap[:, 1, :], red, dt_sb.ap[:], dt_sb, lam_im.ap[:], lam_im, ALU.mult)
    ph.op("dve", lambda e: e.tensor_scalar(out=red.ap[:, 0, :], in0=red.ap[:, 1, :], scalar1=0.5 * math.pi, scalar2=None, op0=ALU.add), reads=[red], writes=[red])
    emit_range_reduce(ph, red, ki, kf, msk)
    ph.op("act", lambda e: e.activation(out=cs.ap[:], in_=red.ap[:], func=AF.Sin), reads=[red], writes=[cs])
    ph.op("dve", lambda e: e.memset(pw_re.ap[:, 0, :], 1.0), writes=[pw_re])
    ph.op("dve", lambda e: e.memset(pw_im.ap[:, 0, :], 0.0), writes=[pw_im])
    tt(pw_re.ap[:, 1, :], pw_re, mag.ap[:], mag, cs.ap[:, 0, :], cs, ALU.mult)
    tt(pw_im.ap[:, 1, :], pw_im, mag.ap[:], mag, cs.ap[:, 1, :], cs, ALU.mult)
    for j in range(1, 8):
        cmul(pw_re.ap[:, j + 1, :], pw_im.ap[:, j + 1, :], pw_re, pw_im, pw_re.ap[:, j, :], pw_im.ap[:, j, :], [pw_re, pw_im],
             pw_re.ap[:, 1, :], pw_im.ap[:, 1, :], [pw_re, pw_im])
    ph.op("dve", lambda e: e.tensor_copy(out=sc_re.ap[:, 0, :], in_=pw_re.ap[:, 8, :]), reads=[pw_re], writes=[sc_re])
    ph.op("dve", lambda e: e.tensor_copy(out=sc_im.ap[:, 0, :], in_=pw_im.ap[:, 8, :]), reads=[pw_im], writes=[sc_im])
    for k in range(NK - 1):
        cmul(sc_re.ap[:, k + 1, :], sc_im.ap[:, k + 1, :], sc_re, sc_im, sc_re.ap[:, k, :], sc_im.ap[:, k, :], [sc_re, sc_im],
             sc_re.ap[:, k, :], sc_im.ap[:, k, :], [sc_re, sc_im])
    ph.op("dve", lambda e: e.tensor_scalar(out=sc_nim.ap[:], in0=sc_im.ap[:], scalar1=-1.0, scalar2=None, op0=ALU.mult), reads=[sc_im], writes=[sc_nim])
    nr, den = sb("nr", [128, 16]), sb("den", [128, 16])
    ph.op("dve", lambda e: e.tensor_scalar(out=nr.ap[:], in0=pw_re.ap[:, 1, :], scalar1=-1.0, scalar2=None, op0=ALU.add), reads=[pw_re], writes=[nr])
    tt(den.ap[:], den, lam_re.ap[:], lam_re, lam_re.ap[:], lam_re, ALU.mult)
    tt(tmpc[4].ap[:], tmpc[4], lam_im.ap[:], lam_im, lam_im.ap[:], lam_im, ALU.mult)
    tt(den.ap[:], den, den.ap[:], den, tmpc[4].ap[:], tmpc[4], ALU.add)
    ph.op("dve", lambda e: e.reciprocal(out=den.ap[:], in_=den.ap[:]), reads=[den], writes=[den])
    tt(tmpc[0].ap[:], tmpc[0], nr.ap[:], nr, lam_re.ap[:], lam_re, ALU.mult)
    tt(tmpc[1].ap[:], tmpc[1], pw_im.ap[:, 1, :], pw_im, lam_im.ap[:], lam_im, ALU.mult)
    tt(tmpc[2].ap[:], tmpc[2], pw_im.ap[:, 1, :], pw_im, lam_re.ap[:], lam_re, ALU.mult)
    tt(tmpc[3].ap[:], tmpc[3], nr.ap[:], nr, lam_im.ap[:], lam_im, ALU.mult)
    tt(cf_re.ap[:], cf_re, tmpc[0].ap[:], tmpc[0], tmpc[1].ap[:], tmpc[1], ALU.add)
    tt(cf_im.ap[:], cf_im, tmpc[2].ap[:], tmpc[2], tmpc[3].ap[:], tmpc[3], ALU.subtract)
    tt(cf_re.ap[:], cf_re, cf_re.ap[:], cf_re, den.ap[:], den, ALU.mult)
    tt(cf_im.ap[:], cf_im, cf_im.ap[:], cf_im, den.ap[:], den, ALU.mult)
    tb3 = [sb(f"tb3{i}", [128, 16, 16]) for i in range(4)]

    def bc(ap2):
        return ap2.unsqueeze(2).to_broadcast([128, 16, 16])

    tt(tb3[0].ap[:], tb3[0], Bre.ap[:], Bre, bc(cf_re.ap[:]), cf_re, ALU.mult)
    tt(tb3[1].ap[:], tb3[1], Bim.ap[:], Bim, bc(cf_im.ap[:]), cf_im, ALU.mult)
    tt(tb3[2].ap[:], tb3[2], Bim.ap[:], Bim, bc(cf_re.ap[:]), cf_re, ALU.mult)
    tt(tb3[3].ap[:], tb3[3], Bre.ap[:], Bre, bc(cf_im.ap[:]), cf_im, ALU.mult)
    tt(Bb_re.ap[:], Bb_re, tb3[0].ap[:], tb3[0], tb3[1].ap[:], tb3[1], ALU.subtract)
    tt(Bb_im.ap[:], Bb_im, tb3[2].ap[:], tb3[2], tb3[3].ap[:], tb3[3], ALU.add)

    E_re, E_im = sb("E_re", [128, 8, 4, 16]), sb("E_im", [128, 8, 4, 16])
    G_re, G_nim = sb("G_re", [128, 9, 4, 16]), sb("G_nim", [128, 9, 4, 16])
    et = [sb(f"et{i}", [128, 4, 16]) for i in range(4)]
    ZE_re, ZE_im = sb("ZE_re", [128, 8, 4, 128], BF16), sb("ZE_im", [128, 8, 4, 128], BF16)
    ZG_re, ZG_nim = sb("ZG_re", [128, 9, 4, 128], BF16), sb("ZG_nim", [128, 9, 4, 128], BF16)
    W_re, W_im = sb("W_re", [128, 8, 4, 128], BF16), sb("W_im", [128, 8, 4, 128], BF16)
    Kb = sb("Kb", [128, 8, 128], BF16)
    u_sb = sb("u", [128, S], BF16)
    S_re = [sb(f"S_re{i}", [128, NCH]) for i in range(4)]
    S_im = [sb(f"S_im{i}", [128, NCH]) for i in range(4)]
    T_re, T_im = sb("T_re", [128, NCH]), sb("T_im", [128, NCH])
    t1, t2 = sb("t1", [128, NCH]), sb("t2", [128, NCH])
    Xb_re = [sb(f"Xb_re{i}", [128, NCH], BF16) for i in range(4)]
    Xb_im = [sb(f"Xb_im{i}", [128, NCH], BF16) for i in range(4)]
    y_sb = sb("y", [128, S])
    GW = min(1024, S)
    g1, g2b = sb("g1", [128, GW]), sb("g2", [128, GW])
    z_bf = sb("zbf", [128, GW], BF16)
    ps_s = [ph.psum(f"pss{i}", [128, 512], F32) for i in range(4)]
    ps_y = [ph.psum(f"psy{i}", [128, 512], F32) for i in range(2)]
    ps_t = [ph.psum(f"pst{i}", [128, 128], BF16) for i in range(1)]
    ps_k = [ph.psum(f"psk{i}", [128, 128], F32) for i in range(1)]
    nps = 0
    npy = 0

    for b in range(4):
        ph.dma("sp", lambda e, b=b: e.dma_start(out=u_sb.ap[:], in_=t["uT"][b * 128:(b + 1) * 128, :]), u_sb, writes=[u_sb])
        prs = slice(4 * b, 4 * b + 4)
        for j in range(9):
            pr = pw_re.ap[:, j, prs].unsqueeze(2).to_broadcast([128, 4, 16])
            pi = pw_im.ap[:, j, prs].unsqueeze(2).to_broadcast([128, 4, 16])
            if j < 8:
                tt(et[0].ap[:], et[0], Bb_re.ap[:, prs, :], Bb_re, pr, pw_re, ALU.mult)
                tt(et[1].ap[:], et[1], Bb_im.ap[:, prs, :], Bb_im, pi, pw_im, ALU.mult)
                tt(et[2].ap[:], et[2], Bb_im.ap[:, prs, :], Bb_im, pr, pw_re, ALU.mult)
                tt(et[3].ap[:], et[3], Bb_re.ap[:, prs, :], Bb_re, pi, pw_im, ALU.mult)
                tt(E_re.ap[:, j], E_re, et[0].ap[:], et[0], et[1].ap[:], et[1], ALU.subtract)
                tt(E_im.ap[:, j], E_im, et[2].ap[:], et[2], et[3].ap[:], et[3], ALU.add)
            tt(et[0].ap[:], et[0], Cre.ap[:, prs, :], Cre, pr, pw_re, ALU.mult)
            tt(et[1].ap[:], et[1], Cim.ap[:, prs, :], Cim, pi, pw_im, ALU.mult)
            tt(et[2].ap[:], et[2], Cim.ap[:, prs, :], Cim, pr, pw_re, ALU.mult)
            tt(et[3].ap[:], et[3], Cre.ap[:, prs, :], Cre, pi, pw_im, ALU.mult)
            tt(G_re.ap[:, j], G_re, et[0].ap[:], et[0], et[1].ap[:], et[1], ALU.subtract)
            ph.op("dve", lambda e, j=j: e.scalar_tensor_tensor(out=G_nim.ap[:, j], in0=et[2].ap[:], scalar=-1.0, in1=et[3].ap[:], op0=ALU.mult, op1=ALU.subtract),
                  reads=[et[2], et[3]], writes=[G_nim])
        for i in range(4):
            mk = zmask.ap[:, i, :].rearrange("p (a c) -> p a c", c=16)
            for j in range(9):
                for (src, dst) in (((E_re, ZE_re), (E_im, ZE_im)) if j < 8 else ()) + ((G_re, ZG_re), (G_nim, ZG_nim)):
                    s_ap = src.ap[:, j, i, :].unsqueeze(1).to_broadcast([128, 8, 16])
                    d_ap = dst.ap[:, j, i, :].rearrange("p (a c) -> p a c", c=16)
                    ph.op("dve", lambda e, s_ap=s_ap, d_ap=d_ap, mk=mk: e.tensor_tensor(out=d_ap, in0=s_ap, in1=mk, op=ALU.mult),
                          reads=[src, zmask], writes=[dst])
        nw = 0
        for (zs, wd) in ((ZE_re, W_re), (ZE_im, W_im)):
            for j in range(8):
                for i in range(4):
                    pt = ps_t[0]
                    ph.op("pe", lambda e, zs=zs, j=j, i=i, pt=pt: e.transpose(pt.ap[:, 0:128], zs.ap[:, j, i, :], ident_b.ap[:]), reads=[zs, ident_b], writes=[pt])
                    if nw % 2 == 0:
                        ph.op("act", lambda e, wd=wd, j=j, i=i, pt=pt: e.activation(out=wd.ap[:, j, i, :], in_=pt.ap[:, 0:128], func=AF.Copy), reads=[pt], writes=[wd])
                    else:
                        ph.op("dve", lambda e, wd=wd, j=j, i=i, pt=pt: e.tensor_copy(out=wd.ap[:, j, i, :], in_=pt.ap[:, 0:128]), reads=[pt], writes=[wd])
                    nw += 1
        for j in range(8):
            pk = ps_k[0]
            n = 0
            for i in range(4):
                for (za, zb) in ((ZE_re, ZG_re), (ZE_im, ZG_nim)):
                    ph.op("pe", lambda e, za=za, zb=zb, j=j, i=i, n=n, pk=pk: e.matmul(pk.ap[:, 0:128], za.ap[:, j, i, :], zb.ap[:, 0, i, :], start=(n == 0), stop=(n == 7)),
                          reads=[za, zb], writes=[pk])
                    n += 1
            if j == 0:
                ph.op("dve", lambda e, pk=pk, b=b: e.scalar_tensor_tensor(out=Kb.ap[:, 0, :], in0=ident_f.ap[:], scalar=d_sb.ap[:, b:b + 1], in1=pk.ap[:, 0:128],
                                                                         op0=ALU.mult, op1=ALU.add), reads=[pk, ident_f, d_sb], writes=[Kb])
            else:
                ph.op("act", lambda e, pk=pk, j=j: e.activation(out=Kb.ap[:, j, :], in_=pk.ap[:, 0:128], func=AF.Copy), reads=[pk], writes=[Kb])
        uv = u_sb.ap[:].rearrange("p (c l) -> p c l", l=LCH)
        for i in range(4):
            for (wd, Sd) in ((W_re, S_re[i]), (W_im, S_im[i])):
                for part in range(NPART):
                    cs_ = slice(part * PW, (part + 1) * PW)
                    pss = ps_s[nps % 4]
                    nps += 1
                    for tl in range(LCH):
                        ph.op("pe", lambda e, wd=wd, i=i, tl=tl, cs_=cs_, pss=pss: e.matmul(pss.ap[:, 0:PW], wd.ap[:, LCH - 1 - tl, i, :], uv[:, cs_, tl],
                                                                                         start=(tl == 0), stop=(tl == LCH - 1)),
                              reads=[wd, u_sb], writes=[pss])
                    ph.op("act", lambda e, Sd=Sd, cs_=cs_, pss=pss: e.activation(out=Sd.ap[:, cs_], in_=pss.ap[:, 0:PW], func=AF.Copy), reads=[pss], writes=[Sd])
        for i in range(4):
            pair = 4 * b + i
            cur = (S_re[i], S_im[i])
            nxt = (T_re, T_im)
            for k in range(NK):
                s = 1 << k
                pr = sc_re.ap[:, k, pair:pair + 1]
                pi = sc_im.ap[:, k, pair:pair + 1]
                npi = sc_nim.ap[:, k, pair:pair + 1]
                cr, ci = cur
                nr_, ni_ = nxt
                ph.op("dve", lambda e, cr=cr, pr=pr, s=s: e.scalar_tensor_tensor(out=t1.ap[:, s:], in0=cr.ap[:, :NCH - s], scalar=pr, in1=cr.ap[:, s:], op0=ALU.mult, op1=ALU.add),
                      reads=[cr, sc_re], writes=[t1])
                ph.op("dve", lambda e, ci=ci, npi=npi, nr_=nr_, s=s: e.scalar_tensor_tensor(out=nr_.ap[:, s:], in0=ci.ap[:, :NCH - s], scalar=npi, in1=t1.ap[:, s:], op0=ALU.mult, op1=ALU.add),
                      reads=[ci, sc_nim, t1], writes=[nr_])
                ph.op("dve", lambda e, ci=ci, pr=pr, s=s: e.scalar_tensor_tensor(out=t2.ap[:, s:], in0=ci.ap[:, :NCH - s], scalar=pr, in1=ci.ap[:, s:], op0=ALU.mult, op1=ALU.add),
                      reads=[ci, sc_re], writes=[t2])
                ph.op("dve", lambda e, cr=cr, pi=pi, ni_=ni_, s=s: e.scalar_tensor_tensor(out=ni_.ap[:, s:], in0=cr.ap[:, :NCH - s], scalar=pi, in1=t2.ap[:, s:], op0=ALU.mult, op1=ALU.add),
                      reads=[cr, sc_im, t2], writes=[ni_])
                ph.op("pool", lambda e, cr=cr, nr_=nr_, s=s: e.tensor_copy(out=nr_.ap[:, 0:s], in_=cr.ap[:, 0:s]), reads=[cr], writes=[nr_])
                ph.op("pool", lambda e, ci=ci, ni_=ni_, s=s: e.tensor_copy(out=ni_.ap[:, 0:s], in_=ci.ap[:, 0:s]), reads=[ci], writes=[ni_])
                cur, nxt = nxt, cur
            fr, fi = cur
            ph.op("pool", lambda e, i=i: e.memset(Xb_re[i].ap[:, 0:1], 0.0), writes=[Xb_re[i]])
            ph.op("pool", lambda e, i=i: e.memset(Xb_im[i].ap[:, 0:1], 0.0), writes=[Xb_im[i]])
            ph.op("act", lambda e, i=i, fr=fr: e.activation(out=Xb_re[i].ap[:, 1:], in_=fr.ap[:, :NCH - 1], func=AF.Copy), reads=[fr], writes=[Xb_re[i]])
            ph.op("act", lambda e, i=i, fi=fi: e.activation(out=Xb_im[i].ap[:, 1:], in_=fi.ap[:, :NCH - 1], func=AF.Copy), reads=[fi], writes=[Xb_im[i]])
        yv = y_sb.ap[:].rearrange("p (c l) -> p c l", l=LCH)
        for tl in range(LCH):
            for part in range(NPART):
                cs_ = slice(part * PW, (part + 1) * PW)
                py = ps_y[npy % 2]
                npy += 1
                nmm = (tl + 1) + 8
                n = 0
                for j in range(tl + 1):
                    ph.op("pe", lambda e, j=j, tl=tl, cs_=cs_, py=py, n=n, nmm=nmm: e.matmul(py.ap[:, 0:PW], Kb.ap[:, j, :], uv[:, cs_, tl - j], start=(n == 0), stop=(n == nmm - 1)),
                          reads=[Kb, u_sb], writes=[py])
                    n += 1
                for i in range(4):
                    for (zg, xb) in ((ZG_re, Xb_re[i]), (ZG_nim, Xb_im[i])):
                        ph.op("pe", lambda e, zg=zg, xb=xb, i=i, tl=tl, cs_=cs_, py=py, n=n, nmm=nmm: e.matmul(py.ap[:, 0:PW], zg.ap[:, tl + 1, i, :], xb.ap[:, cs_],
                                                                                                          start=(n == 0), stop=(n == nmm - 1)),
                              reads=[zg, xb], writes=[py])
                        n += 1
                ph.op("act", lambda e, tl=tl, cs_=cs_, py=py: e.activation(out=yv[:, cs_, tl], in_=py.ap[:, 0:PW], func=AF.Copy), reads=[py], writes=[y_sb])
        for c0 in range(0, S, GW):
            ysl = y_sb.ap[:, c0:c0 + GW]
            ph.op("act", lambda e, ysl=ysl: e.activation(out=g1.ap[:], in_=ysl, func=AF.Square), reads=[y_sb], writes=[g1])
            ph.op("dve", lambda e: e.tensor_scalar(out=g1.ap[:], in0=g1.ap[:], scalar1=0.044715, scalar2=1.0, op0=ALU.mult, op1=ALU.add), reads=[g1], writes=[g1])
            ph.op("dve", lambda e, ysl=ysl: e.tensor_tensor(out=g2b.ap[:], in0=g1.ap[:], in1=ysl, op=ALU.mult), reads=[g1, y_sb], writes=[g2b])
            ph.op("act", lambda e: e.activation(out=g1.ap[:], in_=g2b.ap[:], func=AF.Sigmoid, scale=1.5957691216057308), reads=[g2b], writes=[g1])
            ph.op("dve", lambda e, ysl=ysl: e.tensor_tensor(out=z_bf.ap[:], in0=g1.ap[:], in1=ysl, op=ALU.mult), reads=[g1, y_sb], writes=[z_bf])
            ph.dma("sp", lambda e, b=b, c0=c0: e.dma_start(out=t["zT"][b * 128:(b + 1) * 128, c0:c0 + GW], in_=z_bf.ap[:]), z_bf, reads=[z_bf])
    return ph.finish()


def phaseB3(P, l):
    nc, t, S = P.nc, P.t, P.S
    ph = Phase(nc, f"G{l}", P.pool)
    T = 512
    wg = ph.sbuf("wg", [128, 4, 512], BF16)
    ph.dma("sp", lambda e: e.dma_start(out=wg.ap[:], in_=t["b_ssm_glu"][l].rearrange("(kb p) n -> p kb n", p=128)), wg, writes=[wg])
    zb = [ph.sbuf(f"z{i}", [128, 4, T], BF16) for i in range(2)]
    sg = [ph.sbuf(f"sg{i}", [128, T], F32) for i in range(2)]
    yo = [ph.sbuf(f"yo{i}", [128, 4, T], BF16) for i in range(2)]
    pg = [ph.psum(f"pg{i}", [128, T], F32) for i in range(2)]
    n = 0
    for tt_ in range(S // T):
        z = zb[tt_ % 2]
        y = yo[tt_ % 2]
        ts = slice(tt_ * T, (tt_ + 1) * T)
        ph.dma("sp", lambda e, z=z, ts=ts: e.dma_start(out=z.ap[:], in_=t["zT"].rearrange("(kb p) s -> p kb s", p=128)[:, :, ts]), z, writes=[z])
        for ob in range(4):
            p_ = pg[n % 2]
            s_ = sg[n % 2]
            n += 1
            for kb in range(4):
                ph.op("pe", lambda e, kb=kb, ob=ob, p_=p_, z=z: e.matmul(p_.ap[:], wg.ap[:, kb, ob * 128:(ob + 1) * 128], z.ap[:, kb, :], start=(kb == 0), stop=(kb == 3)),
                      reads=[wg, z], writes=[p_])
            ph.op("act", lambda e, p_=p_, s_=s_: e.activation(out=s_.ap[:], in_=p_.ap[:], func=AF.Sigmoid), reads=[p_], writes=[s_])
            ph.op("dve", lambda e, ob=ob, z=z, y=y, s_=s_: e.tensor_tensor(out=y.ap[:, ob, :], in0=z.ap[:, ob, :], in1=s_.ap[:], op=ALU.mult), reads=[z, s_], writes=[y])
        ph.dma("sp", lambda e, y=y, ts=ts: e.dma_start(out=t["ybT"].rearrange("(kb p) s -> p kb s", p=128)[:, :, ts], in_=y.ap[:]), y, reads=[y])
    return ph.finish()


NBIS = 20


def phaseC(P, l, qb0, qb1):
    nc, t, S = P.nc, P.t, P.S
    ph = Phase(nc, f"C{l}_{qb0}", P.pool)
    NKMAX = qb1 * 128
    NKB = qb1
    kT_sb = ph.sbuf("kT", [128, 2, NKMAX], BF16)
    kiT_sb = ph.sbuf("kiT", [64, NKMAX], BF16)
    v_sb = ph.sbuf("v", [128, NKB, 4, 128], BF16)
    causal = ph.sbuf("causal", [128, 128], F32)
    ident = ph.sbuf("ident", [128, 128], BF16)
    sel = ph.sbuf("sel", [128, 64], F32)
    half = ph.sbuf("half", [128, 1], F32)
    for hp in range(2):
        ph.dma("sp", lambda e, hp=hp: e.dma_start(out=kT_sb.ap[hp * 64:(hp + 1) * 64, :, :], in_=t["kT"][:, 2 * hp:2 * hp + 2, 0:NKMAX]), kT_sb, writes=[kT_sb])
    ph.dma("sp", lambda e: e.dma_start(out=kiT_sb.ap[:], in_=t["kiT"][:, 0:NKMAX]), kiT_sb, writes=[kiT_sb])
    for k0 in range(0, NKB, 16):
        k1 = min(NKB, k0 + 16)
        ph.dma("sp", lambda e, k0=k0, k1=k1: e.dma_start(out=v_sb.ap[:, k0:k1], in_=t["vtm"][k0 * 128:k1 * 128].rearrange("(kb p) g d -> p kb g d", p=128)),
               v_sb, writes=[v_sb])
    ph.dma("sp", lambda e: e.dma_start(out=causal.ap[:], in_=t["c_causal"]), causal, writes=[causal])
    ph.dma("sp", lambda e: e.dma_start(out=ident.ap[:], in_=t["c_ident_bf"]), ident, writes=[ident])
    ph.dma("sp", lambda e: e.dma_start(out=sel.ap[:], in_=t["c_sel"]), sel, writes=[sel])
    ph.op("dve", lambda e: e.memset(half.ap[:], 0.5), writes=[half])

    qblk = [ph.sbuf(f"q{i}", [128, 8, 128], BF16) for i in range(2)]
    qiblk = [ph.sbuf(f"qi{i}", [64, 8, 128], BF16) for i in range(2)]
    wi_sb = [ph.sbuf(f"wi{i}", [128, 8], F32) for i in range(2)]
    acc = ph.sbuf("acc", [128, NKMAX], F32)
    mask = ph.sbuf("mask", [128, NKMAX], BF16)
    maskT = ph.sbuf("maskT", [128, NKB, 128], BF16)
    rl = [ph.sbuf(f"rl{i}", [128, 512], F32) for i in range(2)]
    pT = [ph.sbuf(f"pT{i}", [128, 4, 128], BF16) for i in range(3)]
    pm = [ph.sbuf(f"pm{i}", [128, 4, 128], BF16) for i in range(3)]
    osb = [ph.sbuf(f"osb{i}", [128, 512], F32) for i in range(2)]
    rec = [ph.sbuf(f"rec{i}", [64, 512], F32) for i in range(2)]
    yst = [ph.sbuf(f"yst{i}", [64, 4, 128], BF16) for i in range(2)]
    lo, hi, mid, ge, dd, ee = [ph.sbuf(nm, [128, 1], F32) for nm in ("lo", "hi", "mid", "ge", "dd", "ee")]
    cnt = ph.sbuf("cnt", [128, NBIS], F32)
    ps_o = [ph.psum(f"po{g}", [128, 512], F32) for g in range(4)]
    ps_s = [ph.psum(f"ps{i}", [128, 512], F32) for i in range(2)]
    ps_m = [ph.psum(f"psm{i}", [128, 512], F32) for i in range(2)]
    ps_mt = [ps_m[i].ap[:].bitcast(BF16) for i in range(2)]
    nm_ = 0
    nsc = 0
    npt = 0

    for qb in range(qb0, qb1):
        nk = (qb + 1) * 128
        qs = slice(qb * 128, (qb + 1) * 128)
        qbk, qik, wik = qblk[qb % 2], qiblk[qb % 2], wi_sb[qb % 2]
        for hp in range(2):
            ph.dma("sp", lambda e, hp=hp, qbk=qbk, qs=qs: e.dma_start(out=qbk.ap[hp * 64:(hp + 1) * 64, :, :], in_=t["qT"][:, 8 * hp:8 * hp + 8, qs]), qbk, writes=[qbk])
        ph.dma("sp", lambda e, qik=qik, qs=qs: e.dma_start(out=qik.ap[:], in_=t["qiT"][:, :, qs]), qik, writes=[qik])
        ph.dma("sp", lambda e, wik=wik, qs=qs: e.dma_start(out=wik.ap[:], in_=t["witm"][qs, :]), wik, writes=[wik])
        for k0 in range(0, nk, 512):
            w = min(512, nk - k0)
            for h in range(8):
                pi_ = ps_m[nm_ % 2]
                r_ = rl[nm_ % 2]
                nm_ += 1
                ph.op("pe", lambda e, pi_=pi_, h=h, k0=k0, w=w, qik=qik: e.matmul(pi_.ap[:, 0:w], qik.ap[:, h, :], kiT_sb.ap[:, k0:k0 + w], start=True, stop=True),
                      reads=[qik, kiT_sb], writes=[pi_])
                ph.op("act", lambda e, pi_=pi_, r_=r_, w=w: e.activation(out=r_.ap[:, 0:w], in_=pi_.ap[:, 0:w], func=AF.Relu), reads=[pi_], writes=[r_])
                if h == 0:
                    ph.op("dve", lambda e, r_=r_, k0=k0, w=w, wik=wik: e.tensor_scalar(out=acc.ap[:, k0:k0 + w], in0=r_.ap[:, 0:w], scalar1=wik.ap[:, 0:1], scalar2=None, op0=ALU.mult),
                          reads=[r_, wik], writes=[acc])
                else:
                    ph.op("dve", lambda e, r_=r_, k0=k0, w=w, h=h, wik=wik: e.scalar_tensor_tensor(out=acc.ap[:, k0:k0 + w], in0=r_.ap[:, 0:w], scalar=wik.ap[:, h:h + 1],
                                                                                                 in1=acc.ap[:, k0:k0 + w], op0=ALU.mult, op1=ALU.add),
                          reads=[r_, wik, acc], writes=[acc])
        ph.op("dve", lambda e, nk=nk: e.tensor_reduce(out=lo.ap[:], in_=acc.ap[:, 0:nk], axis=AX.X, op=ALU.min), reads=[acc], writes=[lo])
        ph.op("dve", lambda e, qs=qs: e.tensor_tensor(out=acc.ap[:, qs], in0=acc.ap[:, qs], in1=causal.ap[:], op=ALU.add), reads=[acc, causal], writes=[acc])
        ph.op("dve", lambda e, nk=nk: e.tensor_reduce(out=hi.ap[:], in_=acc.ap[:, 0:nk], axis=AX.X, op=ALU.max), reads=[acc], writes=[hi])
        ph.op("dve", lambda e: e.tensor_scalar(out=hi.ap[:], in0=hi.ap[:], scalar1=1.0, scalar2=None, op0=ALU.add), reads=[hi], writes=[hi])
        ph.op("dve", lambda e: e.memset(cnt.ap[:], 0.0), writes=[cnt])
        for it in range(NBIS):
            ph.op("dve", lambda e: e.scalar_tensor_tensor(out=mid.ap[:], in0=lo.ap[:], scalar=hi.ap[:, 0:1], in1=half.ap[:], op0=ALU.add, op1=ALU.mult),
                  reads=[lo, hi, half], writes=[mid])
            ph.op("dve", lambda e, nk=nk, it=it: e.tensor_scalar(out=mask.ap[:, 0:nk], in0=acc.ap[:, 0:nk], scalar1=mid.ap[:, 0:1], scalar2=0.0, op0=ALU.is_ge, op1=ALU.add,
                                                                accum_out=cnt.ap[:, it:it + 1]), reads=[acc, mid], writes=[mask, cnt])
            ph.op("dve", lambda e, it=it: e.tensor_scalar(out=ge.ap[:], in0=cnt.ap[:, it:it + 1], scalar1=TOPK - 0.5, scalar2=None, op0=ALU.is_ge), reads=[cnt], writes=[ge])
            ph.op("dve", lambda e: e.tensor_tensor(out=dd.ap[:], in0=mid.ap[:], in1=lo.ap[:], op=ALU.subtract), reads=[mid, lo], writes=[dd])
            ph.op("dve", lambda e: e.tensor_tensor(out=ee.ap[:], in0=hi.ap[:], in1=mid.ap[:], op=ALU.subtract), reads=[mid, hi], writes=[ee])
            ph.op("dve", lambda e: e.scalar_tensor_tensor(out=lo.ap[:], in0=dd.ap[:], scalar=ge.ap[:, 0:1], in1=lo.ap[:], op0=ALU.mult, op1=ALU.add),
                  reads=[dd, ge, lo], writes=[lo])
            ph.op("dve", lambda e: e.scalar_tensor_tensor(out=hi.ap[:], in0=ee.ap[:], scalar=ge.ap[:, 0:1], in1=mid.ap[:], op0=ALU.mult, op1=ALU.add),
                  reads=[ee, ge, mid], writes=[hi])
        ph.op("dve", lambda e, nk=nk: e.tensor_scalar(out=mask.ap[:, 0:nk], in0=acc.ap[:, 0:nk], scalar1=lo.ap[:, 0:1], scalar2=None, op0=ALU.is_ge),
              reads=[acc, lo], writes=[mask])
        for kb0 in range(0, qb + 1, 4):
            kb1 = min(qb + 1, kb0 + 4)
            pmt_buf = ps_m[nm_ % 2]
            pmt = ps_mt[nm_ % 2]
            nm_ += 1
            for kb in range(kb0, kb1):
                ph.op("pe", lambda e, kb=kb, kb0=kb0, pmt=pmt: e.transpose(pmt[:, (kb - kb0) * 128:(kb - kb0 + 1) * 128], mask.ap[:, kb * 128:(kb + 1) * 128], ident.ap[:]),
                      reads=[mask, ident], writes=[pmt_buf])
            ph.op("act", lambda e, kb0=kb0, kb1=kb1, pmt=pmt: e.activation(out=maskT.ap[:, kb0:kb1, :], in_=pmt[:, 0:(kb1 - kb0) * 128].rearrange("p (a b) -> p a b", b=128), func=AF.Copy),
                  reads=[pmt_buf], writes=[maskT])
        for kb in range(qb + 1):
            for g in range(4):
                pb = (g // 2) * 64
                gi = g % 2
                ps_ = ps_s[nsc % 2]
                nsc += 1
                pT_ = pT[npt % 3]
                pm_ = pm[npt % 3]
                npt += 1
                ph.op("pe", lambda e, ps_=ps_, pb=pb, gi=gi, kb=kb, qbk=qbk: e.matmul(ps_.ap[:], kT_sb.ap[pb:pb + 64, gi, kb * 128:(kb + 1) * 128],
                                                                                    qbk.ap[pb:pb + 64, gi * 4:gi * 4 + 4, :], start=True, stop=True),
                      reads=[kT_sb, qbk], writes=[ps_])
                ph.op("act", lambda e, ps_=ps_, pT_=pT_: e.activation(out=pT_.ap[:].rearrange("p a b -> p (a b)"), in_=ps_.ap[:], func=AF.Exp), reads=[ps_], writes=[pT_])
                meng = "dve" if (g % 4) != 3 else "pool"
                ph.op(meng, lambda e, pT_=pT_, pm_=pm_, kb=kb: e.tensor_tensor(out=pm_.ap[:], in0=pT_.ap[:], in1=maskT.ap[:, kb, :].unsqueeze(1).to_broadcast([128, 4, 128]), op=ALU.mult),
                      reads=[pT_, maskT], writes=[pm_])
                ph.op("pe", lambda e, g=g, kb=kb, pm_=pm_, qb=qb: e.matmul(ps_o[g].ap[:], v_sb.ap[:, kb, g, :], pm_.ap[:].rearrange("p a b -> p (a b)"), start=(kb == 0), stop=(kb == qb)),
                      reads=[v_sb, pm_], writes=[ps_o[g]])
        for g in range(4):
            ob, rc, ys = osb[g % 2], rec[g % 2], yst[g % 2]
            ph.op("act", lambda e, g=g, ob=ob: e.activation(out=ob.ap[:], in_=ps_o[g].ap[:], func=AF.Copy), reads=[ps_o[g]], writes=[ob])
            pr_ = ps_m[nm_ % 2]
            nm_ += 1
            ph.op("pe", lambda e, pr_=pr_, ob=ob: e.matmul(pr_.ap[0:64, :], sel.ap[:], ob.ap[:], start=True, stop=True), reads=[sel, ob], writes=[pr_])
            ph.op("dve", lambda e, pr_=pr_, rc=rc: e.reciprocal(out=rc.ap[:], in_=pr_.ap[0:64, :]), reads=[pr_], writes=[rc])
            ph.op("dve", lambda e, ob=ob, rc=rc, ys=ys: e.tensor_tensor(out=ys.ap[:].rearrange("p a b -> p (a b)"), in0=ob.ap[0:64, :], in1=rc.ap[:], op=ALU.mult),
                  reads=[ob, rc], writes=[ys])
            ph.dma("sp", lambda e, g=g, ys=ys, qs=qs: e.dma_start(out=t["ycT"].rearrange("(h d) s -> d h s", d=64)[:, 4 * g:4 * g + 4, qs], in_=ys.ap[:]), ys, reads=[ys])
    return ph.finish()


G0 = 4168
NWB = 5


def phaseD(P, l, xsrc, last):
    nc, t, S = P.nc, P.t, P.S
    T = 512
    NT = S // T
    ph = Phase(nc, f"D{l}", P.pool)
    ones_bf = ph.sbuf("ones", [128, 128], BF16)
    eps_sb = ph.sbuf("eps", [128, 1], F32)
    g1_sb = ph.sbuf("g1", [128, KC], F32)
    g2_sb = ph.sbuf("g2", [128, KC], F32)
    g3_sb = ph.sbuf("g3", [128, KC], F32)
    scw = ph.sbuf("scw", [128, 4, 3], F32)
    fw = ph.sbuf("fw", [128, 88, 3], F32)
    ph.op("dve", lambda e: e.memset(ones_bf.ap[:], 1.0), writes=[ones_bf])
    ph.op("dve", lambda e: e.memset(eps_sb.ap[:], NORM_EPS), writes=[eps_sb])
    ph.dma("sp", lambda e: e.dma_start(out=g1_sb.ap[:], in_=t["norm_mix"][l].rearrange("(kc p) -> p kc", p=128), allow_slow_non_contiguous=True), g1_sb, writes=[g1_sb])
    ph.dma("sp", lambda e: e.dma_start(out=g2_sb.ap[:], in_=t["norm_ffn"][l].rearrange("(kc p) -> p kc", p=128), allow_slow_non_contiguous=True), g2_sb, writes=[g2_sb])
    ph.dma("sp", lambda e: e.dma_start(out=g3_sb.ap[:], in_=t["norm_final"].rearrange("(kc p) -> p kc", p=128), allow_slow_non_contiguous=True), g3_sb, writes=[g3_sb])
    for j in range(3):
        ph.dma("sp", lambda e, j=j: e.dma_start(out=scw.ap[:, :, j], in_=t["sc_conv"][l, j].rearrange("(c p) -> p c", p=128), allow_slow_non_contiguous=True), scw, writes=[scw])
        for r0 in range(0, 88, 22):
            ph.dma("sp", lambda e, r0=r0, j=j: e.dma_start(out=fw.ap[:, r0:r0 + 22, j], in_=t["ffn_conv"][l, j].rearrange("(r p) -> p r", p=128)[:, r0:r0 + 22],
                                                           allow_slow_non_contiguous=True), fw, writes=[fw])

    x_sb = ph.sbuf("x", [128, KC, T], F32)
    h_sb = ph.sbuf("h", [128, KC, T], BF16)
    big = ph.sbuf("big", [128, 22, T], BF16)
    ya = ph.sbuf("ya", [128, 4, T], BF16)
    cx = ph.sbuf("cx", [128, 4, T + 2], F32)
    yb = ph.sbuf("yb", [128, 4, T], BF16)
    yc = ph.sbuf("yc", [128, 8, T], BF16)
    sqb = [ph.sbuf(f"sq{i}", [128, T], BF16) for i in range(2)]
    rstd = ph.sbuf("rstd", [128, T], F32)
    cacc = [ph.sbuf(f"cacc{i}", [128, T], F32) for i in range(2)]
    sg = [ph.sbuf(f"sg{i}", [128, T], F32) for i in range(3)]
    mt = [ph.sbuf(f"mt{i}", [128, T], F32) for i in range(2)]
    tmpx = sg[2]
    sl = sg[0:2]
    ost = mt
    ug = [ph.sbuf(f"ug{i}", [128, T + 2], F32) for i in range(2)]
    uu = [ph.sbuf(f"uu{i}", [128, T + 2], F32) for i in range(2)]
    carry = ph.sbuf("carry", [128, 88, 2], F32)
    wb = [ph.sbuf(f"wb{i}", [128, KC * 512], BF16) for i in range(NWB)]
    ps_ss = ph.psum("ss", [128, T], F32)
    psg = [ph.psum(f"pg{i}", [128, T], F32) for i in range(7)]
    ph.op("pool", lambda e: e.memset(cx.ap[:], 0.0), writes=[cx])
    ph.op("pool", lambda e: e.memset(carry.ap[:], 0.0), writes=[carry])
    st = {"nw": 0, "np": 0}

    def wbuf():
        b = wb[st["nw"] % NWB]
        st["nw"] += 1
        return b

    def pbank():
        b = psg[st["np"] % 7]
        st["np"] += 1
        return b

    def load_cols(src2d, c0, ncols, krows=KC):
        b = wbuf()
        v = b.ap[:, 0:krows * ncols].rearrange("p (k n) -> p k n", n=ncols)
        ph.dma("sp", lambda e, v=v: e.dma_start(out=v, in_=src2d.rearrange("(k p) n -> p k n", p=128)[:, :, c0:c0 + ncols]), b, writes=[b])
        return b, v

    xsrc_r = xsrc.rearrange("(kc p) s -> p kc s", p=128)
    xdst_r = t["xres"].rearrange("(kc p) s -> p kc s", p=128)
    out_r = t["outT"].rearrange("(kc p) s -> p kc s", p=128)
    w_in = t["b_w_in"][l]
    for tt_ in range(NT):
        ts = slice(tt_ * T, (tt_ + 1) * T)
        for q4 in range(4):
            ph.dma("sp", lambda e, q4=q4, ts=ts: e.dma_start(out=x_sb.ap[:, q4 * 4:(q4 + 1) * 4, :], in_=xsrc_r[:, q4 * 4:(q4 + 1) * 4, ts]), x_sb, writes=[x_sb])
        ph.dma("sp", lambda e, ts=ts: e.dma_start(out=yb.ap[:], in_=t["ybT"].rearrange("(k p) s -> p k s", p=128)[:, :, ts]), yb, writes=[yb])
        ph.dma("sp", lambda e, ts=ts: e.dma_start(out=yc.ap[:], in_=t["ycT"].rearrange("(k p) s -> p k s", p=128)[:, :, ts]), yc, writes=[yc])
        emit_norm(ph, x_sb, h_sb, g1_sb, ones_bf, eps_sb, ps_ss, sqb, rstd, T)
        pcs = [load_cols(w_in, j * 512, 512) for j in range(3)]
        for c in range(4):
            pacc = []
            for j in range(3):
                pb_ = pbank()
                wbuf_, wv = pcs[j]
                for kc in range(KC):
                    ph.op("pe", lambda e, pb_=pb_, wv=wv, kc=kc, c=c: e.matmul(pb_.ap[:], wv[:, kc, c * 128:(c + 1) * 128], h_sb.ap[:, kc, :], start=(kc == 0), stop=(kc == KC - 1)),
                          reads=[wbuf_, h_sb], writes=[pb_])
                pacc.append(pb_)
            px, pbb, pcc = pacc
            ca = cacc[c % 2]
            ph.op("act", lambda e, px=px: e.activation(out=tmpx.ap[:], in_=px.ap[:], func=AF.Copy), reads=[px], writes=[tmpx])
            ph.op("dve", lambda e, c=c, pcc=pcc: e.tensor_tensor(out=cx.ap[:, c, 2:T + 2], in0=tmpx.ap[:], in1=pcc.ap[:], op=ALU.mult), reads=[tmpx, pcc], writes=[cx])
            ph.op("dve", lambda e, c=c, ca=ca: e.tensor_scalar(out=ca.ap[:], in0=cx.ap[:, c, 2:T + 2], scalar1=scw.ap[:, c, 2:3], scalar2=None, op0=ALU.mult), reads=[cx, scw], writes=[ca])
            ph.op("dve", lambda e, c=c, ca=ca: e.scalar_tensor_tensor(out=ca.ap[:], in0=cx.ap[:, c, 1:T + 1], scalar=scw.ap[:, c, 1:2], in1=ca.ap[:], op0=ALU.mult, op1=ALU.add),
                  reads=[cx, scw, ca], writes=[ca])
            ph.op("dve", lambda e, c=c, ca=ca: e.scalar_tensor_tensor(out=ca.ap[:], in0=cx.ap[:, c, 0:T], scalar=scw.ap[:, c, 0:1], in1=ca.ap[:], op0=ALU.mult, op1=ALU.add),
                  reads=[cx, scw, ca], writes=[ca])
            ph.op("dve", lambda e, c=c, ca=ca, pbb=pbb: e.tensor_tensor(out=ya.ap[:, c, :], in0=ca.ap[:], in1=pbb.ap[:], op=ALU.mult), reads=[ca, pbb], writes=[ya])
            ph.op("pool", lambda e, c=c: e.tensor_copy(out=cx.ap[:, c, 0:2], in_=cx.ap[:, c, T:T + 2]), reads=[cx], writes=[cx])
        for fg in range(4):
            gp = [load_cols(w_in, G0 + j * 2048 + fg * 512, 512) for j in range(3)]
            bb = wbuf()
            bv = bb.ap[:].rearrange("p (k n) -> p k n", n=512)
            ph.dma("sp", lambda e, bv=bv, fg=fg: e.dma_start(out=bv[:, 0:4, :], in_=t["b_w_branch_a"][l].rearrange("(k p) n -> p k n", p=128)[:, :, fg * 512:(fg + 1) * 512]), bb, writes=[bb])
            ph.dma("sp", lambda e, bv=bv, fg=fg: e.dma_start(out=bv[:, 4:8, :], in_=t["b_w_branch_b"][l].rearrange("(k p) n -> p k n", p=128)[:, :, fg * 512:(fg + 1) * 512]), bb, writes=[bb])
            ph.dma("sp", lambda e, bv=bv, fg=fg: e.dma_start(out=bv[:, 8:16, :], in_=t["b_w_branch_c"][l].rearrange("(k p) n -> p k n", p=128)[:, :, fg * 512:(fg + 1) * 512]), bb, writes=[bb])
            for cc in range(4):
                fc = fg * 4 + cc
                cs_ = slice(cc * 128, (cc + 1) * 128)
                sgs = []
                for j in range(3):
                    pb_ = pbank()
                    wbuf_, wv = gp[j]
                    for kc in range(KC):
                        ph.op("pe", lambda e, pb_=pb_, wv=wv, kc=kc, cs_=cs_: e.matmul(pb_.ap[:], wv[:, kc, cs_], h_sb.ap[:, kc, :], start=(kc == 0), stop=(kc == KC - 1)),
                              reads=[wbuf_, h_sb], writes=[pb_])
                    ph.op("act", lambda e, pb_=pb_, j=j: e.activation(out=sg[j].ap[:], in_=pb_.ap[:], func=AF.Sigmoid), reads=[pb_], writes=[sg[j]])
                prods = []
                for j, (src_, k0, nk_) in enumerate(((ya, 0, 4), (yb, 4, 4), (yc, 8, 8))):
                    pb_ = pbank()
                    for kk in range(nk_):
                        ph.op("pe", lambda e, pb_=pb_, kk=kk, k0=k0, nk_=nk_, src_=src_, cs_=cs_, bv=bv: e.matmul(pb_.ap[:], bv[:, k0 + kk, cs_], src_.ap[:, kk, :],
                                                                                                             start=(kk == 0), stop=(kk == nk_ - 1)),
                              reads=[bb, src_], writes=[pb_])
                    prods.append(pb_)
                ph.op("dve", lambda e, p0=prods[0]: e.tensor_tensor(out=mt[0].ap[:], in0=p0.ap[:], in1=sg[0].ap[:], op=ALU.mult), reads=[prods[0], sg[0]], writes=[mt[0]])
                ph.op("dve", lambda e, p1=prods[1]: e.tensor_tensor(out=mt[1].ap[:], in0=p1.ap[:], in1=sg[1].ap[:], op=ALU.mult), reads=[prods[1], sg[1]], writes=[mt[1]])
                ph.op("dve", lambda e: e.tensor_tensor(out=mt[0].ap[:], in0=mt[0].ap[:], in1=mt[1].ap[:], op=ALU.add), reads=[mt[0], mt[1]], writes=[mt[0]])
                ph.op("dve", lambda e, p2=prods[2]: e.tensor_tensor(out=mt[1].ap[:], in0=p2.ap[:], in1=sg[2].ap[:], op=ALU.mult), reads=[prods[2], sg[2]], writes=[mt[1]])
                ph.op("dve", lambda e, fc=fc: e.tensor_tensor(out=big.ap[:, fc, :], in0=mt[0].ap[:], in1=mt[1].ap[:], op=ALU.add), reads=[mt[0], mt[1]], writes=[big])
        for fg in range(4):
            wbuf_, wv = load_cols(t["b_w_out"][l], fg * 512, 512)
            for cc in range(4):
                fc = fg * 4 + cc
                pb_ = pbank()
                for kc in range(KC):
                    ph.op("pe", lambda e, pb_=pb_, wv=wv, kc=kc, cc=cc: e.matmul(pb_.ap[:], wv[:, kc, cc * 128:(cc + 1) * 128], big.ap[:, kc, :], start=(kc == 0), stop=(kc == KC - 1)),
                          reads=[wbuf_, big], writes=[pb_])
                ph.op("dve", lambda e, pb_=pb_, fc=fc: e.tensor_tensor(out=x_sb.ap[:, fc, :], in0=x_sb.ap[:, fc, :], in1=pb_.ap[:], op=ALU.add), reads=[x_sb, pb_], writes=[x_sb])
        emit_norm(ph, x_sb, h_sb, g2_sb, ones_bf, eps_sb, ps_ss, sqb, rstd, T)
        for hf in range(2):
            for i2 in range(11):
                ci0 = hf * 22 + i2 * 2
                b_ = wbuf()
                v_ = b_.ap[:].rearrange("p (k n) -> p k n", n=512)
                ph.dma("sp", lambda e, v_=v_, ci0=ci0: e.dma_start(out=v_[:, :, 0:256], in_=t["b_w_up"][l].rearrange("(k p) n -> p k n", p=128)[:, :, ci0 * 128:ci0 * 128 + 256]), b_, writes=[b_])
                ph.dma("sp", lambda e, v_=v_, ci0=ci0: e.dma_start(out=v_[:, :, 256:512], in_=t["b_w_up"][l].rearrange("(k p) n -> p k n", p=128)[:, :, D_FF + ci0 * 128:D_FF + ci0 * 128 + 256]), b_, writes=[b_])
                for s2 in range(2):
                    ci = ci0 + s2
                    ii = i2 * 2 + s2
                    res = []
                    for gu in range(2):
                        pb_ = pbank()
                        co = gu * 256 + s2 * 128
                        for kc in range(KC):
                            ph.op("pe", lambda e, pb_=pb_, v_=v_, kc=kc, co=co: e.matmul(pb_.ap[:], v_[:, kc, co:co + 128], h_sb.ap[:, kc, :], start=(kc == 0), stop=(kc == KC - 1)),
                                  reads=[b_, h_sb], writes=[pb_])
                        r = gu * 44 + ci
                        stg = (ug if gu == 0 else uu)[ii % 2]
                        ca = cacc[gu]
                        ph.op("pool", lambda e, stg=stg, r=r: e.tensor_copy(out=stg.ap[:, 0:2], in_=carry.ap[:, r, :]), reads=[carry], writes=[stg])
                        ph.op("act", lambda e, stg=stg, pb_=pb_: e.activation(out=stg.ap[:, 2:T + 2], in_=pb_.ap[:], func=AF.Copy), reads=[pb_], writes=[stg])
                        ph.op("pool", lambda e, stg=stg, r=r: e.tensor_copy(out=carry.ap[:, r, :], in_=stg.ap[:, T:T + 2]), reads=[stg], writes=[carry])
                        ph.op("dve", lambda e, stg=stg, ca=ca, r=r: e.tensor_scalar(out=ca.ap[:], in0=stg.ap[:, 2:T + 2], scalar1=fw.ap[:, r, 2:3], scalar2=None, op0=ALU.mult), reads=[stg, fw], writes=[ca])
                        ph.op("dve", lambda e, stg=stg, ca=ca, r=r: e.scalar_tensor_tensor(out=ca.ap[:], in0=stg.ap[:, 1:T + 1], scalar=fw.ap[:, r, 1:2], in1=ca.ap[:], op0=ALU.mult, op1=ALU.add),
                              reads=[stg, fw, ca], writes=[ca])
                        ph.op("dve", lambda e, stg=stg, ca=ca, r=r: e.scalar_tensor_tensor(out=ca.ap[:], in0=stg.ap[:, 0:T], scalar=fw.ap[:, r, 0:1], in1=ca.ap[:], op0=ALU.mult, op1=ALU.add),
                              reads=[stg, fw, ca], writes=[ca])
                        res.append(ca)
                    s_ = sl[ii % 2]
                    ph.op("act", lambda e, s_=s_, cg=res[0]: e.activation(out=s_.ap[:], in_=cg.ap[:], func=AF.Silu), reads=[res[0]], writes=[s_])
                    ph.op("dve", lambda e, s_=s_, cu=res[1], ii=ii: e.tensor_tensor(out=big.ap[:, ii, :], in0=s_.ap[:], in1=cu.ap[:], op=ALU.mult), reads=[s_, res[1]], writes=[big])
            for o2 in range(8):
                b_ = wbuf()
                v_ = b_.ap[:, 0:22 * 256].rearrange("p (k n) -> p k n", n=256)
                ph.dma("sp", lambda e, v_=v_, o2=o2, hf=hf: e.dma_start(out=v_, in_=t["b_w_down"][l, hf * 2816:(hf + 1) * 2816, :].rearrange("(k p) n -> p k n", p=128)[:, :, o2 * 256:(o2 + 1) * 256]),
                       b_, writes=[b_])
                for s2 in range(2):
                    oc = o2 * 2 + s2
                    pb_ = pbank()
                    for i in range(22):
                        ph.op("pe", lambda e, pb_=pb_, v_=v_, i=i, s2=s2: e.matmul(pb_.ap[:], v_[:, i, s2 * 128:(s2 + 1) * 128], big.ap[:, i, :], start=(i == 0), stop=(i == 21)),
                              reads=[b_, big], writes=[pb_])
                    ph.op("dve", lambda e, pb_=pb_, oc=oc: e.tensor_tensor(out=x_sb.ap[:, oc, :], in0=x_sb.ap[:, oc, :], in1=pb_.ap[:], op=ALU.add), reads=[x_sb, pb_], writes=[x_sb])
        if not last:
            for q4 in range(4):
                ph.dma("sp", lambda e, q4=q4, ts=ts: e.dma_start(out=xdst_r[:, q4 * 4:(q4 + 1) * 4, ts], in_=x_sb.ap[:, q4 * 4:(q4 + 1) * 4, :]), x_sb, reads=[x_sb])
        else:
            for kc in range(KC):
                sq = sqb[kc % 2]
                ph.op("act", lambda e, kc=kc, sq=sq: e.activation(out=sq.ap[:], in_=x_sb.ap[:, kc, :], func=AF.Square), reads=[x_sb], writes=[sq])
                ph.op("pe", lambda e, kc=kc, sq=sq: e.matmul(ps_ss.ap[:], ones_bf.ap[:], sq.ap[:], start=(kc == 0), stop=(kc == KC - 1)), reads=[sq, ones_bf], writes=[ps_ss])
            ph.op("act", lambda e: e.activation(out=rstd.ap[:], in_=ps_ss.ap[:], func=AF.Sqrt, bias=eps_sb.ap[:, 0:1], scale=1.0 / D), reads=[ps_ss, eps_sb], writes=[rstd])
            ph.op("dve", lambda e: e.reciprocal(out=rstd.ap[:], in_=rstd.ap[:]), reads=[rstd], writes=[rstd])
            for kc in range(KC):
                o_ = ost[kc % 2]
                ph.op("dve", lambda e, kc=kc, o_=o_: e.scalar_tensor_tensor(out=o_.ap[:], in0=x_sb.ap[:, kc, :], scalar=g3_sb.ap[:, kc:kc + 1], in1=rstd.ap[:], op0=ALU.mult, op1=ALU.mult),
                      reads=[x_sb, g3_sb, rstd], writes=[o_])
                ph.dma("sp", lambda e, kc=kc, o_=o_, ts=ts: e.dma_start(out=out_r[:, kc, ts], in_=o_.ap[:]), o_, reads=[o_])
    return ph.finish()


QCH = 16


def build_program(S, debug_outs=(), nl=DEPTH, last=True):
    P = Prog(S, debug_outs=debug_outs, nl=nl)
    counts = [phase0(P)]
    NQB = S // 128
    for l in range(nl):
        xsrc = P.t["xT"] if l == 0 else P.t["xres"]
        counts.append(phaseA(P, l, xsrc))
        counts.append(phaseB(P, l))
        counts.append(phaseB3(P, l))
        for qb0 in range(0, NQB, QCH):
            counts.append(phaseC(P, l, qb0, min(NQB, qb0 + QCH)))
        counts.append(phaseD(P, l, xsrc, last=(last and l == nl - 1)))
    P.counts = counts
    return P


def make_in_map(inputs, b, S, l0=0, nl=DEPTH, xT=None):
    m = {}
    if xT is None:
        m["xT"] = np.ascontiguousarray(np.asarray(inputs["x"])[b, :S].T)
    else:
        m["xT"] = np.ascontiguousarray(xT)
    m["pos"] = np.ascontiguousarray(np.asarray(inputs["positions"])[b, :S]).astype(np.int32)
    for k in W_SPECS:
        a = np.asarray(inputs[k], dtype=np.float32)
        if k != "norm_final":
            a = a[l0:l0 + nl]
        if k in ("ssm_c_re", "ssm_c_im"):
            a = np.transpose(a, (0, 1, 3, 2))
        m[k] = np.ascontiguousarray(a)
    m.update(host_consts())
    return m


FUSED = True


def kernel(**inputs):
    x = np.asarray(inputs["x"])
    B, S, _ = x.shape
    n_cores = 8
    out = np.empty((B, S, D), np.float32)
    if FUSED:
        P = build_program(S)
        in_maps = [make_in_map(inputs, c % B, S) for c in range(n_cores)]
        res = run_bass_kernel_spmd(P.nc, in_maps, core_ids=list(range(n_cores)))
        for b in range(B):
            out[b] = np.asarray(res.results[b]["outT"], dtype=np.float32).T
        return out
    xs = [None] * B
    for l in range(DEPTH):
        lastl = (l == DEPTH - 1)
        P = build_program(S, debug_outs=(() if lastl else ("xres",)), nl=1, last=lastl)
        in_maps = [make_in_map(inputs, c % B, S, l0=l, nl=1, xT=xs[c % B]) for c in range(n_cores)]
        res = run_bass_kernel_spmd(P.nc, in_maps, core_ids=list(range(n_cores)))
        if lastl:
            for b in range(B):
                out[b] = np.asarray(res.results[b]["outT"], dtype=np.float32).T
        else:
            xs = [np.asarray(res.results[b]["xres"], dtype=np.float32) for b in range(B)]
        del res, in_maps
    return out
```

```python
import math
from contextlib import ExitStack

import numpy as np
import ml_dtypes

import concourse.bass as bass
import concourse.mybir as mybir
from concourse.bass_utils import run_bass_kernel_spmd

F32 = mybir.dt.float32
BF16 = mybir.dt.bfloat16
I32 = mybir.dt.int32
AF = mybir.ActivationFunctionType
ALU = mybir.AluOpType
AX = mybir.AxisListType

D = 2048
KC = D // 128
DEPTH = 2
N_IN = 10312
D_FF = 5632
NEG = -1.0e30
TOPK = 256
NORM_EPS = 1e-6

ENGS = ("pe", "act", "dve", "pool", "sp")


class Buf:
    __slots__ = ("name", "w", "r", "dsem", "dcnt", "ap")

    def __init__(self, name, ap=None):
        self.name = name
        self.w = None
        self.r = {}
        self.dsem = None
        self.dcnt = 0
        self.ap = ap


class SemPool:
    def __init__(self, nc, n=96):
        self.stack = ExitStack()
        self.handles = {}
        self.free = []
        for i in range(n):
            key = f"s{i}"
            self.handles[key] = self.stack.enter_context(nc.semaphore(f"gs{i}"))
            self.free.append((0, i, key))

    def get(self):
        self.free.sort()
        cnt, i, key = self.free.pop(0)
        return key, cnt

    def put(self, key, cnt):
        if cnt < Phase.SEM_ROLL:
            self.free.append((cnt, int(key[1:]), key))


class Phase:
    SEM_ROLL = 30000

    def __init__(self, nc, name, pool):
        self.nc = nc
        self.name = name
        self.pool = pool
        self.stack = ExitStack()
        self.lists = {e: [] for e in ENGS}
        self.cnt = {e: 0 for e in ENGS}
        self.waited = {e: {} for e in ENGS}
        self.sems = pool.handles
        self.key_eng = {}
        self.curkey = {}
        self.retired = []
        for e in ENGS:
            self._roll(e)
        self.dma_bufs = []
        self.dma_done = []

    def _roll(self, e):
        key, cnt = self.pool.get()
        self.key_eng[key] = e
        self.curkey[e] = key
        self.cnt[e] = cnt

    def sbuf(self, name, shape, dtype):
        t = self.stack.enter_context(self.nc.sbuf_tensor(f"{self.name}_{name}", list(shape), dtype))
        return Buf(name, t)

    def psum(self, name, shape, dtype):
        n = 2048 // (2 if dtype == BF16 else 4)
        t = self.stack.enter_context(self.nc.psum_tensor(f"{self.name}_{name}", [128, n], dtype))
        return Buf(name, t)

    def token(self, name):
        return Buf(name)

    def _deps(self, eng, reads, writes):
        deps = {}
        ke = self.key_eng

        def add(k, v):
            if deps.get(k, 0) < v:
                deps[k] = v

        for b in reads:
            if b.w is not None:
                add(*b.w)
        for b in writes:
            if b.w is not None and ke.get(b.w[0]) != eng:
                add(*b.w)
            for k, v in b.r.items():
                if ke.get(k) != eng:
                    add(k, v)
        return deps

    def _emit_waits(self, eng, deps):
        wd = self.waited[eng]
        for k, v in deps.items():
            if wd.get(k, 0) < v:
                wd[k] = v
                self.lists[eng].append(("wait", self.sems[k], v))

    def op(self, eng, fn, reads=(), writes=()):
        deps = self._deps(eng, reads, writes)
        self._emit_waits(eng, deps)
        if self.cnt[eng] >= self.SEM_ROLL:
            self._roll(eng)
        self.cnt[eng] += 1
        key = self.curkey[eng]
        ev = (key, self.cnt[eng])
        self.lists[eng].append(("op", fn, self.sems[key], 1))
        for b in reads:
            if b.r.get(key, 0) < ev[1]:
                b.r[key] = ev[1]
        for b in writes:
            b.w = ev
            b.r = {}
        return ev

    def dma(self, q, fn, carrier, reads=(), writes=()):
        sb = carrier
        if sb.dsem is None or sb.dcnt >= self.SEM_ROLL:
            if sb.dsem is None:
                self.dma_bufs.append(sb)
            else:
                self.dma_done.append((sb.dsem, sb.dcnt))
            sb.dsem, sb.dcnt = self.pool.get()
        deps = self._deps(q, reads, writes)
        self._emit_waits(q, deps)
        sb.dcnt += 16
        ev = (sb.dsem, sb.dcnt)
        self.lists[q].append(("op", fn, self.sems[sb.dsem], 16))
        for b in reads:
            if b.r.get(ev[0], 0) < ev[1]:
                b.r[ev[0]] = ev[1]
        for b in writes:
            b.w = ev
            b.r = {}
        return ev

    def finish(self):
        deps = {}
        for b in self.dma_bufs:
            deps[b.dsem] = b.dcnt
        for k, v in self.dma_done:
            deps[k] = v
        self._emit_waits("sp", deps)
        lists = self.lists

        def run(engh, lst):
            for it in lst:
                if it[0] == "wait":
                    engh.wait_ge(it[1], it[2])
                else:
                    it[1](engh).then_inc(it[2], it[3])

        with self.nc.Block() as block:
            @block.tensor
            def _(e):
                run(e, lists["pe"])

            @block.scalar
            def _(e):
                run(e, lists["act"])

            @block.vector
            def _(e):
                run(e, lists["dve"])

            @block.gpsimd
            def _(e):
                run(e, lists["pool"])

            @block.sync
            def _(e):
                run(e, lists["sp"])
        n = sum(len(v) for v in lists.values())
        self.stack.close()
        for e in ENGS:
            self.pool.put(self.curkey[e], self.cnt[e])
        for b in self.dma_bufs:
            self.pool.put(b.dsem, b.dcnt)
        return n


def host_consts():
    c = {}
    c["c_ident_bf"] = np.eye(128, dtype=np.float32).astype(ml_dtypes.bfloat16)
    c["c_ident_f"] = np.eye(128, dtype=np.float32)
    inv_freq = (500000.0 ** (-np.arange(0, 16, 2, dtype=np.float32) / np.float32(16))).astype(np.float32)
    c["c_invf"] = np.tile(inv_freq[None, :], (128, 1)).astype(np.float32)
    tri = np.zeros((128, 128), np.float32)
    tri[np.triu_indices(128, 1)] = NEG
    c["c_causal"] = tri
    zm = np.zeros((128, 4, 4, 2, 16), np.float32)
    for i in range(4):
        for g2 in range(2):
            zm[g2 * 64:(g2 + 1) * 64, i, i, g2, :] = 1.0
    c["c_zmask"] = zm.reshape(128, 4, 128)
    sel = np.zeros((128, 64), np.float32)
    sel[64 + np.arange(64), np.arange(64)] = 1.0
    c["c_sel"] = sel
    return c


CONST_SPECS = {
    "c_ident_bf": ([128, 128], BF16),
    "c_ident_f": ([128, 128], F32),
    "c_invf": ([128, 8], F32),
    "c_causal": ([128, 128], F32),
    "c_zmask": ([128, 4, 128], F32),
    "c_sel": ([128, 64], F32),
}

W_SPECS = {
    "norm_mix": [DEPTH, D], "w_in": [DEPTH, D, N_IN], "sc_conv": [DEPTH, 3, 512],
    "ssm_a_re": [DEPTH, 32, 64], "ssm_a_im": [DEPTH, 32, 64], "ssm_log_dt": [DEPTH, 32],
    "ssm_b_re": [DEPTH, 32, 64, 16], "ssm_b_im": [DEPTH, 32, 64, 16],
    "ssm_c_re": [DEPTH, 32, 64, 16], "ssm_c_im": [DEPTH, 32, 64, 16],
    "ssm_d": [DEPTH, 512], "ssm_glu": [DEPTH, 512, 512],
    "w_branch_a": [DEPTH, 512, D], "w_branch_b": [DEPTH, 512, D], "w_branch_c": [DEPTH, 1024, D],
    "w_out": [DEPTH, D, D], "norm_ffn": [DEPTH, D], "w_up": [DEPTH, D, 2 * D_FF],
    "ffn_conv": [DEPTH, 3, 2 * D_FF], "w_down": [DEPTH, D_FF, D], "norm_final": [D],
}


def w_specs(nl):
    return {k: ([nl] + v[1:] if k != "norm_final" else v) for k, v in W_SPECS.items()}


BIG_W = ["w_in", "ssm_glu", "w_branch_a", "w_branch_b", "w_branch_c", "w_out", "w_up", "w_down"]


class Prog:
    def __init__(self, S, debug_outs=(), skip=(), nl=DEPTH):
        self.S = S
        self.nl = nl
        self.skip = tuple(skip)
        WS = w_specs(nl)
        self.WS = WS
        nc = bass.Bass("TRN2", target_bir_lowering=False)
        self.nc = nc
        self.pool = SemPool(nc)
        t = {}
        t["xT"] = nc.dram_tensor("xT", [D, S], F32, kind="ExternalInput").ap()
        t["pos"] = nc.dram_tensor("pos", [S], I32, kind="ExternalInput").ap()
        for k, shp in WS.items():
            if k in self.skip:
                continue
            t[k] = nc.dram_tensor(k, shp, F32, kind="ExternalInput").ap()
        for k, (shp, dt) in CONST_SPECS.items():
            t[k] = nc.dram_tensor(k, shp, dt, kind="ExternalInput").ap()
        t["outT"] = nc.dram_tensor("outT", [D, S], F32, kind=("Internal" if "xres" in debug_outs else "ExternalOutput")).ap()

        def scratch(name, shape, dt):
            kind = "ExternalOutput" if name in debug_outs else "Internal"
            t[name] = nc.dram_tensor(name, shape, dt, kind=kind).ap()

        for k in BIG_W:
            if k in self.skip:
                continue
            scratch("b_" + k, WS[k], BF16)
        scratch("cs", [S, 16], F32)
        scratch("xres", [D, S], F32)
        scratch("uT", [512, S], BF16)
        scratch("qT", [64, 16, S], BF16)
        scratch("kT", [64, 4, S], BF16)
        scratch("qiT", [64, 8, S], BF16)
        scratch("kiT", [64, S], BF16)
        scratch("vtm", [S, 4, 128], BF16)
        scratch("witm", [S, 8], F32)
        scratch("zT", [512, S], BF16)
        scratch("ybT", [512, S], BF16)
        scratch("ycT", [1024, S], BF16)
        self.t = t


def emit_range_reduce(ph, red, ki, kf, msk):
    two_pi = 2.0 * math.pi
    ph.op("dve", lambda e: e.tensor_scalar(out=kf.ap[:], in0=red.ap[:], scalar1=1.0 / two_pi, scalar2=None, op0=ALU.mult), reads=[red], writes=[kf])
    ph.op("dve", lambda e: e.tensor_copy(out=ki.ap[:], in_=kf.ap[:]), reads=[kf], writes=[ki])
    ph.op("dve", lambda e: e.tensor_copy(out=kf.ap[:], in_=ki.ap[:]), reads=[ki], writes=[kf])
    ph.op("dve", lambda e: e.scalar_tensor_tensor(out=red.ap[:], in0=kf.ap[:], scalar=-two_pi, in1=red.ap[:], op0=ALU.mult, op1=ALU.add), reads=[kf, red], writes=[red])
    ph.op("dve", lambda e: e.tensor_scalar(out=msk.ap[:], in0=red.ap[:], scalar1=math.pi, scalar2=None, op0=ALU.is_gt), reads=[red], writes=[msk])
    ph.op("dve", lambda e: e.scalar_tensor_tensor(out=red.ap[:], in0=msk.ap[:], scalar=-two_pi, in1=red.ap[:], op0=ALU.mult, op1=ALU.add), reads=[msk, red], writes=[red])
    ph.op("dve", lambda e: e.tensor_scalar(out=msk.ap[:], in0=red.ap[:], scalar1=-math.pi, scalar2=None, op0=ALU.is_lt), reads=[red], writes=[msk])
    ph.op("dve", lambda e: e.scalar_tensor_tensor(out=red.ap[:], in0=msk.ap[:], scalar=two_pi, in1=red.ap[:], op0=ALU.mult, op1=ALU.add), reads=[msk, red], writes=[red])


def phase0(P):
    nc, t, S = P.nc, P.t, P.S
    ph = Phase(nc, "p0", P.pool)
    CW = 4096
    NBUF = 3
    cin = [ph.sbuf(f"cin{i}", [128, CW], F32) for i in range(NBUF)]
    cout = [ph.sbuf(f"cout{i}", [128, CW], BF16) for i in range(NBUF)]
    n = 0
    for k in BIG_W:
        if k in P.skip:
            continue
        shp = W_SPECS[k]
        rows, cols = shp[1], shp[2]
        for l in range(P.nl):
            for r0 in range(0, rows, 128):
                for c0 in range(0, cols, CW):
                    w = min(CW, cols - c0)
                    bi, bo = cin[n % NBUF], cout[n % NBUF]
                    src = t[k][l, r0:r0 + 128, c0:c0 + w]
                    dst = t["b_" + k][l, r0:r0 + 128, c0:c0 + w]
                    ph.dma("sp", lambda e, src=src, bi=bi, w=w: e.dma_start(out=bi.ap[:, 0:w], in_=src), bi, writes=[bi])
                    eng = ("dve", "act", "pool")[n % 3]
                    if eng == "act":
                        ph.op("act", lambda e, bi=bi, bo=bo, w=w: e.activation(out=bo.ap[:, 0:w], in_=bi.ap[:, 0:w], func=AF.Copy), reads=[bi], writes=[bo])
                    else:
                        ph.op(eng, lambda e, bi=bi, bo=bo, w=w: e.tensor_copy(out=bo.ap[:, 0:w], in_=bi.ap[:, 0:w]), reads=[bi], writes=[bo])
                    ph.dma("sp", lambda e, dst=dst, bo=bo, w=w: e.dma_start(out=dst, in_=bo.ap[:, 0:w]), bo, reads=[bo])
                    n += 1
    NB = S // 128
    posi = ph.sbuf("posi", [128, NB], I32)
    posf = ph.sbuf("posf", [128, NB], F32)
    invf = ph.sbuf("invf", [128, 8], F32)
    ang = ph.sbuf("ang", [128, NB, 8], F32)
    red = ph.sbuf("red", [128, NB, 16], F32)
    cs = ph.sbuf("cs", [128, NB, 16], F32)
    negpi = ph.sbuf("negpi", [128, 1], F32)
    ph.dma("sp", lambda e: e.dma_start(out=posi.ap[:], in_=t["pos"].rearrange("(b p) -> p b", p=128),
                                       allow_slow_non_contiguous=True), posi, writes=[posi])
    ph.dma("sp", lambda e: e.dma_start(out=invf.ap[:], in_=t["c_invf"]), invf, writes=[invf])
    ph.op("dve", lambda e: e.memset(negpi.ap[:], -math.pi), writes=[negpi])
    ph.op("dve", lambda e: e.tensor_copy(out=posf.ap[:], in_=posi.ap[:]), reads=[posi], writes=[posf])
    ph.op("dve", lambda e: e.tensor_tensor(out=ang.ap[:], in0=posf.ap[:].unsqueeze(2).to_broadcast([128, NB, 8]),
                                           in1=invf.ap[:].unsqueeze(1).to_broadcast([128, NB, 8]), op=ALU.mult),
          reads=[posf, invf], writes=[ang])
    two_pi = 2.0 * math.pi
    ki = ph.sbuf("ki", [128, NB, 16], I32)
    kf = ph.sbuf("kf", [128, NB, 16], F32)
    msk = ph.sbuf("msk", [128, NB, 16], F32)
    ph.op("dve", lambda e: e.tensor_scalar(out=red.ap[:, :, 0:8], in0=ang.ap[:], scalar1=0.5 * math.pi, scalar2=None, op0=ALU.add), reads=[ang], writes=[red])
    ph.op("dve", lambda e: e.tensor_copy(out=red.ap[:, :, 8:16], in_=ang.ap[:]), reads=[ang], writes=[red])
    emit_range_reduce(ph, red, ki, kf, msk)
    ph.op("act", lambda e: e.activation(out=cs.ap[:], in_=red.ap[:], func=AF.Sin),
          reads=[red], writes=[cs])
    ph.dma("sp", lambda e: e.dma_start(out=t["cs"].rearrange("(b p) c -> p b c", p=128), in_=cs.ap[:]), cs, reads=[cs])
    return ph.finish()


def emit_norm(ph, x_sb, h_sb, g_sb, ones_bf, eps_sb, ps_ss, sq_bufs, rstd_sb, T):
    for kc in range(KC):
        sq = sq_bufs[kc % len(sq_bufs)]
        ph.op("act", lambda e, kc=kc, sq=sq: e.activation(out=sq.ap[:], in_=x_sb.ap[:, kc, :], func=AF.Square),
              reads=[x_sb], writes=[sq])
        ph.op("pe", lambda e, kc=kc, sq=sq: e.matmul(ps_ss.ap[:], ones_bf.ap[:], sq.ap[:], start=(kc == 0), stop=(kc == KC - 1)),
              reads=[sq, ones_bf], writes=[ps_ss])
    ph.op("act", lambda e: e.activation(out=rstd_sb.ap[:], in_=ps_ss.ap[:], func=AF.Sqrt, bias=eps_sb.ap[:, 0:1], scale=1.0 / D),
          reads=[ps_ss, eps_sb], writes=[rstd_sb])
    ph.op("dve", lambda e: e.reciprocal(out=rstd_sb.ap[:], in_=rstd_sb.ap[:]), reads=[rstd_sb], writes=[rstd_sb])
    for kc in range(KC):
        ph.op("dve", lambda e, kc=kc: e.scalar_tensor_tensor(out=h_sb.ap[:, kc, :], in0=x_sb.ap[:, kc, :], scalar=g_sb.ap[:, kc:kc + 1],
                                                            in1=rstd_sb.ap[:], op0=ALU.mult, op1=ALU.mult),
              reads=[x_sb, g_sb, rstd_sb], writes=[h_sb])


A_C0 = 1536
A_NC = 2632
A_TM = 2120


def phaseA(P, l, xsrc):
    nc, t, S = P.nc, P.t, P.S
    T = 512
    NT = S // T
    ph = Phase(nc, f"A{l}", P.pool)
    wA = ph.sbuf("wA", [128, KC, A_NC], BF16)
    for kc in range(KC):
        ph.dma("sp", lambda e, kc=kc: e.dma_start(out=wA.ap[:, kc, :], in_=t["b_w_in"][l, kc * 128:(kc + 1) * 128, A_C0:A_C0 + A_NC]),
               wA, writes=[wA])
    ones_bf = ph.sbuf("ones", [128, 128], BF16)
    ident = ph.sbuf("ident", [128, 128], BF16)
    eps_sb = ph.sbuf("eps", [128, 1], F32)
    g_sb = ph.sbuf("g", [128, KC], F32)
    cs_all = ph.sbuf("csall", [128, S // 128, 16], F32)
    ph.op("dve", lambda e: e.memset(ones_bf.ap[:], 1.0), writes=[ones_bf])
    ph.op("dve", lambda e: e.memset(eps_sb.ap[:], NORM_EPS), writes=[eps_sb])
    ph.dma("sp", lambda e: e.dma_start(out=ident.ap[:], in_=t["c_ident_bf"]), ident, writes=[ident])
    ph.dma("sp", lambda e: e.dma_start(out=g_sb.ap[:], in_=t["norm_mix"][l].rearrange("(kc p) -> p kc", p=128),
                                       allow_slow_non_contiguous=True), g_sb, writes=[g_sb])
    ph.dma("sp", lambda e: e.dma_start(out=cs_all.ap[:], in_=t["cs"].rearrange("(b p) c -> p b c", p=128)), cs_all, writes=[cs_all])

    xb = [ph.sbuf(f"x{i}", [128, KC, T], F32) for i in range(1)]
    h_sb = ph.sbuf("h", [128, KC, T], BF16)
    sqb = [ph.sbuf(f"sq{i}", [128, T], BF16) for i in range(2)]
    rstd = ph.sbuf("rstd", [128, T], F32)
    ps_ss = ph.psum("ss", [128, T], F32)
    ps_u = [ph.psum(f"pu{i}", [128, 512], F32) for i in range(2)]
    ps_p = [ph.psum(f"pp{i}", [128, 512], F32) for i in range(2)]
    ps_t = [ph.psum(f"pt{i}", [64, 512], BF16) for i in range(2)]
    u_st = [ph.sbuf(f"ust{i}", [128, 512], BF16) for i in range(2)]
    P_sb = [ph.sbuf(f"P{i}", [128, A_TM], F32) for i in range(2)]
    R_sb = ph.sbuf("R", [128, 4, A_TM - 8], BF16)
    tmp = [ph.sbuf(f"tmp{i}", [128, 20, 8], F32) for i in range(4)]
    v_st = [ph.sbuf(f"vst{i}", [128, 4, 4, 128], BF16) for i in range(2)]
    wi_st = [ph.sbuf(f"wist{i}", [128, 4, 8], F32) for i in range(2)]
    hd_st = [ph.sbuf(f"hdst{i}", [64, 16, 512], BF16) for i in range(1)]
    for i in range(2):
        ph.op("pool", lambda e, i=i: e.memset(v_st[i].ap[:], 1.0), writes=[v_st[i]])

    xsrc_r = xsrc.rearrange("(kc p) t -> p kc t", p=128)
    npu = 0
    npp = 0
    npt = 0
    for tt in range(NT):
        x_sb = xb[0]
        for q4 in range(4):
            ph.dma("sp", lambda e, q4=q4, tt=tt, x_sb=x_sb: e.dma_start(out=x_sb.ap[:, q4 * 4:(q4 + 1) * 4, :],
                                                                        in_=xsrc_r[:, q4 * 4:(q4 + 1) * 4, tt * T:(tt + 1) * T]),
                   x_sb, writes=[x_sb])
        emit_norm(ph, x_sb, h_sb, g_sb, ones_bf, eps_sb, ps_ss, sqb, rstd, T)
        for c in range(4):
            pu = ps_u[npu % 2]
            ust = u_st[npu % 2]
            npu += 1
            for kc in range(KC):
                ph.op("pe", lambda e, kc=kc, c=c, pu=pu: e.matmul(pu.ap[:], wA.ap[:, kc, c * 128:(c + 1) * 128], h_sb.ap[:, kc, :],
                                                                  start=(kc == 0), stop=(kc == KC - 1)),
                      reads=[wA, h_sb], writes=[pu])
            ph.op("act", lambda e, pu=pu, ust=ust: e.activation(out=ust.ap[:], in_=pu.ap[:], func=AF.Copy), reads=[pu], writes=[ust])
            ph.dma("sp", lambda e, c=c, tt=tt, ust=ust: e.dma_start(out=t["uT"][c * 128:(c + 1) * 128, tt * T:(tt + 1) * T], in_=ust.ap[:]),
                   ust, reads=[ust])
        vst = v_st[tt % 2]
        wist = wi_st[tt % 2]
        hdst = hd_st[0]
        for tb in range(4):
            Pb = P_sb[tb % 2]
            for ct in range(5):
                c0 = 512 + ct * 512
                c1 = min(A_NC, c0 + 512)
                w = c1 - c0
                pp = ps_p[npp % 2]
                npp += 1
                for kc in range(KC):
                    ph.op("pe", lambda e, kc=kc, tb=tb, c0=c0, c1=c1, w=w, pp=pp: e.matmul(
                        pp.ap[:, 0:w], h_sb.ap[:, kc, tb * 128:(tb + 1) * 128], wA.ap[:, kc, c0:c1],
                        start=(kc == 0), stop=(kc == KC - 1)), reads=[wA, h_sb], writes=[pp])
                ph.op("act", lambda e, w=w, pp=pp, ct=ct, Pb=Pb: e.activation(out=Pb.ap[:, ct * 512:ct * 512 + w], in_=pp.ap[:, 0:w], func=AF.Copy),
                      reads=[pp], writes=[Pb])
            blk = tt * 4 + tb
            for (c0, nh) in ((0, 20), (1536, 9)):
                hv = Pb.ap[:, c0:c0 + nh * 64].rearrange("p (h d) -> p h d", d=64)
                x1 = hv[:, :, 0:8]
                x2 = hv[:, :, 8:16]
                cosb = cs_all.ap[:, blk, 0:8].unsqueeze(1).to_broadcast([128, nh, 8])
                sinb = cs_all.ap[:, blk, 8:16].unsqueeze(1).to_broadcast([128, nh, 8])
                tm = [tmp[i].ap[:, 0:nh, :] for i in range(4)]
                ph.op("dve", lambda e, x1=x1, cosb=cosb, o=tm[0]: e.tensor_tensor(out=o, in0=x1, in1=cosb, op=ALU.mult), reads=[Pb, cs_all], writes=[tmp[0]])
                ph.op("dve", lambda e, x2=x2, sinb=sinb, o=tm[1]: e.tensor_tensor(out=o, in0=x2, in1=sinb, op=ALU.mult), reads=[Pb, cs_all], writes=[tmp[1]])
                ph.op("dve", lambda e, x2=x2, cosb=cosb, o=tm[2]: e.tensor_tensor(out=o, in0=x2, in1=cosb, op=ALU.mult), reads=[Pb, cs_all], writes=[tmp[2]])
                ph.op("dve", lambda e, x1=x1, sinb=sinb, o=tm[3]: e.tensor_tensor(out=o, in0=x1, in1=sinb, op=ALU.mult), reads=[Pb, cs_all], writes=[tmp[3]])
                ph.op("dve", lambda e, x1=x1, a=tm[0], b=tm[1]: e.tensor_tensor(out=x1, in0=a, in1=b, op=ALU.subtract), reads=[tmp[0], tmp[1]], writes=[Pb])
                ph.op("dve", lambda e, x2=x2, a=tm[2], b=tm[3]: e.tensor_tensor(out=x2, in0=a, in1=b, op=ALU.add), reads=[tmp[2], tmp[3]], writes=[Pb])
            ph.op("act", lambda e, tb=tb, Pb=Pb: e.activation(out=R_sb.ap[:, tb, 0:1024], in_=Pb.ap[:, 0:1024], func=AF.Copy, scale=0.125), reads=[Pb], writes=[R_sb])
            ph.op("pool", lambda e, tb=tb, Pb=Pb: e.tensor_copy(out=R_sb.ap[:, tb, 1024:1280], in_=Pb.ap[:, 1024:1280]), reads=[Pb], writes=[R_sb])
            ph.op("act", lambda e, tb=tb, Pb=Pb: e.activation(out=R_sb.ap[:, tb, 1536:2048], in_=Pb.ap[:, 1536:2048], func=AF.Copy, scale=0.125), reads=[Pb], writes=[R_sb])
            ph.op("pool", lambda e, tb=tb, Pb=Pb: e.tensor_copy(out=R_sb.ap[:, tb, 2048:2112], in_=Pb.ap[:, 2048:2112]), reads=[Pb], writes=[R_sb])
            ph.op("pool", lambda e, tb=tb, Pb=Pb, vst=vst: e.tensor_copy(out=vst.ap[:, tb, :, 0:64], in_=Pb.ap[:, 1280:1536].rearrange("p (g d) -> p g d", d=64)),
                  reads=[Pb], writes=[vst])
            ph.op("pool", lambda e, tb=tb, Pb=Pb, wist=wist: e.tensor_scalar(out=wist.ap[:, tb, :], in0=Pb.ap[:, 2112:2120], scalar1=8.0 ** -0.5, scalar2=None, op0=ALU.mult),
                  reads=[Pb], writes=[wist])
        heads = [(h * 64) for h in range(20)] + [1536 + h * 64 for h in range(9)]
        ts = slice(tt * T, (tt + 1) * T)
        for (h0, h1) in ((0, 16), (16, 29)):
            for hi in range(h0, h1):
                c0 = heads[hi]
                pt = ps_t[npt % 2]
                npt += 1
                for tb in range(4):
                    ph.op("pe", lambda e, tb=tb, c0=c0, pt=pt: e.transpose(pt.ap[0:64, tb * 128:(tb + 1) * 128], R_sb.ap[:, tb, c0:c0 + 64], ident.ap[:]),
                          reads=[R_sb, ident], writes=[pt])
                if hi % 2 == 0:
                    ph.op("act", lambda e, hi=hi, h0=h0, pt=pt, hdst=hdst: e.activation(out=hdst.ap[:, hi - h0, :], in_=pt.ap[0:64, 0:512], func=AF.Copy), reads=[pt], writes=[hdst])
                else:
                    ph.op("dve", lambda e, hi=hi, h0=h0, pt=pt, hdst=hdst: e.tensor_copy(out=hdst.ap[:, hi - h0, :], in_=pt.ap[0:64, 0:512]), reads=[pt], writes=[hdst])
            if h0 == 0:
                ph.dma("sp", lambda e, hdst=hdst, ts=ts: e.dma_start(out=t["qT"][:, :, ts], in_=hdst.ap[:, 0:16, :]), hdst, reads=[hdst])
            else:
                ph.dma("sp", lambda e, hdst=hdst, ts=ts: e.dma_start(out=t["kT"][:, :, ts], in_=hdst.ap[:, 0:4, :]), hdst, reads=[hdst])
                ph.dma("sp", lambda e, hdst=hdst, ts=ts: e.dma_start(out=t["qiT"][:, :, ts], in_=hdst.ap[:, 4:12, :]), hdst, reads=[hdst])
                ph.dma("sp", lambda e, hdst=hdst, ts=ts: e.dma_start(out=t["kiT"][:, ts], in_=hdst.ap[:, 12, :]), hdst, reads=[hdst])
        ph.dma("sp", lambda e, vst=vst, ts=ts: e.dma_start(out=t["vtm"][ts].rearrange("(tb p) g d -> p tb g d", p=128), in_=vst.ap[:]), vst, reads=[vst])
        ph.dma("sp", lambda e, wist=wist, ts=ts: e.dma_start(out=t["witm"][ts].rearrange("(tb p) c -> p tb c", p=128), in_=wist.ap[:]), wist, reads=[wist])
    return ph.finish()


LCH = 8


def phaseB(P, l):
    nc, t, S = P.nc, P.t, P.S
    ph = Phase(nc, f"B{l}", P.pool)
    NCH = S // LCH
    NK = int(round(math.log2(NCH)))
    assert (1 << NK) == NCH
    PW = min(512, NCH)
    NPART = NCH // PW

    def sb(name, shape, dt=F32):
        return ph.sbuf(name, shape, dt)

    lam_re, lam_im, ldt = sb("lam_re", [128, 16]), sb("lam_im", [128, 16]), sb("ldt", [128, 16])
    Bre, Bim, Cre, Cim = sb("Bre", [128, 16, 16]), sb("Bim", [128, 16, 16]), sb("Cre", [128, 16, 16]), sb("Cim", [128, 16, 16])
    d_sb = sb("d", [128, 4])
    zmask = sb("zmask", [128, 4, 128])
    ident_f = sb("identf", [128, 128])
    ident_b = sb("identb", [128, 128], BF16)
    for g2 in range(2):
        ps = slice(g2 * 64, (g2 + 1) * 64)
        ph.dma("sp", lambda e, ps=ps, g2=g2: e.dma_start(out=lam_re.ap[ps, :], in_=t["ssm_a_re"][l].rearrange("(q g2) n -> g2 n q", g2=2)[g2],
                                                         allow_slow_non_contiguous=True), lam_re, writes=[lam_re])
        ph.dma("sp", lambda e, ps=ps, g2=g2: e.dma_start(out=lam_im.ap[ps, :], in_=t["ssm_a_im"][l].rearrange("(q g2) n -> g2 n q", g2=2)[g2],
                                                         allow_slow_non_contiguous=True), lam_im, writes=[lam_im])
        ph.dma("sp", lambda e, ps=ps, g2=g2: e.dma_start(out=ldt.ap[ps, :], in_=t["ssm_log_dt"][l].rearrange("(q g2) -> g2 q", g2=2)[g2:g2 + 1, :].to_broadcast([64, 16]),
                                                         allow_slow_non_contiguous=True), ldt, writes=[ldt])
        for (dst, nm) in ((Bre, "ssm_b_re"), (Bim, "ssm_b_im"), (Cre, "ssm_c_re"), (Cim, "ssm_c_im")):
            ph.dma("sp", lambda e, ps=ps, g2=g2, dst=dst, nm=nm: e.dma_start(out=dst.ap[ps, :, :], in_=t[nm][l].rearrange("(q g2) n p -> g2 n q p", g2=2)[g2]),
                   dst, writes=[dst])
    ph.dma("sp", lambda e: e.dma_start(out=d_sb.ap[:], in_=t["ssm_d"][l].rearrange("(b p) -> p b", p=128), allow_slow_non_contiguous=True), d_sb, writes=[d_sb])
    ph.dma("sp", lambda e: e.dma_start(out=zmask.ap[:], in_=t["c_zmask"]), zmask, writes=[zmask])
    ph.dma("sp", lambda e: e.dma_start(out=ident_f.ap[:], in_=t["c_ident_f"]), ident_f, writes=[ident_f])
    ph.dma("sp", lambda e: e.dma_start(out=ident_b.ap[:], in_=t["c_ident_bf"]), ident_b, writes=[ident_b])

    tmpc = [sb(f"tc{i}", [128, 16]) for i in range(6)]
    dt_sb, mag = sb("dt", [128, 16]), sb("mag", [128, 16])
    red = sb("red", [128, 2, 16])
    ki, kf, msk = sb("ki", [128, 2, 16], I32), sb("kf", [128, 2, 16]), sb("msk", [128, 2, 16])
    cs = sb("cs", [128, 2, 16])
    pw_re, pw_im = sb("pw_re", [128, 9, 16]), sb("pw_im", [128, 9, 16])
    sc_re, sc_im, sc_nim = sb("sc_re", [128, NK, 16]), sb("sc_im", [128, NK, 16]), sb("sc_nim", [128, NK, 16])
    cf_re, cf_im = sb("cf_re", [128, 16]), sb("cf_im", [128, 16])
    Bb_re, Bb_im = sb("Bb_re", [128, 16, 16]), sb("Bb_im", [128, 16, 16])

    def tt(out, obuf, a, abuf, b, bbuf, op):
        ph.op("dve", lambda e: e.tensor_tensor(out=out, in0=a, in1=b, op=op), reads=[abuf, bbuf], writes=[obuf])

    def cmul(o_re, o_im, obuf_re, obuf_im, a_re, a_im, abufs, b_re, b_im, bbufs):
        tcs = [x.ap[:] for x in tmpc]
        rb = list(abufs) + list(bbufs)
        ph.op("dve", lambda e: e.tensor_tensor(out=tcs[0], in0=a_re, in1=b_re, op=ALU.mult), reads=rb, writes=[tmpc[0]])
        ph.op("dve", lambda e: e.tensor_tensor(out=tcs[1], in0=a_im, in1=b_im, op=ALU.mult), reads=rb, writes=[tmpc[1]])
        ph.op("dve", lambda e: e.tensor_tensor(out=tcs[2], in0=a_re, in1=b_im, op=ALU.mult), reads=rb, writes=[tmpc[2]])
        ph.op("dve", lambda e: e.tensor_tensor(out=tcs[3], in0=a_im, in1=b_re, op=ALU.mult), reads=rb, writes=[tmpc[3]])
        ph.op("dve", lambda e: e.tensor_tensor(out=o_re, in0=tcs[0], in1=tcs[1], op=ALU.subtract), reads=[tmpc[0], tmpc[1]], writes=[obuf_re])
        ph.op("dve", lambda e: e.tensor_tensor(out=o_im, in0=tcs[2], in1=tcs[3], op=ALU.add), reads=[tmpc[2], tmpc[3]], writes=[obuf_im])

    ph.op("act", lambda e: e.activation(out=dt_sb.ap[:], in_=ldt.ap[:], func=AF.Exp), reads=[ldt], writes=[dt_sb])
    tt(mag.ap[:], mag, dt_sb.ap[:], dt_sb, lam_re.ap[:], lam_re, ALU.mult)
    ph.op("act", lambda e: e.activation(out=mag.ap[:], in_=mag.ap[:], func=AF.Exp), reads=[mag], writes=[mag])
    tt(red.ap[:, 1, :], red, dt_sb.ap[:], dt_sb, lam_im.ap[:], lam_im, ALU.mult)
    ph.op("dve", lambda e: e.tensor_scalar(out=red.ap[:, 0, :], in0=red.ap[:, 1, :], scalar1=0.5 * math.pi, scalar2=None, op0=ALU.add), reads=[red], writes=[red])
    emit_range_reduce(ph, red, ki, kf, msk)
    ph.op("act", lambda e: e.activation(out=cs.ap[:], in_=red.ap[:], func=AF.Sin), reads=[red], writes=[cs])
    ph.op("dve", lambda e: e.memset(pw_re.ap[:, 0, :], 1.0), writes=[pw_re])
    ph.op("dve", lambda e: e.memset(pw_im.ap[:, 0, :], 0.0), writes=[pw_im])
    tt(pw_re.ap[:, 1, :], pw_re, mag.ap[:], mag, cs.ap[:, 0, :], cs, ALU.mult)
    tt(pw_im.ap[:, 1, :], pw_im, mag.ap[:], mag, cs.ap[:, 1, :], cs, ALU.mult)
    for j in range(1, 8):
        cmul(pw_re.ap[:, j + 1, :], pw_im.ap[:, j + 1, :], pw_re, pw_im, pw_re.ap[:, j, :], pw_im.ap[:, j, :], [pw_re, pw_im],
             pw_re.ap[:, 1, :], pw_im.ap[:, 1, :], [pw_re, pw_im])
    ph.op("dve", lambda e: e.tensor_copy(out=sc_re.ap[:, 0, :], in_=pw_re.ap[:, 8, :]), reads=[pw_re], writes=[sc_re])
    ph.op("dve", lambda e: e.tensor_copy(out=sc_im.ap[:, 0, :], in_=pw_im.ap[:, 8, :]), reads=[pw_im], writes=[sc_im])
    for k in range(NK - 1):
        cmul(sc_re.ap[:, k + 1, :], sc_im.ap[:, k + 1, :], sc_re, sc_im, sc_re.ap[:, k, :], sc_im.ap[:, k, :], [sc_re, sc_im],
             sc_re.ap[:, k, :], sc_im.ap[:, k, :], [sc_re, sc_im])
    ph.op("dve", lambda e: e.tensor_scalar(out=sc_nim.ap[:], in0=sc_im.ap[:], scalar1=-1.0, scalar2=None, op0=ALU.mult), reads=[sc_im], writes=[sc_nim])
    nr, den = sb("nr", [128, 16]), sb("den", [128, 16])
    ph.op("dve", lambda e: e.tensor_scalar(out=nr.ap[:], in0=pw_re.ap[:, 1, :], scalar1=-1.0, scalar2=None, op0=ALU.add), reads=[pw_re], writes=[nr])
    tt(den.ap[:], den, lam_re.ap[:], lam_re, lam_re.ap[:], lam_re, ALU.mult)
    tt(tmpc[4].ap[:], tmpc[4], lam_im.ap[:], lam_im, lam_im.ap[:], lam_im, ALU.mult)
    tt(den.ap[:], den, den.ap[:], den, tmpc[4].ap[:], tmpc[4], ALU.add)
    ph.op("dve", lambda e: e.reciprocal(out=den.ap[:], in_=den.ap[:]), reads=[den], writes=[den])
    tt(tmpc[0].ap[:], tmpc[0], nr.ap[:], nr, lam_re.ap[:], lam_re, ALU.mult)
    tt(tmpc[1].ap[:], tmpc[1], pw_im.ap[:, 1, :], pw_im, lam_im.ap[:], lam_im, ALU.mult)
    tt(tmpc[2].ap[:], tmpc[2], pw_im.ap[:, 1, :], pw_im, lam_re.ap[:], lam_re, ALU.mult)
    tt(tmpc[3].ap[:], tmpc[3], nr.ap[:], nr, lam_im.ap[:], lam_im, ALU.mult)
    tt(cf_re.ap[:], cf_re, tmpc[0].ap[:], tmpc[0], tmpc[1].ap[:], tmpc[1], ALU.add)
    tt(cf_im.ap[:], cf_im, tmpc[2].ap[:], tmpc[2], tmpc[3].ap[:], tmpc[3], ALU.subtract)
    tt(cf_re.ap[:], cf_re, cf_re.ap[:], cf_re, den.ap[:], den, ALU.mult)
    tt(cf_im.ap[:], cf_im, cf_im.ap[:], cf_im, den.ap[:], den, ALU.mult)
    tb3 = [sb(f"tb3{i}", [128, 16, 16]) for i in range(4)]

    def bc(ap2):
        return ap2.unsqueeze(2).to_broadcast([128, 16, 16])

    tt(tb3[0].ap[:], tb3[0], Bre.ap[:], Bre, bc(cf_re.ap[:]), cf_re, ALU.mult)
    tt(tb3[1].ap[:], tb3[1], Bim.ap[:], Bim, bc(cf_im.ap[:]), cf_im, ALU.mult)
    tt(tb3[2].ap[:], tb3[2], Bim.ap[:], Bim, bc(cf_re.ap[:]), cf_re, ALU.mult)
    tt(tb3[3].ap[:], tb3[3], Bre.ap[:], Bre, bc(cf_im.ap[:]), cf_im, ALU.mult)
    tt(Bb_re.ap[:], Bb_re, tb3[0].ap[:], tb3[0], tb3[1].ap[:], tb3[1], ALU.subtract)
    tt(Bb_im.ap[:], Bb_im, tb3[2].ap[:], tb3[2], tb3[3].ap[:], tb3[3], ALU.add)

    E_re, E_im = sb("E_re", [128, 8, 4, 16]), sb("E_im", [128, 8, 4, 16])
    G_re, G_nim = sb("G_re", [128, 9, 4, 16]), sb("G_nim", [128, 9, 4, 16])
    et = [sb(f"et{i}", [128, 4, 16]) for i in range(4)]
    ZE_re, ZE_im = sb("ZE_re", [128, 8, 4, 128], BF16), sb("ZE_im", [128, 8, 4, 128], BF16)
    ZG_re, ZG_nim = sb("ZG_re", [128, 9, 4, 128], BF16), sb("ZG_nim", [128, 9, 4, 128], BF16)
    W_re, W_im = sb("W_re", [128, 8, 4, 128], BF16), sb("W_im", [128, 8, 4, 128], BF16)
    Kb = sb("Kb", [128, 8, 128], BF16)
    u_sb = sb("u", [128, S], BF16)
    S_re = [sb(f"S_re{i}", [128, NCH]) for i in range(4)]
    S_im = [sb(f"S_im{i}", [128, NCH]) for i in range(4)]
    T_re, T_im = sb("T_re", [128, NCH]), sb("T_im", [128, NCH])
    t1, t2 = sb("t1", [128, NCH]), sb("t2", [128, NCH])
    Xb_re = [sb(f"Xb_re{i}", [128, NCH], BF16) for i in range(4)]
    Xb_im = [sb(f"Xb_im{i}", [128, NCH], BF16) for i in range(4)]
    y_sb = sb("y", [128, S])
    GW = min(1024, S)
    g1, g2b = sb("g1", [128, GW]), sb("g2", [128, GW])
    z_bf = sb("zbf", [128, GW], BF16)
    ps_s = [ph.psum(f"pss{i}", [128, 512], F32) for i in range(4)]
    ps_y = [ph.psum(f"psy{i}", [128, 512], F32) for i in range(2)]
    ps_t = [ph.psum(f"pst{i}", [128, 128], BF16) for i in range(1)]
    ps_k = [ph.psum(f"psk{i}", [128, 128], F32) for i in range(1)]
    nps = 0
    npy = 0

    for b in range(4):
        ph.dma("sp", lambda e, b=b: e.dma_start(out=u_sb.ap[:], in_=t["uT"][b * 128:(b + 1) * 128, :]), u_sb, writes=[u_sb])
        prs = slice(4 * b, 4 * b + 4)
        for j in range(9):
            pr = pw_re.ap[:, j, prs].unsqueeze(2).to_broadcast([128, 4, 16])
            pi = pw_im.ap[:, j, prs].unsqueeze(2).to_broadcast([128, 4, 16])
            if j < 8:
                tt(et[0].ap[:], et[0], Bb_re.ap[:, prs, :], Bb_re, pr, pw_re, ALU.mult)
                tt(et[1].ap[:], et[1], Bb_im.ap[:, prs, :], Bb_im, pi, pw_im, ALU.mult)
                tt(et[2].ap[:], et[2], Bb_im.ap[:, prs, :], Bb_im, pr, pw_re, ALU.mult)
                tt(et[3].ap[:], et[3], Bb_re.ap[:, prs, :], Bb_re, pi, pw_im, ALU.mult)
                tt(E_re.ap[:, j], E_re, et[0].ap[:], et[0], et[1].ap[:], et[1], ALU.subtract)
                tt(E_im.ap[:, j], E_im, et[2].ap[:], et[2], et[3].ap[:], et[3], ALU.add)
            tt(et[0].ap[:], et[0], Cre.ap[:, prs, :], Cre, pr, pw_re, ALU.mult)
            tt(et[1].ap[:], et[1], Cim.ap[:, prs, :], Cim, pi, pw_im, ALU.mult)
            tt(et[2].ap[:], et[2], Cim.ap[:, prs, :], Cim, pr, pw_re, ALU.mult)
            tt(et[3].ap[:], et[3], Cre.ap[:, prs, :], Cre, pi, pw_im, ALU.mult)
            tt(G_re.ap[:, j], G_re, et[0].ap[:], et[0], et[1].ap[:], et[1], ALU.subtract)
            ph.op("dve", lambda e, j=j: e.scalar_tensor_tensor(out=G_nim.ap[:, j], in0=et[2].ap[:], scalar=-1.0, in1=et[3].ap[:], op0=ALU.mult, op1=ALU.subtract),
                  reads=[et[2], et[3]], writes=[G_nim])
        for i in range(4):
            mk = zmask.ap[:, i, :].rearrange("p (a c) -> p a c", c=16)
            for j in range(9):
                for (src, dst) in (((E_re, ZE_re), (E_im, ZE_im)) if j < 8 else ()) + ((G_re, ZG_re), (G_nim, ZG_nim)):
                    s_ap = src.ap[:, j, i, :].unsqueeze(1).to_broadcast([128, 8, 16])
                    d_ap = dst.ap[:, j, i, :].rearrange("p (a c) -> p a c", c=16)
                    ph.op("dve", lambda e, s_ap=s_ap, d_ap=d_ap, mk=mk: e.tensor_tensor(out=d_ap, in0=s_ap, in1=mk, op=ALU.mult),
                          reads=[src, zmask], writes=[dst])
        nw = 0
        for (zs, wd) in ((ZE_re, W_re), (ZE_im, W_im)):
            for j in range(8):
                for i in range(4):
                    pt = ps_t[0]
                    ph.op("pe", lambda e, zs=zs, j=j, i=i, pt=pt: e.transpose(pt.ap[:, 0:128], zs.ap[:, j, i, :], ident_b.ap[:]), reads=[zs, ident_b], writes=[pt])
                    if nw % 2 == 0:
                        ph.op("act", lambda e, wd=wd, j=j, i=i, pt=pt: e.activation(out=wd.ap[:, j, i, :], in_=pt.ap[:, 0:128], func=AF.Copy), reads=[pt], writes=[wd])
                    else:
                        ph.op("dve", lambda e, wd=wd, j=j, i=i, pt=pt: e.tensor_copy(out=wd.ap[:, j, i, :], in_=pt.ap[:, 0:128]), reads=[pt], writes=[wd])
                    nw += 1
        for j in range(8):
            pk = ps_k[0]
            n = 0
            for i in range(4):
                for (za, zb) in ((ZE_re, ZG_re), (ZE_im, ZG_nim)):
                    ph.op("pe", lambda e, za=za, zb=zb, j=j, i=i, n=n, pk=pk: e.matmul(pk.ap[:, 0:128], za.ap[:, j, i, :], zb.ap[:, 0, i, :], start=(n == 0), stop=(n == 7)),
                          reads=[za, zb], writes=[pk])
                    n += 1
            if j == 0:
                ph.op("dve", lambda e, pk=pk, b=b: e.scalar_tensor_tensor(out=Kb.ap[:, 0, :], in0=ident_f.ap[:], scalar=d_sb.ap[:, b:b + 1], in1=pk.ap[:, 0:128],
                                                                         op0=ALU.mult, op1=ALU.add), reads=[pk, ident_f, d_sb], writes=[Kb])
            else:
                ph.op("act", lambda e, pk=pk, j=j: e.activation(out=Kb.ap[:, j, :], in_=pk.ap[:, 0:128], func=AF.Copy), reads=[pk], writes=[Kb])
        uv = u_sb.ap[:].rearrange("p (c l) -> p c l", l=LCH)
        for i in range(4):
            for (wd, Sd) in ((W_re, S_re[i]), (W_im, S_im[i])):
                for part in range(NPART):
                    cs_ = slice(part * PW, (part + 1) * PW)
                    pss = ps_s[nps % 4]
                    nps += 1
                    for tl in range(LCH):
                        ph.op("pe", lambda e, wd=wd, i=i, tl=tl, cs_=cs_, pss=pss: e.matmul(pss.ap[:, 0:PW], wd.ap[:, LCH - 1 - tl, i, :], uv[:, cs_, tl],
                                                                                         start=(tl == 0), stop=(tl == LCH - 1)),
                              reads=[wd, u_sb], writes=[pss])
                    ph.op("act", lambda e, Sd=Sd, cs_=cs_, pss=pss: e.activation(out=Sd.ap[:, cs_], in_=pss.ap[:, 0:PW], func=AF.Copy), reads=[pss], writes=[Sd])
        for i in range(4):
            pair = 4 * b + i
            cur = (S_re[i], S_im[i])
            nxt = (T_re, T_im)
            for k in range(NK):
                s = 1 << k
                pr = sc_re.ap[:, k, pair:pair + 1]
                pi = sc_im.ap[:, k, pair:pair + 1]
                npi = sc_nim.ap[:, k, pair:pair + 1]
                cr, ci = cur
                nr_, ni_ = nxt
                ph.op("dve", lambda e, cr=cr, pr=pr, s=s: e.scalar_tensor_tensor(out=t1.ap[:, s:], in0=cr.ap[:, :NCH - s], scalar=pr, in1=cr.ap[:, s:], op0=ALU.mult, op1=ALU.add),
                      reads=[cr, sc_re], writes=[t1])
                ph.op("dve", lambda e, ci=ci, npi=npi, nr_=nr_, s=s: e.scalar_tensor_tensor(out=nr_.ap[:, s:], in0=ci.ap[:, :NCH - s], scalar=npi, in1=t1.ap[:, s:], op0=ALU.mult, op1=ALU.add),
                      reads=[ci, sc_nim, t1], writes=[nr_])
                ph.op("dve", lambda e, ci=ci, pr=pr, s=s: e.scalar_tensor_tensor(out=t2.ap[:, s:], in0=ci.ap[:, :NCH - s], scalar=pr, in1=ci.ap[:, s:], op0=ALU.mult, op1=ALU.add),
                      reads=[ci, sc_re], writes=[t2])
                ph.op("dve", lambda e, cr=cr, pi=pi, ni_=ni_, s=s: e.scalar_tensor_tensor(out=ni_.ap[:, s:], in0=cr.ap[:, :NCH - s], scalar=pi, in1=t2.ap[:, s:], op0=ALU.mult, op1=ALU.add),
                      reads=[cr, sc_im, t2], writes=[ni_])
                ph.op("pool", lambda e, cr=cr, nr_=nr_, s=s: e.tensor_copy(out=nr_.ap[:, 0:s], in_=cr.ap[:, 0:s]), reads=[cr], writes=[nr_])
                ph.op("pool", lambda e, ci=ci, ni_=ni_, s=s: e.tensor_copy(out=ni_.ap[:, 0:s], in_=ci.ap[:, 0:s]), reads=[ci], writes=[ni_])
                cur, nxt = nxt, cur
            fr, fi = cur
            ph.op("pool", lambda e, i=i: e.memset(Xb_re[i].ap[:, 0:1], 0.0), writes=[Xb_re[i]])
            ph.op("pool", lambda e, i=i: e.memset(Xb_im[i].ap[:, 0:1], 0.0), writes=[Xb_im[i]])
            ph.op("act", lambda e, i=i, fr=fr: e.activation(out=Xb_re[i].ap[:, 1:], in_=fr.ap[:, :NCH - 1], func=AF.Copy), reads=[fr], writes=[Xb_re[i]])
            ph.op("act", lambda e, i=i, fi=fi: e.activation(out=Xb_im[i].ap[:, 1:], in_=fi.ap[:, :NCH - 1], func=AF.Copy), reads=[fi], writes=[Xb_im[i]])
        yv = y_sb.ap[:].rearrange("p (c l) -> p c l", l=LCH)
        for tl in range(LCH):
            for part in range(NPART):
                cs_ = slice(part * PW, (part + 1) * PW)
                py = ps_y[npy % 2]
                npy += 1
                nmm = (tl + 1) + 8
                n = 0
                for j in range(tl + 1):
                    ph.op("pe", lambda e, j=j, tl=tl, cs_=cs_, py=py, n=n, nmm=nmm: e.matmul(py.ap[:, 0:PW], Kb.ap[:, j, :], uv[:, cs_, tl - j], start=(n == 0), stop=(n == nmm - 1)),
                          reads=[Kb, u_sb], writes=[py])
                    n += 1
                for i in range(4):
                    for (zg, xb) in ((ZG_re, Xb_re[i]), (ZG_nim, Xb_im[i])):
                        ph.op("pe", lambda e, zg=zg, xb=xb, i=i, tl=tl, cs_=cs_, py=py, n=n, nmm=nmm: e.matmul(py.ap[:, 0:PW], zg.ap[:, tl + 1, i, :], xb.ap[:, cs_],
                                                                                                          start=(n == 0), stop=(n == nmm - 1)),
                              reads=[zg, xb], writes=[py])
                        n += 1
                ph.op("act", lambda e, tl=tl, cs_=cs_, py=py: e.activation(out=yv[:, cs_, tl], in_=py.ap[:, 0:PW], func=AF.Copy), reads=[py], writes=[y_sb])
        for c0 in range(0, S, GW):
            ysl = y_sb.ap[:, c0:c0 + GW]
            ph.op("act", lambda e, ysl=ysl: e.activation(out=g1.ap[:], in_=ysl, func=AF.Square), reads=[y_sb], writes=[g1])
            ph.op("dve", lambda e: e.tensor_scalar(out=g1.ap[:], in0=g1.ap[:], scalar1=0.044715, scalar2=1.0, op0=ALU.mult, op1=ALU.add), reads=[g1], writes=[g1])
            ph.op("dve", lambda e, ysl=ysl: e.tensor_tensor(out=g2b.ap[:], in0=g1.ap[:], in1=ysl, op=ALU.mult), reads=[g1, y_sb], writes=[g2b])
            ph.op("act", lambda e: e.activation(out=g1.ap[:], in_=g2b.ap[:], func=AF.Sigmoid, scale=1.5957691216057308), reads=[g2b], writes=[g1])
            ph.op("dve", lambda e, ysl=ysl: e.tensor_tensor(out=z_bf.ap[:], in0=g1.ap[:], in1=ysl, op=ALU.mult), reads=[g1, y_sb], writes=[z_bf])
            ph.dma("sp", lambda e, b=b, c0=c0: e.dma_start(out=t["zT"][b * 128:(b + 1) * 128, c0:c0 + GW], in_=z_bf.ap[:]), z_bf, reads=[z_bf])
    return ph.finish()


def phaseB3(P, l):
    nc, t, S = P.nc, P.t, P.S
    ph = Phase(nc, f"G{l}", P.pool)
    T = 512
    wg = ph.sbuf("wg", [128, 4, 512], BF16)
    ph.dma("sp", lambda e: e.dma_start(out=wg.ap[:], in_=t["b_ssm_glu"][l].rearrange("(kb p) n -> p kb n", p=128)), wg, writes=[wg])
    zb = [ph.sbuf(f"z{i}", [128, 4, T], BF16) for i in range(2)]
    sg = [ph.sbuf(f"sg{i}", [128, T], F32) for i in range(2)]
    yo = [ph.sbuf(f"yo{i}", [128, 4, T], BF16) for i in range(2)]
    pg = [ph.psum(f"pg{i}", [128, T], F32) for i in range(2)]
    n = 0
    for tt_ in range(S // T):
        z = zb[tt_ % 2]
        y = yo[tt_ % 2]
        ts = slice(tt_ * T, (tt_ + 1) * T)
        ph.dma("sp", lambda e, z=z, ts=ts: e.dma_start(out=z.ap[:], in_=t["zT"].rearrange("(kb p) s -> p kb s", p=128)[:, :, ts]), z, writes=[z])
        for ob in range(4):
            p_ = pg[n % 2]
            s_ = sg[n % 2]
            n += 1
            for kb in range(4):
                ph.op("pe", lambda e, kb=kb, ob=ob, p_=p_, z=z: e.matmul(p_.ap[:], wg.ap[:, kb, ob * 128:(ob + 1) * 128], z.ap[:, kb, :], start=(kb == 0), stop=(kb == 3)),
                      reads=[wg, z], writes=[p_])
            ph.op("act", lambda e, p_=p_, s_=s_: e.activation(out=s_.ap[:], in_=p_.ap[:], func=AF.Sigmoid), reads=[p_], writes=[s_])
            ph.op("dve", lambda e, ob=ob, z=z, y=y, s_=s_: e.tensor_tensor(out=y.ap[:, ob, :], in0=z.ap[:, ob, :], in1=s_.ap[:], op=ALU.mult), reads=[z, s_], writes=[y])
        ph.dma("sp", lambda e, y=y, ts=ts: e.dma_start(out=t["ybT"].rearrange("(kb p) s -> p kb s", p=128)[:, :, ts], in_=y.ap[:]), y, reads=[y])
    return ph.finish()


NBIS = 18


def phaseC(P, l, qb0, qb1):
    nc, t, S = P.nc, P.t, P.S
    ph = Phase(nc, f"C{l}_{qb0}", P.pool)
    NKMAX = qb1 * 128
    NKB = qb1
    kT_sb = ph.sbuf("kT", [128, 2, NKMAX], BF16)
    kiT_sb = ph.sbuf("kiT", [64, NKMAX], BF16)
    v_sb = ph.sbuf("v", [128, NKB, 4, 128], BF16)
    causal = ph.sbuf("causal", [128, 128], F32)
    ident = ph.sbuf("ident", [128, 128], BF16)
    sel = ph.sbuf("sel", [128, 64], F32)
    half = ph.sbuf("half", [128, 1], F32)
    for hp in range(2):
        ph.dma("sp", lambda e, hp=hp: e.dma_start(out=kT_sb.ap[hp * 64:(hp + 1) * 64, :, :], in_=t["kT"][:, 2 * hp:2 * hp + 2, 0:NKMAX]), kT_sb, writes=[kT_sb])
    ph.dma("sp", lambda e: e.dma_start(out=kiT_sb.ap[:], in_=t["kiT"][:, 0:NKMAX]), kiT_sb, writes=[kiT_sb])
    for k0 in range(0, NKB, 16):
        k1 = min(NKB, k0 + 16)
        ph.dma("sp", lambda e, k0=k0, k1=k1: e.dma_start(out=v_sb.ap[:, k0:k1], in_=t["vtm"][k0 * 128:k1 * 128].rearrange("(kb p) g d -> p kb g d", p=128)),
               v_sb, writes=[v_sb])
    ph.dma("sp", lambda e: e.dma_start(out=causal.ap[:], in_=t["c_causal"]), causal, writes=[causal])
    ph.dma("sp", lambda e: e.dma_start(out=ident.ap[:], in_=t["c_ident_bf"]), ident, writes=[ident])
    ph.dma("sp", lambda e: e.dma_start(out=sel.ap[:], in_=t["c_sel"]), sel, writes=[sel])
    ph.op("dve", lambda e: e.memset(half.ap[:], 0.5), writes=[half])

    qblk = [ph.sbuf(f"q{i}", [128, 8, 128], BF16) for i in range(2)]
    qiblk = [ph.sbuf(f"qi{i}", [64, 8, 128], BF16) for i in range(2)]
    wi_sb = [ph.sbuf(f"wi{i}", [128, 8], F32) for i in range(2)]
    acc = ph.sbuf("acc", [128, NKMAX], F32)
    mask = ph.sbuf("mask", [128, NKMAX], BF16)
    maskT = ph.sbuf("maskT", [128, NKB, 128], BF16)
    rl = [ph.sbuf(f"rl{i}", [128, 512], F32) for i in range(2)]
    pT = [ph.sbuf(f"pT{i}", [128, 4, 128], BF16) for i in range(3)]
    pm = [ph.sbuf(f"pm{i}", [128, 4, 128], BF16) for i in range(3)]
    osb = [ph.sbuf(f"osb{i}", [128, 512], F32) for i in range(2)]
    rec = [ph.sbuf(f"rec{i}", [64, 512], F32) for i in range(2)]
    yst = [ph.sbuf(f"yst{i}", [64, 4, 128], BF16) for i in range(2)]
    lo, hi, mid, ge, dd, ee = [ph.sbuf(nm, [128, 1], F32) for nm in ("lo", "hi", "mid", "ge", "dd", "ee")]
    cnt = ph.sbuf("cnt", [128, NBIS], F32)
    cnt2 = ph.sbuf("cnt2", [128, NBIS], F32)
    tot = ph.sbuf("tot", [128, 1], F32)
    maskhi = ph.token("maskhi")
    ps_o = [ph.psum(f"po{g}", [128, 512], F32) for g in range(4)]
    ps_s = [ph.psum(f"ps{i}", [128, 512], F32) for i in range(2)]
    ps_m = [ph.psum(f"psm{i}", [128, 512], F32) for i in range(2)]
    ps_mt = [ps_m[i].ap[:].bitcast(BF16) for i in range(2)]
    nm_ = 0
    nsc = 0
    npt = 0

    for qb in range(qb0, qb1):
        nk = (qb + 1) * 128
        qs = slice(qb * 128, (qb + 1) * 128)
        qbk, qik, wik = qblk[qb % 2], qiblk[qb % 2], wi_sb[qb % 2]
        for hp in range(2):
            ph.dma("sp", lambda e, hp=hp, qbk=qbk, qs=qs: e.dma_start(out=qbk.ap[hp * 64:(hp + 1) * 64, :, :], in_=t["qT"][:, 8 * hp:8 * hp + 8, qs]), qbk, writes=[qbk])
        ph.dma("sp", lambda e, qik=qik, qs=qs: e.dma_start(out=qik.ap[:], in_=t["qiT"][:, :, qs]), qik, writes=[qik])
        ph.dma("sp", lambda e, wik=wik, qs=qs: e.dma_start(out=wik.ap[:], in_=t["witm"][qs, :]), wik, writes=[wik])
        for k0 in range(0, nk, 512):
            w = min(512, nk - k0)
            for h in range(8):
                pi_ = ps_m[nm_ % 2]
                r_ = rl[nm_ % 2]
                nm_ += 1
                ph.op("pe", lambda e, pi_=pi_, h=h, k0=k0, w=w, qik=qik: e.matmul(pi_.ap[:, 0:w], qik.ap[:, h, :], kiT_sb.ap[:, k0:k0 + w], start=True, stop=True),
                      reads=[qik, kiT_sb], writes=[pi_])
                ph.op("act", lambda e, pi_=pi_, r_=r_, w=w: e.activation(out=r_.ap[:, 0:w], in_=pi_.ap[:, 0:w], func=AF.Relu), reads=[pi_], writes=[r_])
                if h == 0:
                    ph.op("dve", lambda e, r_=r_, k0=k0, w=w, wik=wik: e.tensor_scalar(out=acc.ap[:, k0:k0 + w], in0=r_.ap[:, 0:w], scalar1=wik.ap[:, 0:1], scalar2=None, op0=ALU.mult),
                          reads=[r_, wik], writes=[acc])
                else:
                    ph.op("dve", lambda e, r_=r_, k0=k0, w=w, h=h, wik=wik: e.scalar_tensor_tensor(out=acc.ap[:, k0:k0 + w], in0=r_.ap[:, 0:w], scalar=wik.ap[:, h:h + 1],
                                                                                                 in1=acc.ap[:, k0:k0 + w], op0=ALU.mult, op1=ALU.add),
                          reads=[r_, wik, acc], writes=[acc])
        ph.op("dve", lambda e, nk=nk: e.tensor_reduce(out=lo.ap[:], in_=acc.ap[:, 0:nk], axis=AX.X, op=ALU.min), reads=[acc], writes=[lo])
        ph.op("dve", lambda e, qs=qs: e.tensor_tensor(out=acc.ap[:, qs], in0=acc.ap[:, qs], in1=causal.ap[:], op=ALU.add), reads=[acc, causal], writes=[acc])
        ph.op("dve", lambda e, nk=nk: e.tensor_reduce(out=hi.ap[:], in_=acc.ap[:, 0:nk], axis=AX.X, op=ALU.max), reads=[acc], writes=[hi])
        ph.op("dve", lambda e: e.tensor_scalar(out=hi.ap[:], in0=hi.ap[:], scalar1=1.0, scalar2=None, op0=ALU.add), reads=[hi], writes=[hi])
        ph.op("dve", lambda e: e.memset(cnt.ap[:], 0.0), writes=[cnt])
        kd = nk if nk <= 512 else max(128, (int(nk * 0.42) // 128) * 128)
        n2 = nk - kd
        if n2 > 0:
            ph.op("dve", lambda e: e.memset(cnt2.ap[:], 0.0), writes=[cnt2])
        for it in range(NBIS):
            ph.op("dve", lambda e: e.scalar_tensor_tensor(out=mid.ap[:], in0=lo.ap[:], scalar=hi.ap[:, 0:1], in1=half.ap[:], op0=ALU.add, op1=ALU.mult),
                  reads=[lo, hi, half], writes=[mid])
            ph.op("dve", lambda e, kd=kd, it=it: e.tensor_scalar(out=mask.ap[:, 0:kd], in0=acc.ap[:, 0:kd], scalar1=mid.ap[:, 0:1], scalar2=0.0, op0=ALU.is_ge, op1=ALU.add,
                                                                accum_out=cnt.ap[:, it:it + 1]), reads=[acc, mid], writes=[mask, cnt])
            if n2 > 0:
                ph.op("act", lambda e, kd=kd, nk=nk, it=it: e.activation(out=mask.ap[:, kd:nk], in_=acc.ap[:, kd:nk], func=AF.Sign, bias=mid.ap[:, 0:1], scale=-1.0,
                                                                          accum_out=cnt2.ap[:, it:it + 1]), reads=[acc, mid], writes=[maskhi, cnt2])
                ph.op("dve", lambda e, it=it: e.scalar_tensor_tensor(out=tot.ap[:], in0=cnt2.ap[:, it:it + 1], scalar=-0.5, in1=cnt.ap[:, it:it + 1], op0=ALU.mult, op1=ALU.add),
                      reads=[cnt, cnt2], writes=[tot])
                ph.op("dve", lambda e, n2=n2: e.tensor_scalar(out=ge.ap[:], in0=tot.ap[:], scalar1=TOPK - 0.5 - 0.5 * n2, scalar2=None, op0=ALU.is_ge), reads=[tot], writes=[ge])
            else:
                ph.op("dve", lambda e, it=it: e.tensor_scalar(out=ge.ap[:], in0=cnt.ap[:, it:it + 1], scalar1=TOPK - 0.5, scalar2=None, op0=ALU.is_ge), reads=[cnt], writes=[ge])
            ph.op("dve", lambda e: e.tensor_tensor(out=dd.ap[:], in0=mid.ap[:], in1=lo.ap[:], op=ALU.subtract), reads=[mid, lo], writes=[dd])
            ph.op("dve", lambda e: e.tensor_tensor(out=ee.ap[:], in0=hi.ap[:], in1=mid.ap[:], op=ALU.subtract), reads=[mid, hi], writes=[ee])
            ph.op("dve", lambda e: e.scalar_tensor_tensor(out=lo.ap[:], in0=dd.ap[:], scalar=ge.ap[:, 0:1], in1=lo.ap[:], op0=ALU.mult, op1=ALU.add),
                  reads=[dd, ge, lo], writes=[lo])
            ph.op("dve", lambda e: e.scalar_tensor_tensor(out=hi.ap[:], in0=ee.ap[:], scalar=ge.ap[:, 0:1], in1=mid.ap[:], op0=ALU.mult, op1=ALU.add),
                  reads=[ee, ge, mid], writes=[hi])
        ph.op("dve", lambda e, nk=nk: e.tensor_scalar(out=mask.ap[:, 0:nk], in0=acc.ap[:, 0:nk], scalar1=lo.ap[:, 0:1], scalar2=None, op0=ALU.is_ge),
              reads=[acc, lo], writes=[mask, maskhi])
        for kb0 in range(0, qb + 1, 4):
            kb1 = min(qb + 1, kb0 + 4)
            pmt_buf = ps_m[nm_ % 2]
            pmt = ps_mt[nm_ % 2]
            nm_ += 1
            for kb in range(kb0, kb1):
                ph.op("pe", lambda e, kb=kb, kb0=kb0, pmt=pmt: e.transpose(pmt[:, (kb - kb0) * 128:(kb - kb0 + 1) * 128], mask.ap[:, kb * 128:(kb + 1) * 128], ident.ap[:]),
                      reads=[mask, maskhi, ident], writes=[pmt_buf])
            ph.op("act", lambda e, kb0=kb0, kb1=kb1, pmt=pmt: e.activation(out=maskT.ap[:, kb0:kb1, :], in_=pmt[:, 0:(kb1 - kb0) * 128].rearrange("p (a b) -> p a b", b=128), func=AF.Copy),
                  reads=[pmt_buf], writes=[maskT])
        for kb in range(qb + 1):
            for g in range(4):
                pb = (g // 2) * 64
                gi = g % 2
                ps_ = ps_s[nsc % 2]
                nsc += 1
                pT_ = pT[npt % 3]
                pm_ = pm[npt % 3]
                npt += 1
                ph.op("pe", lambda e, ps_=ps_, pb=pb, gi=gi, kb=kb, qbk=qbk: e.matmul(ps_.ap[:], kT_sb.ap[pb:pb + 64, gi, kb * 128:(kb + 1) * 128],
                                                                                    qbk.ap[pb:pb + 64, gi * 4:gi * 4 + 4, :], start=True, stop=True),
                      reads=[kT_sb, qbk], writes=[ps_])
                ph.op("act", lambda e, ps_=ps_, pT_=pT_: e.activation(out=pT_.ap[:].rearrange("p a b -> p (a b)"), in_=ps_.ap[:], func=AF.Exp), reads=[ps_], writes=[pT_])
                meng = "dve" if (g % 4) != 3 else "pool"
                ph.op(meng, lambda e, pT_=pT_, pm_=pm_, kb=kb: e.tensor_tensor(out=pm_.ap[:], in0=pT_.ap[:], in1=maskT.ap[:, kb, :].unsqueeze(1).to_broadcast([128, 4, 128]), op=ALU.mult),
                      reads=[pT_, maskT], writes=[pm_])
                ph.op("pe", lambda e, g=g, kb=kb, pm_=pm_, qb=qb: e.matmul(ps_o[g].ap[:], v_sb.ap[:, kb, g, :], pm_.ap[:].rearrange("p a b -> p (a b)"), start=(kb == 0), stop=(kb == qb)),
                      reads=[v_sb, pm_], writes=[ps_o[g]])
        for g in range(4):
            ob, rc, ys = osb[g % 2], rec[g % 2], yst[g % 2]
            ph.op("act", lambda e, g=g, ob=ob: e.activation(out=ob.ap[:], in_=ps_o[g].ap[:], func=AF.Copy), reads=[ps_o[g]], writes=[ob])
            pr_ = ps_m[nm_ % 2]
            nm_ += 1
            ph.op("pe", lambda e, pr_=pr_, ob=ob: e.matmul(pr_.ap[0:64, :], sel.ap[:], ob.ap[:], start=True, stop=True), reads=[sel, ob], writes=[pr_])
            ph.op("dve", lambda e, pr_=pr_, rc=rc: e.reciprocal(out=rc.ap[:], in_=pr_.ap[0:64, :]), reads=[pr_], writes=[rc])
            ph.op("dve", lambda e, ob=ob, rc=rc, ys=ys: e.tensor_tensor(out=ys.ap[:].rearrange("p a b -> p (a b)"), in0=ob.ap[0:64, :], in1=rc.ap[:], op=ALU.mult),
                  reads=[ob, rc], writes=[ys])
            ph.dma("sp", lambda e, g=g, ys=ys, qs=qs: e.dma_start(out=t["ycT"].rearrange("(h d) s -> d h s", d=64)[:, 4 * g:4 * g + 4, qs], in_=ys.ap[:]), ys, reads=[ys])
    return ph.finish()


G0 = 4168
NWB = 5


def phaseD(P, l, xsrc, last):
    nc, t, S = P.nc, P.t, P.S
    T = 512
    NT = S // T
    ph = Phase(nc, f"D{l}", P.pool)
    ones_bf = ph.sbuf("ones", [128, 128], BF16)
    eps_sb = ph.sbuf("eps", [128, 1], F32)
    g1_sb = ph.sbuf("g1", [128, KC], F32)
    g2_sb = ph.sbuf("g2", [128, KC], F32)
    g3_sb = ph.sbuf("g3", [128, KC], F32)
    scw = ph.sbuf("scw", [128, 4, 3], F32)
    fw = ph.sbuf("fw", [128, 88, 3], F32)
    ph.op("dve", lambda e: e.memset(ones_bf.ap[:], 1.0), writes=[ones_bf])
    ph.op("dve", lambda e: e.memset(eps_sb.ap[:], NORM_EPS), writes=[eps_sb])
    ph.dma("sp", lambda e: e.dma_start(out=g1_sb.ap[:], in_=t["norm_mix"][l].rearrange("(kc p) -> p kc", p=128), allow_slow_non_contiguous=True), g1_sb, writes=[g1_sb])
    ph.dma("sp", lambda e: e.dma_start(out=g2_sb.ap[:], in_=t["norm_ffn"][l].rearrange("(kc p) -> p kc", p=128), allow_slow_non_contiguous=True), g2_sb, writes=[g2_sb])
    ph.dma("sp", lambda e: e.dma_start(out=g3_sb.ap[:], in_=t["norm_final"].rearrange("(kc p) -> p kc", p=128), allow_slow_non_contiguous=True), g3_sb, writes=[g3_sb])
    for j in range(3):
        ph.dma("sp", lambda e, j=j: e.dma_start(out=scw.ap[:, :, j], in_=t["sc_conv"][l, j].rearrange("(c p) -> p c", p=128), allow_slow_non_contiguous=True), scw, writes=[scw])
        for r0 in range(0, 88, 22):
            ph.dma("sp", lambda e, r0=r0, j=j: e.dma_start(out=fw.ap[:, r0:r0 + 22, j], in_=t["ffn_conv"][l, j].rearrange("(r p) -> p r", p=128)[:, r0:r0 + 22],
                                                           allow_slow_non_contiguous=True), fw, writes=[fw])

    x_sb = ph.sbuf("x", [128, KC, T], F32)
    h_sb = ph.sbuf("h", [128, KC, T], BF16)
    big = ph.sbuf("big", [128, 22, T], BF16)
    ya = ph.sbuf("ya", [128, 4, T], BF16)
    cx = ph.sbuf("cx", [128, 4, T + 2], F32)
    yb = ph.sbuf("yb", [128, 4, T], BF16)
    yc = ph.sbuf("yc", [128, 8, T], BF16)
    sqb = [ph.sbuf(f"sq{i}", [128, T], BF16) for i in range(2)]
    rstd = ph.sbuf("rstd", [128, T], F32)
    cacc = [ph.sbuf(f"cacc{i}", [128, T], F32) for i in range(2)]
    sg = [ph.sbuf(f"sg{i}", [128, T], F32) for i in range(3)]
    mt = [ph.sbuf(f"mt{i}", [128, T], F32) for i in range(2)]
    tmpx = sg[2]
    sl = sg[0:2]
    ost = mt
    ug = [ph.sbuf(f"ug{i}", [128, T + 2], F32) for i in range(2)]
    uu = [ph.sbuf(f"uu{i}", [128, T + 2], F32) for i in range(2)]
    carry = ph.sbuf("carry", [128, 88, 2], F32)
    wb = [ph.sbuf(f"wb{i}", [128, KC * 512], BF16) for i in range(NWB)]
    ps_ss = ph.psum("ss", [128, T], F32)
    psg = [ph.psum(f"pg{i}", [128, T], F32) for i in range(7)]
    ph.op("pool", lambda e: e.memset(cx.ap[:], 0.0), writes=[cx])
    ph.op("pool", lambda e: e.memset(carry.ap[:], 0.0), writes=[carry])
    st = {"nw": 0, "np": 0}

    def wbuf():
        b = wb[st["nw"] % NWB]
        st["nw"] += 1
        return b

    def pbank():
        b = psg[st["np"] % 7]
        st["np"] += 1
        return b

    def load_cols(src2d, c0, ncols, krows=KC):
        b = wbuf()
        v = b.ap[:, 0:krows * ncols].rearrange("p (k n) -> p k n", n=ncols)
        ph.dma("sp", lambda e, v=v: e.dma_start(out=v, in_=src2d.rearrange("(k p) n -> p k n", p=128)[:, :, c0:c0 + ncols]), b, writes=[b])
        return b, v

    xsrc_r = xsrc.rearrange("(kc p) s -> p kc s", p=128)
    xdst_r = t["xres"].rearrange("(kc p) s -> p kc s", p=128)
    out_r = t["outT"].rearrange("(kc p) s -> p kc s", p=128)
    w_in = t["b_w_in"][l]
    for tt_ in range(NT):
        ts = slice(tt_ * T, (tt_ + 1) * T)
        for q4 in range(4):
            ph.dma("sp", lambda e, q4=q4, ts=ts: e.dma_start(out=x_sb.ap[:, q4 * 4:(q4 + 1) * 4, :], in_=xsrc_r[:, q4 * 4:(q4 + 1) * 4, ts]), x_sb, writes=[x_sb])
        ph.dma("sp", lambda e, ts=ts: e.dma_start(out=yb.ap[:], in_=t["ybT"].rearrange("(k p) s -> p k s", p=128)[:, :, ts]), yb, writes=[yb])
        ph.dma("sp", lambda e, ts=ts: e.dma_start(out=yc.ap[:], in_=t["ycT"].rearrange("(k p) s -> p k s", p=128)[:, :, ts]), yc, writes=[yc])
        emit_norm(ph, x_sb, h_sb, g1_sb, ones_bf, eps_sb, ps_ss, sqb, rstd, T)
        pcs = [load_cols(w_in, j * 512, 512) for j in range(3)]
        for c in range(4):
            pacc = []
            for j in range(3):
                pb_ = pbank()
                wbuf_, wv = pcs[j]
                for kc in range(KC):
                    ph.op("pe", lambda e, pb_=pb_, wv=wv, kc=kc, c=c: e.matmul(pb_.ap[:], wv[:, kc, c * 128:(c + 1) * 128], h_sb.ap[:, kc, :], start=(kc == 0), stop=(kc == KC - 1)),
                          reads=[wbuf_, h_sb], writes=[pb_])
                pacc.append(pb_)
            px, pbb, pcc = pacc
            ca = cacc[c % 2]
            ph.op("act", lambda e, px=px: e.activation(out=tmpx.ap[:], in_=px.ap[:], func=AF.Copy), reads=[px], writes=[tmpx])
            ph.op("dve", lambda e, c=c, pcc=pcc: e.tensor_tensor(out=cx.ap[:, c, 2:T + 2], in0=tmpx.ap[:], in1=pcc.ap[:], op=ALU.mult), reads=[tmpx, pcc], writes=[cx])
            ph.op("dve", lambda e, c=c, ca=ca: e.tensor_scalar(out=ca.ap[:], in0=cx.ap[:, c, 2:T + 2], scalar1=scw.ap[:, c, 2:3], scalar2=None, op0=ALU.mult), reads=[cx, scw], writes=[ca])
            ph.op("dve", lambda e, c=c, ca=ca: e.scalar_tensor_tensor(out=ca.ap[:], in0=cx.ap[:, c, 1:T + 1], scalar=scw.ap[:, c, 1:2], in1=ca.ap[:], op0=ALU.mult, op1=ALU.add),
                  reads=[cx, scw, ca], writes=[ca])
            ph.op("dve", lambda e, c=c, ca=ca: e.scalar_tensor_tensor(out=ca.ap[:], in0=cx.ap[:, c, 0:T], scalar=scw.ap[:, c, 0:1], in1=ca.ap[:], op0=ALU.mult, op1=ALU.add),
                  reads=[cx, scw, ca], writes=[ca])
            ph.op("dve", lambda e, c=c, ca=ca, pbb=pbb: e.tensor_tensor(out=ya.ap[:, c, :], in0=ca.ap[:], in1=pbb.ap[:], op=ALU.mult), reads=[ca, pbb], writes=[ya])
            ph.op("pool", lambda e, c=c: e.tensor_copy(out=cx.ap[:, c, 0:2], in_=cx.ap[:, c, T:T + 2]), reads=[cx], writes=[cx])
        for fg in range(4):
            gp = [load_cols(w_in, G0 + j * 2048 + fg * 512, 512) for j in range(3)]
            bb = wbuf()
            bv = bb.ap[:].rearrange("p (k n) -> p k n", n=512)
            ph.dma("sp", lambda e, bv=bv, fg=fg: e.dma_start(out=bv[:, 0:4, :], in_=t["b_w_branch_a"][l].rearrange("(k p) n -> p k n", p=128)[:, :, fg * 512:(fg + 1) * 512]), bb, writes=[bb])
            ph.dma("sp", lambda e, bv=bv, fg=fg: e.dma_start(out=bv[:, 4:8, :], in_=t["b_w_branch_b"][l].rearrange("(k p) n -> p k n", p=128)[:, :, fg * 512:(fg + 1) * 512]), bb, writes=[bb])
            ph.dma("sp", lambda e, bv=bv, fg=fg: e.dma_start(out=bv[:, 8:16, :], in_=t["b_w_branch_c"][l].rearrange("(k p) n -> p k n", p=128)[:, :, fg * 512:(fg + 1) * 512]), bb, writes=[bb])
            for cc in range(4):
                fc = fg * 4 + cc
                cs_ = slice(cc * 128, (cc + 1) * 128)
                sgs = []
                for j in range(3):
                    pb_ = pbank()
                    wbuf_, wv = gp[j]
                    for kc in range(KC):
                        ph.op("pe", lambda e, pb_=pb_, wv=wv, kc=kc, cs_=cs_: e.matmul(pb_.ap[:], wv[:, kc, cs_], h_sb.ap[:, kc, :], start=(kc == 0), stop=(kc == KC - 1)),
                              reads=[wbuf_, h_sb], writes=[pb_])
                    ph.op("act", lambda e, pb_=pb_, j=j: e.activation(out=sg[j].ap[:], in_=pb_.ap[:], func=AF.Sigmoid), reads=[pb_], writes=[sg[j]])
                prods = []
                for j, (src_, k0, nk_) in enumerate(((ya, 0, 4), (yb, 4, 4), (yc, 8, 8))):
                    pb_ = pbank()
                    for kk in range(nk_):
                        ph.op("pe", lambda e, pb_=pb_, kk=kk, k0=k0, nk_=nk_, src_=src_, cs_=cs_, bv=bv: e.matmul(pb_.ap[:], bv[:, k0 + kk, cs_], src_.ap[:, kk, :],
                                                                                                             start=(kk == 0), stop=(kk == nk_ - 1)),
                              reads=[bb, src_], writes=[pb_])
                    prods.append(pb_)
                ph.op("dve", lambda e, p0=prods[0]: e.tensor_tensor(out=mt[0].ap[:], in0=p0.ap[:], in1=sg[0].ap[:], op=ALU.mult), reads=[prods[0], sg[0]], writes=[mt[0]])
                ph.op("dve", lambda e, p1=prods[1]: e.tensor_tensor(out=mt[1].ap[:], in0=p1.ap[:], in1=sg[1].ap[:], op=ALU.mult), reads=[prods[1], sg[1]], writes=[mt[1]])
                ph.op("dve", lambda e: e.tensor_tensor(out=mt[0].ap[:], in0=mt[0].ap[:], in1=mt[1].ap[:], op=ALU.add), reads=[mt[0], mt[1]], writes=[mt[0]])
                ph.op("dve", lambda e, p2=prods[2]: e.tensor_tensor(out=mt[1].ap[:], in0=p2.ap[:], in1=sg[2].ap[:], op=ALU.mult), reads=[prods[2], sg[2]], writes=[mt[1]])
                ph.op("dve", lambda e, fc=fc: e.tensor_tensor(out=big.ap[:, fc, :], in0=mt[0].ap[:], in1=mt[1].ap[:], op=ALU.add), reads=[mt[0], mt[1]], writes=[big])
        for fg in range(4):
            wbuf_, wv = load_cols(t["b_w_out"][l], fg * 512, 512)
            for cc in range(4):
                fc = fg * 4 + cc
                pb_ = pbank()
                for kc in range(KC):
                    ph.op("pe", lambda e, pb_=pb_, wv=wv, kc=kc, cc=cc: e.matmul(pb_.ap[:], wv[:, kc, cc * 128:(cc + 1) * 128], big.ap[:, kc, :], start=(kc == 0), stop=(kc == KC - 1)),
                          reads=[wbuf_, big], writes=[pb_])
                ph.op("dve", lambda e, pb_=pb_, fc=fc: e.tensor_tensor(out=x_sb.ap[:, fc, :], in0=x_sb.ap[:, fc, :], in1=pb_.ap[:], op=ALU.add), reads=[x_sb, pb_], writes=[x_sb])
        emit_norm(ph, x_sb, h_sb, g2_sb, ones_bf, eps_sb, ps_ss, sqb, rstd, T)
        for hf in range(2):
            for i2 in range(11):
                ci0 = hf * 22 + i2 * 2
                b_ = wbuf()
                v_ = b_.ap[:].rearrange("p (k n) -> p k n", n=512)
                ph.dma("sp", lambda e, v_=v_, ci0=ci0: e.dma_start(out=v_[:, :, 0:256], in_=t["b_w_up"][l].rearrange("(k p) n -> p k n", p=128)[:, :, ci0 * 128:ci0 * 128 + 256]), b_, writes=[b_])
                ph.dma("sp", lambda e, v_=v_, ci0=ci0: e.dma_start(out=v_[:, :, 256:512], in_=t["b_w_up"][l].rearrange("(k p) n -> p k n", p=128)[:, :, D_FF + ci0 * 128:D_FF + ci0 * 128 + 256]), b_, writes=[b_])
                for s2 in range(2):
                    ci = ci0 + s2
                    ii = i2 * 2 + s2
                    res = []
                    for gu in range(2):
                        pb_ = pbank()
                        co = gu * 256 + s2 * 128
                        for kc in range(KC):
                            ph.op("pe", lambda e, pb_=pb_, v_=v_, kc=kc, co=co: e.matmul(pb_.ap[:], v_[:, kc, co:co + 128], h_sb.ap[:, kc, :], start=(kc == 0), stop=(kc == KC - 1)),
                                  reads=[b_, h_sb], writes=[pb_])
                        r = gu * 44 + ci
                        stg = (ug if gu == 0 else uu)[ii % 2]
                        ca = cacc[gu]
                        ph.op("pool", lambda e, stg=stg, r=r: e.tensor_copy(out=stg.ap[:, 0:2], in_=carry.ap[:, r, :]), reads=[carry], writes=[stg])
                        ph.op("act", lambda e, stg=stg, pb_=pb_: e.activation(out=stg.ap[:, 2:T + 2], in_=pb_.ap[:], func=AF.Copy), reads=[pb_], writes=[stg])
                        ph.op("pool", lambda e, stg=stg, r=r: e.tensor_copy(out=carry.ap[:, r, :], in_=stg.ap[:, T:T + 2]), reads=[stg], writes=[carry])
                        ph.op("dve", lambda e, stg=stg, ca=ca, r=r: e.tensor_scalar(out=ca.ap[:], in0=stg.ap[:, 2:T + 2], scalar1=fw.ap[:, r, 2:3], scalar2=None, op0=ALU.mult), reads=[stg, fw], writes=[ca])
                        ph.op("dve", lambda e, stg=stg, ca=ca, r=r: e.scalar_tensor_tensor(out=ca.ap[:], in0=stg.ap[:, 1:T + 1], scalar=fw.ap[:, r, 1:2], in1=ca.ap[:], op0=ALU.mult, op1=ALU.add),
                              reads=[stg, fw, ca], writes=[ca])
                        ph.op("dve", lambda e, stg=stg, ca=ca, r=r: e.scalar_tensor_tensor(out=ca.ap[:], in0=stg.ap[:, 0:T], scalar=fw.ap[:, r, 0:1], in1=ca.ap[:], op0=ALU.mult, op1=ALU.add),
                              reads=[stg, fw, ca], writes=[ca])
                        res.append(ca)
                    s_ = sl[ii % 2]
                    ph.op("act", lambda e, s_=s_, cg=res[0]: e.activation(out=s_.ap[:], in_=cg.ap[:], func=AF.Silu), reads=[res[0]], writes=[s_])
                    ph.op("dve", lambda e, s_=s_, cu=res[1], ii=ii: e.tensor_tensor(out=big.ap[:, ii, :], in0=s_.ap[:], in1=cu.ap[:], op=ALU.mult), reads=[s_, res[1]], writes=[big])
            for o2 in range(8):
                b_ = wbuf()
                v_ = b_.ap[:, 0:22 * 256].rearrange("p (k n) -> p k n", n=256)
                ph.dma("sp", lambda e, v_=v_, o2=o2, hf=hf: e.dma_start(out=v_, in_=t["b_w_down"][l, hf * 2816:(hf + 1) * 2816, :].rearrange("(k p) n -> p k n", p=128)[:, :, o2 * 256:(o2 + 1) * 256]),
                       b_, writes=[b_])
                for s2 in range(2):
                    oc = o2 * 2 + s2
                    pb_ = pbank()
                    for i in range(22):
                        ph.op("pe", lambda e, pb_=pb_, v_=v_, i=i, s2=s2: e.matmul(pb_.ap[:], v_[:, i, s2 * 128:(s2 + 1) * 128], big.ap[:, i, :], start=(i == 0), stop=(i == 21)),
                              reads=[b_, big], writes=[pb_])
                    ph.op("dve", lambda e, pb_=pb_, oc=oc: e.tensor_tensor(out=x_sb.ap[:, oc, :], in0=x_sb.ap[:, oc, :], in1=pb_.ap[:], op=ALU.add), reads=[x_sb, pb_], writes=[x_sb])
        if not last:
            for q4 in range(4):
                ph.dma("sp", lambda e, q4=q4, ts=ts: e.dma_start(out=xdst_r[:, q4 * 4:(q4 + 1) * 4, ts], in_=x_sb.ap[:, q4 * 4:(q4 + 1) * 4, :]), x_sb, reads=[x_sb])
        else:
            for kc in range(KC):
                sq = sqb[kc % 2]
                ph.op("act", lambda e, kc=kc, sq=sq: e.activation(out=sq.ap[:], in_=x_sb.ap[:, kc, :], func=AF.Square), reads=[x_sb], writes=[sq])
                ph.op("pe", lambda e, kc=kc, sq=sq: e.matmul(ps_ss.ap[:], ones_bf.ap[:], sq.ap[:], start=(kc == 0), stop=(kc == KC - 1)), reads=[sq, ones_bf], writes=[ps_ss])
            ph.op("act", lambda e: e.activation(out=rstd.ap[:], in_=ps_ss.ap[:], func=AF.Sqrt, bias=eps_sb.ap[:, 0:1], scale=1.0 / D), reads=[ps_ss, eps_sb], writes=[rstd])
            ph.op("dve", lambda e: e.reciprocal(out=rstd.ap[:], in_=rstd.ap[:]), reads=[rstd], writes=[rstd])
            for kc in range(KC):
                o_ = ost[kc % 2]
                ph.op("dve", lambda e, kc=kc, o_=o_: e.scalar_tensor_tensor(out=o_.ap[:], in0=x_sb.ap[:, kc, :], scalar=g3_sb.ap[:, kc:kc + 1], in1=rstd.ap[:], op0=ALU.mult, op1=ALU.mult),
                      reads=[x_sb, g3_sb, rstd], writes=[o_])
                ph.dma("sp", lambda e, kc=kc, o_=o_, ts=ts: e.dma_start(out=out_r[:, kc, ts], in_=o_.ap[:]), o_, reads=[o_])
    return ph.finish()


QCH = 16


def build_program(S, debug_outs=(), nl=DEPTH, last=True):
    P = Prog(S, debug_outs=debug_outs, nl=nl)
    counts = [phase0(P)]
    NQB = S // 128
    for l in range(nl):
        xsrc = P.t["xT"] if l == 0 else P.t["xres"]
        counts.append(phaseA(P, l, xsrc))
        counts.append(phaseB(P, l))
        counts.append(phaseB3(P, l))
        for qb0 in range(0, NQB, QCH):
            counts.append(phaseC(P, l, qb0, min(NQB, qb0 + QCH)))
        counts.append(phaseD(P, l, xsrc, last=(last and l == nl - 1)))
    P.counts = counts
    return P


def make_in_map(inputs, b, S, l0=0, nl=DEPTH, xT=None):
    m = {}
    if xT is None:
        m["xT"] = np.ascontiguousarray(np.asarray(inputs["x"])[b, :S].T)
    else:
        m["xT"] = np.ascontiguousarray(xT)
    m["pos"] = np.ascontiguousarray(np.asarray(inputs["positions"])[b, :S]).astype(np.int32)
    for k in W_SPECS:
        a = np.asarray(inputs[k], dtype=np.float32)
        if k != "norm_final":
            a = a[l0:l0 + nl]
        if k in ("ssm_c_re", "ssm_c_im"):
            a = np.transpose(a, (0, 1, 3, 2))
        m[k] = np.ascontiguousarray(a)
    m.update(host_consts())
    return m


FUSED = True


def kernel(**inputs):
    x = np.asarray(inputs["x"])
    B, S, _ = x.shape
    n_cores = 8
    out = np.empty((B, S, D), np.float32)
    if FUSED:
        P = build_program(S)
        in_maps = [make_in_map(inputs, c % B, S) for c in range(n_cores)]
        res = run_bass_kernel_spmd(P.nc, in_maps, core_ids=list(range(n_cores)))
        for b in range(B):
            out[b] = np.asarray(res.results[b]["outT"], dtype=np.float32).T
        return out
    xs = [None] * B
    for l in range(DEPTH):
        lastl = (l == DEPTH - 1)
        P = build_program(S, debug_outs=(() if lastl else ("xres",)), nl=1, last=lastl)
        in_maps = [make_in_map(inputs, c % B, S, l0=l, nl=1, xT=xs[c % B]) for c in range(n_cores)]
        res = run_bass_kernel_spmd(P.nc, in_maps, core_ids=list(range(n_cores)))
        if lastl:
            for b in range(B):
                out[b] = np.asarray(res.results[b]["outT"], dtype=np.float32).T
        else:
            xs = [np.asarray(res.results[b]["xres"], dtype=np.float32) for b in range(B)]
        del res, in_maps
    return out
```
